# Optimizing a Trainium2 kernel written in Bass

```python
import math
import jax, jax.numpy as jnp
from jax import lax
import numpy as np

D_MODEL = 1024
BATCH = 16
SEQ = 2048
DEPTH = 4

N_A_LAYERS = DEPTH // 2
N_B_LAYERS = DEPTH - N_A_LAYERS
SSM_GROUP = 16
SSM_GROUPS = D_MODEL // SSM_GROUP
SSM_STATE = 64
DT_MIN = 1e-3
DT_MAX = 1e-1
HEAD_DIM = 64
N_HEADS = D_MODEL // HEAD_DIM
DILATED_BRANCHES = ((128, 1), (512, 4), (2048, 16))
N_BRANCHES = len(DILATED_BRANCHES)
BRANCH_WIDTH = N_HEADS * HEAD_DIM
Q_WIDTH = N_BRANCHES * BRANCH_WIDTH
D_FF = 4 * D_MODEL
BLOCK = 128
EPS = 1e-6
NEG = -1e30

kernel_name = "yoco_s5_dilated_attn_hybrid"


def rms_norm(x, g):
    xf = x.astype(jnp.float32)
    y = xf * lax.rsqrt(jnp.mean(xf * xf, axis=-1, keepdims=True) + EPS)
    return (y * g.astype(jnp.float32)).astype(x.dtype)


def ada_chunks(c, w, b, n):
    m = jax.nn.silu(c) @ w + b
    return jnp.split(m[:, None, :], n, axis=-1)


def s5_mixer(u, lam_re, lam_im, log_dt, b_re, b_im, c_re, c_im, d_skip, w_glu):
    bsz, seq, dm = u.shape
    f32 = jnp.float32
    lam = lax.complex(lam_re.astype(f32), lam_im.astype(f32))
    dt = jnp.exp(log_dt.astype(f32))[:, None]
    lam_bar = jnp.exp(lam * dt)
    b_mat = lax.complex(b_re.astype(f32), b_im.astype(f32))
    b_bar = ((lam_bar - 1.0) / lam)[..., None] * b_mat
    c_mat = lax.complex(c_re.astype(f32), c_im.astype(f32))
    uf = u.astype(f32)
    ug = uf.reshape(bsz, seq, SSM_GROUPS, SSM_GROUP).astype(jnp.complex64)
    bu = jnp.einsum('bsgc,gpc->bsgp', ug, b_bar)
    a = jnp.broadcast_to(lam_bar, (1, seq) + lam_bar.shape)

    def combine(left, right):
        a_l, b_l = left
        a_r, b_r = right
        return a_r * a_l, a_r * b_l + b_r

    _, state = lax.associative_scan(combine, (a, bu), axis=1)
    y = jnp.einsum('bsgp,gcp->bsgc', state, c_mat).real.reshape(bsz, seq, dm)
    y = y + d_skip.astype(f32) * uf
    z = jax.nn.gelu(y).astype(u.dtype)
    val, gate = jnp.split(z @ w_glu, 2, axis=-1)
    return val * jax.nn.sigmoid(gate)


def to_dilated_blocks(t, dil):
    bsz, seq = t.shape[:2]
    rest = t.shape[2:]
    sub = seq // dil
    nb = -(-sub // BLOCK)
    t = jnp.swapaxes(t.reshape((bsz, sub, dil) + rest), 1, 2)
    t = jnp.pad(t, [(0, 0), (0, 0), (0, nb * BLOCK - sub)] + [(0, 0)] * len(rest))
    return t.reshape((bsz, dil, nb, BLOCK) + rest)


def from_dilated_blocks(t, seq):
    bsz, dil, nb, blk = t.shape[:4]
    rest = t.shape[4:]
    sub = seq // dil
    t = t.reshape((bsz, dil, nb * blk) + rest)[:, :, :sub]
    return jnp.swapaxes(t, 1, 2).reshape((bsz, seq) + rest)


def band_keys(t):
    prev = jnp.concatenate([jnp.zeros_like(t[:, :, :1]), t[:, :, :-1]], axis=2)
    return jnp.concatenate([prev, t], axis=3)


def band_mask(nb, span):
    qi = jnp.arange(BLOCK)[:, None]
    kj = jnp.arange(2 * BLOCK)[None, :] - BLOCK
    dist = qi - kj
    rel = (dist >= 0) & (dist <= span)
    abs_k = jnp.arange(nb)[:, None, None] * BLOCK + kj[None]
    return rel[None] & (abs_k >= 0)


def dilated_branch(q, k_band, v_band, span, dil):
    f32 = jnp.float32
    seq = q.shape[1]
    qb = to_dilated_blocks(q, dil).astype(f32)
    nb = qb.shape[2]
    s = jnp.einsum('brnqhe,brnkhe->brnhqk', qb, k_band.astype(f32)) * (HEAD_DIM ** -0.5)
    s = jnp.where(band_mask(nb, span)[None, None, :, None], s, NEG)
    m = jnp.max(s, axis=-1, keepdims=True)
    p = jnp.exp(s - m)
    den = jnp.sum(p, axis=-1)
    o = jnp.einsum('brnhqk,brnkhe->brnqhe', p, v_band.astype(f32))
    o = o / jnp.swapaxes(den, 3, 4)[..., None]
    lse = jnp.swapaxes(m[..., 0] + jnp.log(den), 3, 4)
    return from_dilated_blocks(o, seq), from_dilated_blocks(lse, seq)


def shared_kv(h, c, kv_g, kv_ada_w, kv_ada_b, w_kv):
    bsz, seq, _ = h.shape
    shift, scale = ada_chunks(c, kv_ada_w, kv_ada_b, 2)
    u = rms_norm(h, kv_g) * (1.0 + scale) + shift
    kv = (u @ w_kv).reshape(bsz, seq, 2, N_BRANCHES, N_HEADS, HEAD_DIM)
    k_bands, v_bands = [], []
    for i, (win, dil) in enumerate(DILATED_BRANCHES):
        k_bands.append(band_keys(to_dilated_blocks(kv[:, :, 0, i], dil)))
        v_bands.append(band_keys(to_dilated_blocks(kv[:, :, 1, i], dil)))
    return k_bands, v_bands


def dilated_mixer(u, w_q, k_bands, v_bands, w_o):
    bsz, seq, _ = u.shape
    q = (u @ w_q).reshape(bsz, seq, N_BRANCHES, N_HEADS, HEAD_DIM)
    outs, lses = [], []
    for i, (win, dil) in enumerate(DILATED_BRANCHES):
        o, l = dilated_branch(q[:, :, i], k_bands[i], v_bands[i], win // dil, dil)
        outs.append(o)
        lses.append(l)
    weights = jax.nn.softmax(jnp.stack(lses, axis=-1), axis=-1)
    o = jnp.einsum('gbshe,bshg->bshe', jnp.stack(outs), weights)
    return o.reshape(bsz, seq, BRANCH_WIDTH).astype(u.dtype) @ w_o


def setup_inputs(seed: int = 0) -> dict:
    key = jax.random.key(seed)
    ks = jax.random.split(key, 24)
    f32 = jnp.float32

    def nrm(k, shape, std):
        return jax.random.normal(k, shape, f32) * std

    n_idx = jnp.arange(SSM_STATE, dtype=f32)
    gp = (N_A_LAYERS, SSM_GROUPS, SSM_STATE)
    return {
        "x": nrm(ks[0], (BATCH, SEQ, D_MODEL), 1.0),
        "c": nrm(ks[1], (BATCH, D_MODEL), 1.0),
        "ln_g": 1.0 + nrm(ks[2], (DEPTH, 2, D_MODEL), 0.02),
        "ada_w": nrm(ks[3], (DEPTH, 2, D_MODEL, 3 * D_MODEL), 0.5 * D_MODEL ** -0.5),
        "ada_b": nrm(ks[4], (DEPTH, 2, 3 * D_MODEL), 0.02),
        "ssm_lam_re": -0.5 + nrm(ks[5], gp, 0.01),
        "ssm_lam_im": math.pi * n_idx + nrm(ks[6], gp, 0.01),
        "ssm_log_dt": jax.random.uniform(ks[7], (N_A_LAYERS, SSM_GROUPS), f32, math.log(DT_MIN), math.log(DT_MAX)),
        "ssm_b_re": nrm(ks[8], gp + (SSM_GROUP,), (2 * SSM_GROUP) ** -0.5),
        "ssm_b_im": nrm(ks[9], gp + (SSM_GROUP,), (2 * SSM_GROUP) ** -0.5),
        "ssm_c_re": nrm(ks[10], (N_A_LAYERS, SSM_GROUPS, SSM_GROUP, SSM_STATE), 0.5),
        "ssm_c_im": nrm(ks[11], (N_A_LAYERS, SSM_GROUPS, SSM_GROUP, SSM_STATE), 0.5),
        "ssm_d": nrm(ks[12], (N_A_LAYERS, D_MODEL), 1.0),
        "ssm_w_glu": nrm(ks[13], (N_A_LAYERS, D_MODEL, 2 * D_MODEL), D_MODEL ** -0.5),
        "kv_g": 1.0 + nrm(ks[14], (D_MODEL,), 0.02),
        "kv_ada_w": nrm(ks[15], (D_MODEL, 2 * D_MODEL), 0.5 * D_MODEL ** -0.5),
        "kv_ada_b": nrm(ks[16], (2 * D_MODEL,), 0.02),
        "w_kv": nrm(ks[17], (D_MODEL, 2 * Q_WIDTH), D_MODEL ** -0.5),
        "attn_w_q": nrm(ks[18], (N_B_LAYERS, D_MODEL, Q_WIDTH), D_MODEL ** -0.5),
        "attn_w_o": nrm(ks[19], (N_B_LAYERS, BRANCH_WIDTH, D_MODEL), BRANCH_WIDTH ** -0.5),
        "mlp_w1": nrm(ks[20], (DEPTH, D_MODEL, D_FF), D_MODEL ** -0.5),
        "mlp_w2": nrm(ks[21], (DEPTH, D_FF, D_MODEL), D_FF ** -0.5),
        "final_g": 1.0 + nrm(ks[22], (D_MODEL,), 0.02),
    }


def reference(x, c, ln_g, ada_w, ada_b, ssm_lam_re, ssm_lam_im, ssm_log_dt, ssm_b_re, ssm_b_im,
              ssm_c_re, ssm_c_im, ssm_d, ssm_w_glu, kv_g, kv_ada_w, kv_ada_b, w_kv,
              attn_w_q, attn_w_o, mlp_w1, mlp_w2, final_g):
    h = x
    k_bands, v_bands = None, None
    for layer in range(DEPTH):
        if layer == N_A_LAYERS:
            k_bands, v_bands = shared_kv(h, c, kv_g, kv_ada_w, kv_ada_b, w_kv)
        shift, scale, gate = ada_chunks(c, ada_w[layer, 0], ada_b[layer, 0], 3)
        u = rms_norm(h, ln_g[layer, 0]) * (1.0 + scale) + shift
        if layer < N_A_LAYERS:
            y = s5_mixer(u, ssm_lam_re[layer], ssm_lam_im[layer], ssm_log_dt[layer], ssm_b_re[layer],
                         ssm_b_im[layer], ssm_c_re[layer], ssm_c_im[layer], ssm_d[layer], ssm_w_glu[layer])
        else:
            j = layer - N_A_LAYERS
            y = dilated_mixer(u, attn_w_q[j], k_bands, v_bands, attn_w_o[j])
        h = h + gate * y
        shift, scale, gate = ada_chunks(c, ada_w[layer, 1], ada_b[layer, 1], 3)
        u = rms_norm(h, ln_g[layer, 1]) * (1.0 + scale) + shift
        h = h + gate * (jnp.square(jax.nn.relu(u @ mlp_w1[layer])) @ mlp_w2[layer])
    return rms_norm(h, final_g)
```

```python
import contextlib
import numpy as np
import concourse.bass as bass
import concourse.mybir as mybir
from concourse.bass_utils import run_bass_kernel_spmd

F32 = mybir.dt.float32
BF16 = mybir.dt.bfloat16
I32 = mybir.dt.int32
AF = mybir.ActivationFunctionType
ALU = mybir.AluOpType

ENGS = ("pe", "act", "dve", "pool", "sp")
S = 2048
D = 1024
NMOD = 26624
NCH = NMOD // 128
EPS = 1e-6


class Res:
    __slots__ = ("w", "r")

    def __init__(self):
        self.w = None
        self.r = []


class Prog:
    def __init__(self, nc):
        self.nc = nc
        self.ops = {e: [] for e in ENGS}
        self.cnt = {}
        self.seen = {e: {} for e in ENGS}
        self.sems = {}
        self.nd = 0
        self.bar = None
        self.bar_done = {e: None for e in ENGS}

    def dsem(self):
        k = "d%d" % self.nd
        self.nd += 1
        self.cnt[k] = 0
        return k

    def barrier(self):
        self.bar = dict(self.cnt)

    def op(self, eng, fn, reads=(), writes=(), dma=None, after=()):
        waits = {}
        for k_, v_ in after:
            waits[k_] = v_

        def need(dep, war=False):
            if dep is None:
                return
            k, v = dep
            if k == eng and (war or eng == "pe"):
                return
            if waits.get(k, 0) < v:
                waits[k] = v

        if self.bar is not None and self.bar_done[eng] is not self.bar:
            self.bar_done[eng] = self.bar
            for k, v in self.bar.items():
                if v > 0 and k != eng:
                    waits[k] = v
        for r in reads:
            need(r.w)
        for w in writes:
            need(w.w)
            for d in w.r:
                need(d, war=True)
        final = []
        seen = self.seen[eng]
        for k, v in waits.items():
            if seen.get(k, 0) >= v:
                continue
            seen[k] = v
            final.append((k, v))
        key, inc = (eng, 1) if dma is None else (dma, 16)
        self.cnt[key] = self.cnt.get(key, 0) + inc
        me = (key, self.cnt[key])
        self.ops[eng].append((final, fn, key, inc))
        for r in reads:
            r.r.append(me)
            if len(r.r) > 64:
                r.r = _compress(r.r)
        for w in writes:
            w.w = me
            w.r = []
        return me

    def emit(self, final_waits):
        nc = self.nc
        with contextlib.ExitStack() as st:
            for k in list(self.cnt.keys()):
                self.sems[k] = st.enter_context(nc.semaphore("s_" + k))
            block = st.enter_context(nc.Block())
            sems = self.sems

            def run(engname, e):
                for waits, fn, key, inc in self.ops[engname]:
                    for k, v in waits:
                        e.wait_ge(sems[k], v)
                    fn(e).then_inc(sems[key], inc)
                if engname == "sp":
                    for k, v in final_waits:
                        e.wait_ge(sems[k], v)

            @block.tensor
            def _(e):
                run("pe", e)

            @block.scalar
            def _(e):
                run("act", e)

            @block.vector
            def _(e):
                run("dve", e)

            @block.gpsimd
            def _(e):
                run("pool", e)

            @block.sync
            def _(e):
                run("sp", e)


def _compress(lst):
    best = {}
    for k, v in lst:
        if best.get(k, 0) < v:
            best[k] = v
    return list(best.items())


def build_program(stop_after=99, dbg=None, nseq=2):
    nc = bass.Bass("TRN2", target_bir_lowering=False)
    P = Prog(nc)

    def din(name, shape, dt=F32):
        return nc.dram_tensor(name, list(shape), dt, kind="ExternalInput").ap()

    def dscr(name, shape, dt=BF16):
        return nc.dram_tensor(name, list(shape), dt, kind="Internal").ap()

    x_d = din("x", [2, S, D])
    cT_d = din("cT", [128, 8, 2])
    adaw_d = din("adaw", [D, NMOD])
    adab_d = din("adab", [128, NCH])
    lng_d = din("lng", [128, 9, 8])
    fing_d = din("fing", [128, D])
    dsk_d = din("dsk", [128, 2, 8])
    lamre_d = din("lamre", [2, 64, 64])
    lamim_d = din("lamim", [2, 64, 64])
    logdt_d = din("logdt", [2, 64, 1])
    bre_d = din("bre", [2, 64, 1024])
    bim_d = din("bim", [2, 64, 1024])
    cre_d = din("cre", [2, 1024, 64])
    cim_d = din("cim", [2, 1024, 64])
    identf_d = din("identf", [128, 128])
    identb_d = din("identb", [128, 128], BF16)
    mcur_d = din("mcur", [128, 128], BF16)
    mprev_d = din("mprev", [128, 128], BF16)
    w1_d = din("w1", [4, D, 4096])
    w2_d = din("w2", [4, 4096, D])
    wglu_d = din("wglu", [2, D, 2048])
    wkv_d = din("wkv", [D, 6144])
    wq_d = din("wq", [2, D, 3072])
    wo_d = din("wo", [2, D, D])
    out_d = nc.dram_tensor("out", [2, S, D], F32, kind="ExternalOutput").ap()
    if dbg is not None and dbg[0] == "w":
        dbg_w = nc.dram_tensor("dbg_w", [D, 2048], BF16, kind="ExternalOutput").ap()
    if dbg is not None and dbg[0] == "p":
        dbg_l = nc.dram_tensor("dbg_l", [64, 256], F32, kind="ExternalOutput").ap()
        dbg_bb = nc.dram_tensor("dbg_bb", [64, 16, 2, 64], BF16, kind="ExternalOutput").ap()

    w1_b = dscr("w1b", [4, D, 4096])
    w2_b = dscr("w2b", [4, 4096, D])
    wglu_b = dscr("wglub", [2, D, 2048])
    wkv_b = dscr("wkvb", [D, 6144])
    wq_b = dscr("wqb", [2, D, 3072])
    wo_b = dscr("wob", [2, D, D])
    kt_s = dscr("kts", [2, 24, 128, S])
    v_s = dscr("vs", [2, S, 3072])
    bb_s = dscr("bbs", [2, 64, 16, 128])

    st = contextlib.ExitStack()

    def sb(name, shape, dt):
        return st.enter_context(nc.sbuf_tensor(name, list(shape), dt))

    h = sb("h", [128, 8, S], F32)
    u = sb("u", [128, 8, S], BF16)
    big = sb("big", [128, 16384], BF16)
    slabs = [sb("slab%d" % i, [128, 8, 512], BF16) for i in range(3)]
    sq = [sb("sq%d" % i, [128, 512], BF16) for i in range(2)]
    rs = sb("rs", [128, 512], F32)
    rstd = rs
    tmpf = [sb("tmpf%d" % i, [128, 512], F32) for i in range(2)]
    identf = sb("identf_s", [128, 128], F32)
    identb = sb("identb_s", [128, 128], BF16)
    onesb = sb("onesb", [128, 128], BF16)
    mcur = sb("mcur_s", [128, 128], BF16)
    mprev = sb("mprev_s", [128, 128], BF16)
    mod = sb("mod", [128, NCH, 2], F32)
    adab = sb("adab_s", [128, NCH], F32)
    lng = sb("lng_s", [128, 9, 8], F32)
    Gm = sb("Gm", [128, 9, 8, 2], F32)
    dsk = sb("dsk_s", [128, 2, 8], F32)
    cT = sb("cT_s", [128, 8, 2], F32)
    scT = sb("scT", [128, 8, 2], F32)
    arena = sb("arena", [128, 10752], F32)

    def carve(off, shape, dt, parts=128):
        nel = int(np.prod(shape[1:]))
        nb = nel * (2 if dt == BF16 else 4)
        assert off % 4 == 0 and off + nb <= 10752 * 4, (off, nb)
        ap = arena[0:parts, off // 4:(off + nb + 3) // 4]
        if dt == BF16:
            ap = ap.bitcast(BF16)
        elif dt == I32:
            ap = ap.bitcast(I32)
        if len(shape) == 3:
            ap = ap.rearrange("p (a b) -> p a b", b=shape[2])
        elif len(shape) == 4:
            ap = ap.rearrange("p (a b c) -> p a b c", b=shape[2], c=shape[3])
        return ap

    ps = [st.enter_context(nc.psum_tensor("ps%d" % i, [128, 512], F32)) for i in range(8)]
    psr = [Res() for _ in range(8)]

    r_h = [Res() for _ in range(4)]
    r_u = [Res() for _ in range(4)]
    r_big = Res()
    r_slab = [Res() for _ in range(3)]
    d_slab = [P.dsem() for _ in range(3)]
    r_sq = [Res(), Res()]
    r_rs = Res()
    r_rstd = Res()
    r_tmpf = [Res(), Res()]
    r_const = Res()
    r_mod = Res()
    d_const = P.dsem()
    d_constp = P.dsem()
    d_misc = [P.dsem() for _ in range(8)]
    misc_i = [0]
    slab_i = [0]
    rot = {"a": 0}

    def next_misc():
        k = d_misc[misc_i[0] % len(d_misc)]
        misc_i[0] += 1
        return k

    for (dst, src) in [(identf, identf_d), (adab, adab_d), (lng, lng_d), (dsk, dsk_d), (cT, cT_d)]:
        P.op("sp", lambda e, dst=dst, src=src: e.dma_start(out=dst[:], in_=src), writes=[r_const], dma=d_const)
    for (dst, src) in [(identb, identb_d), (mcur, mcur_d), (mprev, mprev_d)]:
        P.op("pool", lambda e, dst=dst, src=src: e.dma_start(out=dst[:], in_=src), writes=[r_const], dma=d_constp)
    r_const.w = (d_const, P.cnt[d_const])
    P.op("dve", lambda e: e.memset(onesb[:], 1.0), writes=[r_const])
    r_const.w = None
    d_castAB = [P.dsem(), P.dsem()]
    cast_n = [0]
    r_wcast = Res()
    r_wcast2 = Res()

    def cast2d(dst, src, rows, cols):
        for r0 in range(0, rows, 128):
            for c0 in range(0, cols, 1024):
                c1 = min(cols, c0 + 1024)
                g_ = cast_n[0] // 24
                dk = d_castAB[g_ % 2]
                aft = [(dk, P.cnt[dk])] if (cast_n[0] % 24 == 0 and g_ >= 2) else []
                cast_n[0] += 1
                P.op("pool", lambda e, r0=r0, c0=c0, c1=c1: e.dma_start(
                    out=dst[r0:r0 + 128, c0:c1], in_=src[r0:r0 + 128, c0:c1], max_dma_last_dim=4096),
                    dma=dk, after=aft)

    for l in range(2):
        cast2d(wglu_b[l], wglu_d[l], D, 2048)
    for l in range(4):
        cast2d(w1_b[l], w1_d[l], D, 4096)
        cast2d(w2_b[l], w2_d[l], 4096, D)
    cast2d(wkv_b, wkv_d, D, 6144)
    for l in range(2):
        cast2d(wq_b[l], wq_d[l], D, 3072)
        cast2d(wo_b[l], wo_d[l], D, D)
    r_wcast.w = (d_castAB[0], P.cnt[d_castAB[0]])
    r_wcast2.w = (d_castAB[1], P.cnt[d_castAB[1]])

    P.barrier()
    P.op("act", lambda e: e.activation(out=scT[:], in_=cT[:], func=AF.Silu), writes=[r_const])
    aslab = [carve(0, [128, 8, 512], F32), carve(16384, [128, 8, 512], F32)]
    r_aslab = [Res(), Res()]
    d_aslab = [P.dsem(), P.dsem()]
    adaw_v = adaw_d.rearrange("(kc p) n -> p kc n", p=128)
    nsl = NMOD // 512
    for s_ in range(nsl):
        bi = s_ % 2
        P.op("sp", lambda e, s_=s_, bi=bi: e.dma_start(out=aslab[bi], in_=adaw_v[:, :, s_ * 512:(s_ + 1) * 512]),
             writes=[r_aslab[bi]], dma=d_aslab[bi])
        for j in range(4):
            ch = s_ * 4 + j
            for kc in range(8):
                P.op("pe", lambda e, bi=bi, j=j, kc=kc, ch=ch: e.matmul(
                    ps[7][:, 2 * ch:2 * ch + 2], lhsT=aslab[bi][:, kc, j * 128:(j + 1) * 128], rhs=scT[:, kc, :],
                    start=(kc == 0), stop=(kc == 7)), reads=[r_aslab[bi], r_const], writes=[psr[7]])
    P.op("dve", lambda e: e.tensor_tensor(out=mod[:], in0=ps[7][:, 0:2 * NCH].rearrange("p (c b) -> p c b", b=2),
                                          in1=adab[:].unsqueeze(2).to_broadcast([128, NCH, 2]), op=ALU.add),
         reads=[psr[7]], writes=[r_mod])
    for n in range(9):
        base = (n * 3072 + 1024) // 128 if n < 8 else (24576 + 1024) // 128
        P.op("dve", lambda e, n=n, base=base: e.scalar_tensor_tensor(
            out=Gm[:, n, :, :], in0=mod[:, base:base + 8, :], scalar=1.0,
            in1=lng[:, n, :].unsqueeze(2).to_broadcast([128, 8, 2]), op0=ALU.add, op1=ALU.mult),
            reads=[r_mod], writes=[r_mod])

    def mod_shift(n, kc, b):
        base = (n * 3072) // 128 if n < 8 else 24576 // 128
        return mod[:, base + kc, b:b + 1]

    def mod_gate(n, kc, b):
        base = (n * 3072 + 2048) // 128
        return mod[:, base + kc, b:b + 1]

    def T(t):
        return slice(t * 512, (t + 1) * 512)

    def norm(n, b, out_buf, r_out):
        for t in range(4):
            for kc in range(8):
                i = kc % 2
                P.op("act", lambda e, kc=kc, i=i, t=t: e.activation(out=sq[i][:], in_=h[:, kc, T(t)], func=AF.Square),
                     reads=[r_h[t]], writes=[r_sq[i]])
                P.op("pe", lambda e, kc=kc, i=i: e.matmul(ps[6][:], lhsT=onesb[:], rhs=sq[i][:],
                                                          start=(kc == 0), stop=(kc == 7)),
                     reads=[r_sq[i]], writes=[psr[6]])
            P.op("act", lambda e: e.activation(out=rs[:], in_=ps[6][:], func=AF.Sqrt, bias=EPS, scale=1.0 / D),
                 reads=[psr[6]], writes=[r_rs, r_rstd])
            P.op("dve", lambda e: e.reciprocal(out=rstd[:], in_=rs[:]), reads=[r_rs], writes=[r_rs, r_rstd])
            for kc in range(8):
                i = kc % 2
                P.op("pool", lambda e, kc=kc, i=i, t=t: e.tensor_tensor(out=tmpf[i][:], in0=h[:, kc, T(t)], in1=rstd[:],
                                                                         op=ALU.mult),
                     reads=[r_h[t], r_rstd], writes=[r_tmpf[i]])
                P.op("act", lambda e, kc=kc, i=i, t=t: e.activation(
                    out=out_buf[:, kc, T(t)], in_=tmpf[i][:], func=AF.Identity,
                    bias=mod_shift(n, kc, b), scale=Gm[:, n, kc, b:b + 1]),
                    reads=[r_tmpf[i], r_mod], writes=[r_out[t]])

    def load_slab(view):
        i = slab_i[0] % 3
        slab_i[0] += 1
        P.op("sp", lambda e, i=i, view=view: e.dma_start(out=slabs[i][:, 0:view.shape[1], 0:view.shape[2]], in_=view),
             reads=[r_wcast, r_wcast2], writes=[r_slab[i]], dma=d_slab[i])
        return i

    def gemm_fm(src, r_src, KC, wv, colgroups, epi, tiles=range(4)):
        for t in tiles:
            for grp in colgroups:
                c0 = grp[0]
                banks = []
                for _ in grp:
                    banks.append(rot["a"] % 6)
                    rot["a"] += 1
                for kg in range(KC // 8):
                    si = load_slab(wv[:, kg * 8:(kg + 1) * 8, c0:grp[-1] + 128])
                    for j, c in enumerate(grp):
                        for kc in range(8):
                            kk = kg * 8 + kc
                            P.op("pe", lambda e, si=si, j=j, c=c, kc=kc, kk=kk, bk=banks[j], t=t, c0=c0: e.matmul(
                                ps[bk][:], lhsT=slabs[si][:, kc, c - c0:c - c0 + 128], rhs=src[:, kk, T(t)],
                                start=(kk == 0), stop=(kk == KC - 1)),
                                reads=[r_slab[si], r_src[t] if isinstance(r_src, list) else r_src], writes=[psr[banks[j]]])
                for j, c in enumerate(grp):
                    epi(t, c // 128, ps[banks[j]], psr[banks[j]])

    def wview(w2d):
        return w2d.rearrange("(kc p) n -> p kc n", p=128)

    def resid_epi(n, b):
        def epi(t, oc, p_ap, p_res):
            P.op("dve", lambda e, t=t, oc=oc, p_ap=p_ap: e.scalar_tensor_tensor(
                out=h[:, oc, T(t)], in0=p_ap[:], scalar=mod_gate(n, oc, b), in1=h[:, oc, T(t)],
                op0=ALU.mult, op1=ALU.add), reads=[p_res, r_mod, r_h[t]], writes=[r_h[t]])
        return epi

    hid = big[:, :].rearrange("p (a b) -> p a b", b=512)
    r_hid = Res()
    relu_t = [carve(0, [128, 512], BF16), carve(1024, [128, 512], BF16)]
    r_relu = [Res(), Res()]

    def mlp(l, b):
        n = l * 2 + 1
        w1v = wview(w1_b[l])
        w2v = wview(w2_b[l])
        for t in range(4):
            def epi1(t_, hc, p_ap, p_res):
                i = hc % 2
                P.op("act", lambda e, i=i, p_ap=p_ap: e.activation(out=relu_t[i], in_=p_ap[:], func=AF.Relu),
                     reads=[p_res], writes=[r_relu[i]])
                P.op("pool", lambda e, i=i, hc=hc: e.tensor_tensor(out=hid[:, hc, :], in0=relu_t[i], in1=relu_t[i],
                                                                    op=ALU.mult),
                     reads=[r_relu[i]], writes=[r_hid])
            gemm_fm(u, r_u, 8, w1v, [[c0 + 128 * j for j in range(4)] for c0 in range(0, 4096, 512)], epi1, tiles=[t])

            class HidSrc:
                def __getitem__(self, idx):
                    return hid[idx[0], idx[1], :]
            gemm_fm(HidSrc(), r_hid, 32, w2v, [[c0 + 128 * j for j in range(4)] for c0 in (0, 512)],
                    lambda t_, oc, p_ap, p_res, t=t: resid_epi(n, b)(t, oc, p_ap, p_res), tiles=[t])

    tok = [carve(0, [128, D], F32), carve(4096, [128, D], F32)]
    r_tok = [Res(), Res()]
    d_tok = [P.dsem(), P.dsem()]
    otok = [carve(8192, [128, D], F32), carve(12288, [128, D], F32)]
    r_otok = [Res(), Res()]
    d_otok = [P.dsem(), P.dsem()]
    ss1 = carve(16384, [128, 1], F32)
    ss2 = carve(16400, [128, 1], F32)
    ss3 = carve(16416, [128, 1], F32)
    junk = carve(16448, [128, D], F32)
    fing = carve(20544, [128, D], F32)
    r_ss = Res()
    last_out = []

    def load_x(b):
        for tb in range(16):
            i = tb % 2
            P.op("sp", lambda e, tb=tb, i=i: e.dma_start(out=tok[i], in_=x_d[b, tb * 128:(tb + 1) * 128, :]),
                 writes=[r_tok[i]], dma=d_tok[i])
            for half in range(2):
                bk = rot["a"] % 6
                rot["a"] += 1
                for j in range(4):
                    kc = half * 4 + j
                    P.op("pe", lambda e, i=i, kc=kc, j=j, bk=bk: e.transpose(
                        out=ps[bk][:, j * 128:(j + 1) * 128], in_=tok[i][:, kc * 128:(kc + 1) * 128], identity=identf[:]),
                        reads=[r_tok[i]], writes=[psr[bk]])
                P.op("act", lambda e, half=half, bk=bk, tb=tb: e.activation(
                    out=h[:, half * 4:half * 4 + 4, tb * 128:(tb + 1) * 128],
                    in_=ps[bk][:].rearrange("p (a b) -> p a b", b=128), func=AF.Copy),
                    reads=[psr[bk]], writes=[r_h[tb // 4]])

    def store_out(b, final=True):
        P.op("sp", lambda e: e.dma_start(out=fing, in_=fing_d), writes=[r_ss], dma=d_tok[0])
        for tb in range(16):
            i = tb % 2
            for half in range(2):
                for j in range(4):
                    kc = half * 4 + j
                    P.op("pe", lambda e, kc=kc, j=j, half=half, tb=tb: e.transpose(
                        out=ps[half][:, j * 128:(j + 1) * 128], in_=h[:, kc, tb * 128:(tb + 1) * 128], identity=identf[:]),
                        reads=[r_h[tb // 4]], writes=[psr[half]])
            if final:
                for half in range(2):
                    P.op("act", lambda e, half=half: e.activation(
                        out=junk[:, half * 512:(half + 1) * 512], in_=ps[half][:], func=AF.Square,
                        accum_out=(ss1 if half == 0 else ss2)), reads=[psr[half]], writes=[r_ss])
                P.op("dve", lambda e: e.tensor_tensor(out=ss3, in0=ss1, in1=ss2, op=ALU.add), reads=[r_ss], writes=[r_ss])
                P.op("act", lambda e: e.activation(out=ss1, in_=ss3, func=AF.Sqrt, bias=EPS, scale=1.0 / D),
                     reads=[r_ss], writes=[r_ss])
                P.op("dve", lambda e: e.reciprocal(out=ss2, in_=ss1), reads=[r_ss], writes=[r_ss])
                for half in range(2):
                    P.op("dve", lambda e, half=half, i=i: e.scalar_tensor_tensor(
                        out=otok[i][:, half * 512:(half + 1) * 512], in0=ps[half][:], scalar=ss2,
                        in1=fing[:, half * 512:(half + 1) * 512], op0=ALU.mult, op1=ALU.mult),
                        reads=[psr[half], r_ss], writes=[r_otok[i]])
            else:
                for half in range(2):
                    P.op("act", lambda e, half=half, i=i: e.activation(
                        out=otok[i][:, half * 512:(half + 1) * 512], in_=ps[half][:], func=AF.Copy),
                        reads=[psr[half]], writes=[r_otok[i]])
            me = P.op("sp", lambda e, i=i, tb=tb: e.dma_start(out=out_d[b, tb * 128:(tb + 1) * 128, :], in_=otok[i]),
                      reads=[r_otok[i]], dma=d_otok[i])
            r_otok[i].r.append(me)
            last_out.append(me)

    def s5_prep(l):
        P.barrier()
        o = [0]

        def A(shape, dt=F32, parts=64):
            nel = int(np.prod(shape[1:]))
            nb = ((nel * (2 if dt == BF16 else 4) + 3) // 4) * 4
            ap = carve(o[0], shape, dt, parts)
            o[0] += nb
            return ap
        lre = A([64, 64]); lim = A([64, 64]); ldt = A([64, 1]); dt_ = A([64, 1])
        a_ = A([64, 64]); th = A([64, 64]); ea = A([64, 64]); yy = A([64, 64]); ki = A([64, 64], I32)
        kf = A([64, 64]); ff = A([64, 64]); sn = A([64, 64]); cs = A([64, 64])
        lbr = A([64, 64]); lbi = A([64, 64]); den = A([64, 64]); nr = A([64, 64]); ni = A([64, 64])
        qr = A([64, 64]); qi = A([64, 64]); t1 = A([64, 64]); t2 = A([64, 64])
        br = A([64, 64, 16]); bi_ = A([64, 64, 16])
        bbr = A([64, 64, 16]); bbi = A([64, 64, 16]); t3 = A([64, 64, 16])
        bout = A([64, 16, 2, 64], BF16)
        R = Res()
        dd = next_misc()
        for dst, src in [(lre, lamre_d[l]), (lim, lamim_d[l]), (ldt, logdt_d[l])]:
            P.op("sp", lambda e, dst=dst, src=src: e.dma_start(out=dst, in_=src), writes=[R], dma=dd)
        P.op("sp", lambda e: e.dma_start(out=br, in_=bre_d[l].rearrange("g (p c) -> g p c", c=16)), writes=[R], dma=dd)
        P.op("sp", lambda e: e.dma_start(out=bi_, in_=bim_d[l].rearrange("g (p c) -> g p c", c=16)), writes=[R], dma=dd)

        def V(fn):
            P.op("dve", fn, reads=[R], writes=[R])

        def ACT(fn):
            P.op("act", fn, reads=[R], writes=[R])
        ACT(lambda e: e.activation(out=dt_, in_=ldt, func=AF.Exp))
        V(lambda e: e.tensor_scalar(out=a_, in0=lre, scalar1=dt_, scalar2=None, op0=ALU.mult))
        V(lambda e: e.tensor_scalar(out=th, in0=lim, scalar1=dt_, scalar2=None, op0=ALU.mult))
        ACT(lambda e: e.activation(out=ea, in_=a_, func=AF.Exp))

        def sin_of(dst, offs):
            V(lambda e: e.tensor_scalar(out=yy, in0=th, scalar1=1.0 / (2 * np.pi), scalar2=offs, op0=ALU.mult, op1=ALU.add))
            V(lambda e: e.tensor_copy(out=ki, in_=yy))
            V(lambda e: e.tensor_copy(out=kf, in_=ki))
            V(lambda e: e.tensor_tensor(out=ff, in0=yy, in1=kf, op=ALU.subtract))
            V(lambda e: e.scalar_tensor_tensor(out=ff, in0=ff, scalar=0.0, in1=ff, op0=ALU.is_lt, op1=ALU.add))
            V(lambda e: e.tensor_scalar(out=ff, in0=ff, scalar1=2 * np.pi, scalar2=-np.pi, op0=ALU.mult, op1=ALU.add))
            V(lambda e: e.tensor_scalar(out=ff, in0=ff, scalar1=-3.14159, scalar2=3.14159, op0=ALU.max, op1=ALU.min))
            ACT(lambda e: e.activation(out=dst, in_=ff, func=AF.Sin))
        sin_of(sn, 0.5)
        sin_of(cs, 0.75)
        V(lambda e: e.tensor_tensor(out=lbr, in0=ea, in1=cs, op=ALU.mult))
        V(lambda e: e.tensor_tensor(out=lbi, in0=ea, in1=sn, op=ALU.mult))
        V(lambda e: e.tensor_scalar(out=nr, in0=lbr, scalar1=-1.0, scalar2=None, op0=ALU.add))
        V(lambda e: e.tensor_tensor(out=t1, in0=lre, in1=lre, op=ALU.mult))
        V(lambda e: e.tensor_tensor(out=t2, in0=lim, in1=lim, op=ALU.mult))
        V(lambda e: e.tensor_tensor(out=den, in0=t1, in1=t2, op=ALU.add))
        V(lambda e: e.reciprocal(out=den, in_=den))
        V(lambda e: e.tensor_tensor(out=t1, in0=nr, in1=lre, op=ALU.mult))
        V(lambda e: e.tensor_tensor(out=t2, in0=lbi, in1=lim, op=ALU.mult))
        V(lambda e: e.tensor_tensor(out=qr, in0=t1, in1=t2, op=ALU.add))
        V(lambda e: e.tensor_tensor(out=qr, in0=qr, in1=den, op=ALU.mult))
        V(lambda e: e.tensor_tensor(out=t1, in0=lbi, in1=lre, op=ALU.mult))
        V(lambda e: e.tensor_tensor(out=t2, in0=nr, in1=lim, op=ALU.mult))
        V(lambda e: e.tensor_tensor(out=qi, in0=t1, in1=t2, op=ALU.subtract))
        V(lambda e: e.tensor_tensor(out=qi, in0=qi, in1=den, op=ALU.mult))
        qrb = qr.unsqueeze(2).to_broadcast([64, 64, 16])
        qib = qi.unsqueeze(2).to_broadcast([64, 64, 16])
        V(lambda e: e.tensor_tensor(out=bbr, in0=br, in1=qrb, op=ALU.mult))
        V(lambda e: e.tensor_tensor(out=t3, in0=bi_, in1=qib, op=ALU.mult))
        V(lambda e: e.tensor_tensor(out=bbr, in0=bbr, in1=t3, op=ALU.subtract))
        V(lambda e: e.tensor_tensor(out=bbi, in0=bi_, in1=qrb, op=ALU.mult))
        V(lambda e: e.tensor_tensor(out=t3, in0=br, in1=qib, op=ALU.mult))
        V(lambda e: e.tensor_tensor(out=bbi, in0=bbi, in1=t3, op=ALU.add))
        V(lambda e: e.tensor_copy(out=bout[:, :, 0, :], in_=bbr.rearrange("g p c -> g c p")))
        V(lambda e: e.tensor_copy(out=bout[:, :, 1, :], in_=bbi.rearrange("g p c -> g c p")))
        P.op("sp", lambda e: e.dma_start(out=bb_s[l].rearrange("g c (r p) -> g c r p", r=2), in_=bout), reads=[R], writes=[R], dma=dd)
        if dbg is not None and dbg[0] == "p":
            for i_, src_ in enumerate((lbr, lbi, qr, qi)):
                P.op("sp", lambda e, i_=i_, src_=src_: e.dma_start(out=dbg_l[:, i_ * 64:(i_ + 1) * 64], in_=src_), reads=[R], writes=[R], dma=dd)
            P.op("sp", lambda e: e.dma_start(out=dbg_bb, in_=bout), reads=[R], writes=[R], dma=dd)
        return lbr, lbi, R

    def s5_layer_setup(l):
        lbr, lbi, R = s5_prep(l)
        o = [0]

        def A(shape, dt=F32, parts=128):
            nel = int(np.prod(shape[1:]))
            nb = ((nel * (2 if dt == BF16 else 4) + 3) // 4) * 4
            ap = carve(o[0], shape, dt, parts)
            o[0] += nb
            return ap
        cw = carve(37888, [64, 2, 1024], BF16, 64)
        ca = carve(41984, [64, 2, 64], F32, 64)
        cb = carve(42496, [64, 2, 64], F32, 64)
        R2 = Res()
        dd = next_misc()
        for src, idx in ((lbr, 0), (lbi, 1)):
            P.op("pe", lambda e, src=src, idx=idx: e.transpose(out=ps[6][0:64, idx * 64:(idx + 1) * 64], in_=src,
                                                              identity=identf[0:64, 0:64]),
                 reads=[R], writes=[psr[6]])
        P.op("dve", lambda e: e.tensor_copy(out=ca[:, 0, :], in_=ps[6][0:64, 0:64]), reads=[psr[6]], writes=[R2])
        P.op("dve", lambda e: e.tensor_copy(out=ca[:, 1, :], in_=ps[6][0:64, 0:64]), reads=[psr[6]], writes=[R2])
        P.op("dve", lambda e: e.tensor_copy(out=cb[:, 1, :], in_=ps[6][0:64, 64:128]), reads=[psr[6]], writes=[R2])
        P.op("dve", lambda e: e.tensor_scalar(out=cb[:, 0, :], in0=ps[6][0:64, 64:128], scalar1=-1.0, scalar2=None,
                                              op0=ALU.mult), reads=[psr[6]], writes=[R2])
        P.barrier()
        bbpad = A([128, 64, 128], BF16)
        cnat = A([128, 8, 64], F32)
        P.op("pool", lambda e: e.memset(bbpad, 0.0), writes=[R2])
        for g in range(64):
            gl = g % 8
            P.op("sp", lambda e, g=g, gl=gl: e.dma_start(out=bbpad[16 * gl:16 * gl + 16, g, :], in_=bb_s[l, g]),
                 reads=[R, R2], writes=[R2], dma=dd)
        for idx, cd, sgn in ((0, cre_d, 1.0), (1, cim_d, -1.0)):
            P.op("sp", lambda e, cd=cd: e.dma_start(out=cnat, in_=cd[l].rearrange("(a q) p -> q a p", q=128)),
                 reads=[R2], writes=[R2], dma=dd)
            for half in range(2):
                for j in range(4):
                    a = half * 4 + j
                    P.op("pe", lambda e, a=a, j=j: e.transpose(out=ps[5][0:64, j * 128:(j + 1) * 128], in_=cnat[:, a, :],
                                                                identity=identf[:]), reads=[R2], writes=[psr[5]])
                P.op("dve", lambda e, idx=idx, half=half, sgn=sgn: e.tensor_scalar(
                    out=cw[:, idx, half * 512:(half + 1) * 512], in0=ps[5][0:64, :], scalar1=sgn, scalar2=None,
                    op0=ALU.mult), reads=[psr[5]], writes=[R2])
        return dict(bbpad=bbpad, cw=cw, ca=ca, cb=cb, R=R2, off=o[0])

    TC = 32

    def s5_mixer(l, b, L):
        o = [L["off"]]

        def A(shape, dt=F32, parts=128):
            nel = int(np.prod(shape[1:]))
            nb = ((nel * (2 if dt == BF16 else 4) + 3) // 4) * 4
            ap = carve(o[0], shape, dt, parts)
            o[0] += nb
            return ap
        R2 = L["R"]
        bbpad, cw, ca, cb = L["bbpad"], L["cw"], L["ca"], L["cb"]
        Vb = big[0:64, :].bitcast(F32)[:, 0:2 * 64 * TC].rearrange("p (r g t) -> p r g t", r=2, g=64)
        Xb = big[0:64, :].bitcast(F32)[:, 4096:4096 + 2 * 64 * TC // 2].bitcast(BF16).rearrange(
            "p (r g t) -> p r g t", r=2, g=64)
        Zst = A([64, 2, 64], F32, 64)
        m1 = [A([64, 2, 32], F32, 64), A([64, 2, 32], F32, 64)]
        m2 = [A([64, 2, 32], F32, 64), A([64, 2, 32], F32, 64)]
        ytok = A([TC, D], F32, TC)
        zp = A([128, 8, TC], F32)
        g1 = A([128, 8, TC], F32)
        g2 = A([128, 8, TC], F32)
        rV = [Res(), Res()]
        rX = Res()
        rY = Res()
        rZ = Res()
        rZs = [Res(), Res()]
        rZ2 = Res()
        rm = [Res(), Res()]
        P.op("dve", lambda e: e.memset(Zst, 0.0), writes=[rZs[0], rZs[1]])
        nck = S // TC
        for ck in range(nck):
            tsl = slice(ck * TC, (ck + 1) * TC)
            t4 = ck * TC // 512
            per_bank = 512 // TC
            tiles = [(g, ri) for g in range(64) for ri in range(2)]
            for b0 in range(0, 128, per_bank):
                bk = rot["a"] % 5
                rot["a"] += 1
                grp = tiles[b0:b0 + per_bank]
                for j, (g, ri) in enumerate(grp):
                    P.op("pe", lambda e, g=g, ri=ri, j=j, bk=bk, tsl=tsl: e.matmul(
                        ps[bk][0:64, j * TC:(j + 1) * TC], lhsT=bbpad[:, g, ri * 64:(ri + 1) * 64],
                        rhs=u[:, g // 8, tsl], start=True, stop=True),
                        reads=[R2, r_u[t4]], writes=[psr[bk]])
                g0 = grp[0][0]
                ng = len(grp) // 2
                hv = 0 if g0 < 32 else 1
                P.op("act", lambda e, bk=bk, g0=g0, ng=ng: e.activation(
                    out=Vb[:, :, g0:g0 + ng, :].rearrange("p r g t -> p g r t"),
                    in_=ps[bk][0:64, 0:ng * 2 * TC].rearrange("p (g r t) -> p g r t", r=2, t=TC), func=AF.Copy),
                    reads=[psr[bk]], writes=[rV[hv]])
            for hv, eng in ((0, "dve"), (1, "pool")):
                gs = slice(hv * 32, hv * 32 + 32)
                for t in range(TC):
                    if t == 0:
                        prev = Zst[:, :, gs]
                        prev_f = [Zst[:, 1, gs], Zst[:, 0, gs]]
                        rp = [rZs[hv]]
                    else:
                        prev = Vb[:, :, gs, t - 1]
                        prev_f = [Vb[:, 1, gs, t - 1], Vb[:, 0, gs, t - 1]]
                        rp = [rV[hv]]
                    cur = Vb[:, :, gs, t]
                    P.op(eng, lambda e, prev=prev, hv=hv, gs=gs: e.tensor_tensor(out=m1[hv], in0=prev, in1=ca[:, :, gs], op=ALU.mult),
                         reads=rp + [R2], writes=[rm[hv]])
                    P.op(eng, lambda e, prev_f=prev_f, hv=hv, gs=gs: e.tensor_tensor(out=m2[hv][:, 0, :], in0=prev_f[0], in1=cb[:, 0, gs], op=ALU.mult),
                         reads=rp + [R2], writes=[rm[hv]])
                    P.op(eng, lambda e, prev_f=prev_f, hv=hv, gs=gs: e.tensor_tensor(out=m2[hv][:, 1, :], in0=prev_f[1], in1=cb[:, 1, gs], op=ALU.mult),
                         reads=rp + [R2], writes=[rm[hv]])
                    P.op(eng, lambda e, hv=hv: e.tensor_tensor(out=m1[hv], in0=m1[hv], in1=m2[hv], op=ALU.add),
                         reads=[rm[hv]], writes=[rm[hv]])
                    P.op(eng, lambda e, cur=cur, hv=hv: e.tensor_tensor(out=cur, in0=cur, in1=m1[hv], op=ALU.add),
                         reads=[rm[hv], rV[hv]], writes=[rV[hv]])
                P.op(eng, lambda e, gs=gs: e.tensor_copy(out=Zst[:, :, gs], in_=Vb[:, :, gs, TC - 1]),
                     reads=[rV[hv]], writes=[rZs[hv]])
                P.op(eng, lambda e, gs=gs: e.tensor_copy(out=Xb[:, :, gs, :], in_=Vb[:, :, gs, :]),
                     reads=[rV[hv]], writes=[rX])
            for half in range(2):
                bk = 5
                for gg in range(32):
                    g = half * 32 + gg
                    for ri in range(2):
                        P.op("pe", lambda e, g=g, gg=gg, ri=ri: e.matmul(
                            ps[5][0:TC, gg * 16:(gg + 1) * 16], lhsT=Xb[:, ri, g, :], rhs=cw[:, ri, g * 16:(g + 1) * 16],
                            start=(ri == 0), stop=(ri == 1)), reads=[rX, R2], writes=[psr[5]])
                P.op("act", lambda e, half=half: e.activation(out=ytok[:, half * 512:(half + 1) * 512], in_=ps[5][0:TC, :],
                                                              func=AF.Copy), reads=[psr[5]], writes=[rY])
            for kc in range(8):
                P.op("pe", lambda e, kc=kc: e.transpose(out=ps[6][:, kc * TC:(kc + 1) * TC], in_=ytok[:, kc * 128:(kc + 1) * 128],
                                                        identity=identf[0:TC, 0:TC]), reads=[rY], writes=[psr[6]])
            P.op("dve", lambda e, tsl=tsl: e.tensor_tensor(out=zp, in0=u[:, :, tsl],
                                                          in1=dsk[:, l, :].unsqueeze(2).to_broadcast([128, 8, TC]), op=ALU.mult),
                 reads=[r_u[t4]], writes=[rZ])
            P.op("dve", lambda e: e.tensor_tensor(out=zp, in0=zp, in1=ps[6][:, 0:8 * TC].rearrange("p (k t) -> p k t", t=TC),
                                                  op=ALU.add), reads=[psr[6], rZ], writes=[rZ])
            P.op("dve", lambda e: e.tensor_tensor(out=g1, in0=zp, in1=zp, op=ALU.mult), reads=[rZ], writes=[rZ])
            P.op("dve", lambda e: e.tensor_scalar(out=g1, in0=g1, scalar1=0.044715, scalar2=1.0, op0=ALU.mult, op1=ALU.add),
                 reads=[rZ], writes=[rZ])
            P.op("dve", lambda e: e.scalar_tensor_tensor(out=g1, in0=g1, scalar=0.0, in1=zp, op0=ALU.add, op1=ALU.mult), reads=[rZ], writes=[rZ])
            P.op("dve", lambda e: e.tensor_scalar(out=g1, in0=g1, scalar1=-30.0, scalar2=None, op0=ALU.max), reads=[rZ], writes=[rZ])
            P.op("act", lambda e: e.activation(out=g2, in_=g1, func=AF.Sigmoid, scale=1.5957691216057308),
                 reads=[rZ], writes=[rZ])
            if dbg == ("y", l):
                P.op("dve", lambda e, tsl=tsl: e.tensor_copy(out=u[:, :, tsl], in_=ps[6][:, 0:8 * TC].rearrange("p (k t) -> p k t", t=TC)),
                     reads=[rZ, psr[6], r_u[t4]], writes=[rZ2])
            else:
                P.op("dve", lambda e, tsl=tsl: e.tensor_tensor(out=u[:, :, tsl], in0=zp, in1=g2, op=ALU.mult),
                     reads=[rZ, r_u[t4]], writes=[rZ2])

    def glu(l, b):
        n = l * 2
        wv = wview(wglu_b[l])
        sg = [carve(0, [128, 512], F32), carve(2048, [128, 512], F32)]
        yv = [carve(4096, [128, 512], F32), carve(6144, [128, 512], F32)]
        r_sg = [Res(), Res()]
        r_yv = [Res(), Res()]
        for t in range(4):
            for c0 in (0, 512):
                pend = {}

                def epi(t_, oc, p_ap, p_res, pend=pend, t=t):
                    if oc < 8:
                        pend[oc] = (p_ap, p_res)
                        if dbg is not None and dbg[0] in ("gi", "gn", "gs"):
                            P.op("dve", lambda e, p_ap=p_ap, oc=oc, t=t: e.tensor_copy(out=h[:, oc, T(t)], in_=p_ap[:]),
                                 reads=[p_res, r_h[t]], writes=[r_h[t]])
                        return
                    if dbg is not None and dbg[0] in ("gi", "gn", "gs"):
                        return
                    ov = oc - 8
                    i = ov % 2
                    vp, vr = pend[ov]
                    if dbg is not None and dbg[0] in ("gv", "gg"):
                        src_, sr_ = (vp, vr) if dbg[0] == "gv" else (p_ap, p_res)
                        P.op("dve", lambda e, src_=src_, ov=ov, t=t: e.tensor_copy(out=h[:, ov, T(t)], in_=src_[:]),
                             reads=[sr_, vr, p_res, r_h[t]], writes=[r_h[t]])
                        return
                    P.op("act", lambda e, i=i, p_ap=p_ap: e.activation(out=sg[i], in_=p_ap[:], func=AF.Sigmoid),
                         reads=[p_res], writes=[r_sg[i]])
                    P.op("dve", lambda e, i=i, vp=vp: e.tensor_tensor(out=yv[i], in0=vp[:], in1=sg[i], op=ALU.mult),
                         reads=[vr, r_sg[i]], writes=[r_yv[i]])
                    P.op("dve", lambda e, i=i, ov=ov, t=t: e.scalar_tensor_tensor(
                        out=h[:, ov, T(t)], in0=yv[i], scalar=mod_gate(n, ov, b), in1=h[:, ov, T(t)],
                        op0=ALU.mult, op1=ALU.add), reads=[r_yv[i], r_mod, r_h[t]], writes=[r_h[t]])
                for sub in (0, 256):
                    gemm_fm(u, r_u, 8, wv, [[c0 + sub, c0 + sub + 128]], epi, tiles=[t])
                    gemm_fm(u, r_u, 8, wv, [[1024 + c0 + sub, 1024 + c0 + sub + 128]], epi, tiles=[t])

    def kv_phase(b):
        norm(8, b, u, r_u)
        wv = wview(wkv_b)
        stg = [carve(i * 1024, [128, 512], BF16) for i in range(4)]
        r_stg = [Res() for _ in range(4)]
        d_stg = [P.dsem() for _ in range(4)]
        cnt = [0]

        def epi(t, oc, p_ap, p_res):
            i = cnt[0] % 4
            cnt[0] += 1
            P.op("act", lambda e, i=i, p_ap=p_ap: e.activation(out=stg[i], in_=p_ap[:], func=AF.Copy),
                 reads=[p_res], writes=[r_stg[i]])
            me = P.op("sp", lambda e, i=i, oc=oc, t=t: e.dma_start(out=kt_s[b, oc, :, T(t)], in_=stg[i]),
                      reads=[r_stg[i]], dma=d_stg[i])
            r_stg[i].r.append(me)
            r_kv.w = me if r_kv.w is None or True else r_kv.w
            kv_w.append(me)
        gemm_fm(u, r_u, 8, wv, [[c0 + 128 * j for j in range(4)] for c0 in range(0, 3072, 512)], epi)
        for c0 in range(3072, 6144, 512):
            si = load_slab(wv[:, 0:8, c0:c0 + 512])
            for tb in range(16):
                bk = rot["a"] % 6
                rot["a"] += 1
                for kc in range(8):
                    P.op("pe", lambda e, si=si, kc=kc, tb=tb, bk=bk: e.matmul(
                        ps[bk][:], lhsT=u[:, kc, tb * 128:(tb + 1) * 128], rhs=slabs[si][:, kc, :],
                        start=(kc == 0), stop=(kc == 7)), reads=[r_slab[si], r_u[tb // 4]], writes=[psr[bk]])
                i = cnt[0] % 4
                cnt[0] += 1
                P.op("act", lambda e, i=i, bk=bk: e.activation(out=stg[i], in_=ps[bk][:], func=AF.Copy),
                     reads=[psr[bk]], writes=[r_stg[i]])
                me = P.op("sp", lambda e, i=i, tb=tb, c0=c0: e.dma_start(
                    out=v_s[b, tb * 128:(tb + 1) * 128, c0 - 3072:c0 - 3072 + 512], in_=stg[i]),
                    reads=[r_stg[i]], dma=d_stg[i])
                r_stg[i].r.append(me)
                kv_w.append(me)

    r_kv = Res()
    kv_w = []
    DIL = (1, 4, 16)

    def sl_(start, n, step):
        return slice(start, start + (n - 1) * step + 1, step)

    def attention(l, b):
        j_ = l - 2
        n = l * 2
        qT = carve(0, [128, 3, S], BF16)
        kT = carve(12288, [128, 3, S], BF16)
        Vt = carve(24576, [128, 3, 16, 128], BF16)
        pT = [carve(36864, [128, 512], BF16), carve(37888, [128, 512], BF16)]
        rec = carve(38912, [128, 1024], F32)
        oT = big[:, :].rearrange("p (a b) -> p a b", b=S)
        r_q = Res(); r_k = Res(); r_v = Res(); r_p = [Res(), Res()]; r_rec = Res(); r_qs = Res(); r_o = Res()
        d_k = next_misc(); d_v = next_misc(); d_q = next_misc()
        wqv = wview(wq_b[j_])
        pcount = [0]
        for hp in range(8):
            for br in range(3):
                P.op("sp", lambda e, br=br, hp=hp: e.dma_start(out=kT[:, br, :], in_=kt_s[b, br * 8 + hp]),
                     reads=[r_kv], writes=[r_k], dma=d_k)
                d = DIL[br]
                vsrc = v_s[b].rearrange("(n p r) f -> p r n f", p=128, r=d)[:, :, :, br * 1024 + hp * 128: br * 1024 + hp * 128 + 128]
                for r in range(d):
                    nb_ = 16 // d
                    P.op("sp", lambda e, br=br, r=r, nb_=nb_, vsrc=vsrc: e.dma_start(
                        out=Vt[:, br, r * nb_:(r + 1) * nb_, :], in_=vsrc[:, r, :, :]),
                        reads=[r_kv], writes=[r_v], dma=d_v)
            for br in range(3):
                c0 = br * 1024 + hp * 128
                si = load_slab(wqv[:, :, c0:c0 + 128])
                for t in range(4):
                    bk = rot["a"] % 4
                    rot["a"] += 1
                    for kc in range(8):
                        P.op("pe", lambda e, kc=kc, t=t, bk=bk, si=si: e.matmul(ps[bk][:], lhsT=slabs[si][:, kc, 0:128], rhs=u[:, kc, T(t)],
                                                                         start=(kc == 0), stop=(kc == 7)),
                             reads=[r_slab[si], r_u[t]], writes=[psr[bk]])
                    P.op("act", lambda e, br=br, t=t, bk=bk: e.activation(out=qT[:, br, T(t)], in_=ps[bk][:], func=AF.Copy,
                                                                          scale=0.125), reads=[psr[bk]], writes=[r_q])
            for qh in range(2):
                NB = (4, 5)
                DB = (6, 7)
                first = {}
                for hh in range(2):
                    hs = slice(64 * hh, 64 * hh + 64)
                    tl = []
                    for nn in range(8):
                        nblk = qh * 8 + nn
                        for kb, msk in ((nblk - 1, mprev), (nblk, mcur)):
                            if kb < 0:
                                continue
                            tl.append((0, slice(kb * 128, kb * 128 + 128), slice(nblk * 128, nblk * 128 + 128),
                                       msk[:, :], kb, nn // 4, slice((nn % 4) * 128, (nn % 4) * 128 + 128), 128))
                    for r in range(4):
                        for nn in range(2):
                            nblk = qh * 2 + nn
                            for kb, msk in ((nblk - 1, mprev), (nblk, mcur)):
                                if kb < 0:
                                    continue
                                tl.append((1, sl_(r + 512 * kb, 128, 4),
                                           sl_(r + 512 * nblk, 128, 4), msk[:, :], r * 4 + kb,
                                           nn, sl_(r, 128, 4), 128))
                    for r in range(16):
                        for mm in range(2):
                            m_ = qh * 2 + mm
                            tl.append((2, sl_(r, 128, 16), sl_(r + 512 * m_, 32, 16),
                                       mcur[:, 32 * m_:32 * m_ + 32], r, mm, sl_(r, 32, 16), 32))
                    i0 = 0
                    while i0 < len(tl):
                        grp = []
                        w_ = 0
                        while i0 < len(tl) and w_ + tl[i0][7] <= 512:
                            grp.append((tl[i0], w_))
                            w_ += tl[i0][7]
                            i0 += 1
                        bk = pcount[0] % 2 * 1 + 0
                        bk = pcount[0] % 4
                        pi = pcount[0] % 2
                        pcount[0] += 1
                        for (br, kap, qap, mk, vb, ob, oc_, nc_), off in grp:
                            P.op("pe", lambda e, br=br, kap=kap, qap=qap, off=off, nc_=nc_, bk=bk, hs=hs: e.matmul(
                                ps[bk][:, off:off + nc_], lhsT=kT[hs, br, kap], rhs=qT[hs, br, qap], start=True, stop=False),
                                reads=[r_k, r_q], writes=[psr[bk]])
                            P.op("pe", lambda e, mk=mk, off=off, nc_=nc_, bk=bk: e.matmul(
                                ps[bk][:, off:off + nc_], lhsT=identb[:], rhs=mk, start=False, stop=True),
                                writes=[psr[bk]])
                        P.op("act", lambda e, bk=bk, pi=pi, w_=w_: e.activation(out=pT[pi][:, 0:w_], in_=ps[bk][:, 0:w_], func=AF.Exp),
                             reads=[psr[bk]], writes=[r_p[pi]])
                        for (br, kap, qap, mk, vb, ob, oc_, nc_), off in grp:
                            for (bank, lhs) in ((NB[ob], Vt[:, br, vb, hs]), (DB[ob], onesb[:, 0:64])):
                                key = (bank, hh)
                                st_ = key not in first
                                first[key] = 1
                                P.op("pe", lambda e, bank=bank, lhs=lhs, oc_=oc_, pi=pi, off=off, nc_=nc_, st_=st_, hs=hs: e.matmul(
                                    ps[bank][hs, oc_], lhsT=lhs, rhs=pT[pi][:, off:off + nc_], start=st_, stop=False,
                                    skip_group_check=True),
                                    reads=[r_p[pi], r_v], writes=[psr[bank]])
                for ob in range(2):
                    P.op("dve", lambda e, ob=ob: e.reciprocal(out=rec[:, ob * 512:(ob + 1) * 512], in_=ps[DB[ob]][:]),
                         reads=[psr[DB[ob]]], writes=[r_rec])
                    P.op("dve", lambda e, ob=ob, hp=hp, qh=qh: e.tensor_tensor(
                        out=oT[:, hp, qh * 1024 + ob * 512: qh * 1024 + (ob + 1) * 512], in0=ps[NB[ob]][:],
                        in1=rec[:, ob * 512:(ob + 1) * 512], op=ALU.mult),
                        reads=[psr[NB[ob]], r_rec], writes=[r_o])
        rot["a"] = 0
        gemm_fm(oT, r_o, 8, wview(wo_b[j_]), [[c0 + 128 * j for j in range(4)] for c0 in (0, 512)], resid_epi(n, b))

    step = [0]

    def done():
        step[0] += 1
        return step[0] >= stop_after

    def u_to_h():
        P.barrier()
        for t in range(4):
            for kc in range(8):
                P.op("act", lambda e, t=t, kc=kc: e.activation(out=h[:, kc, T(t)], in_=u[:, kc, T(t)], func=AF.Copy),
                     reads=[r_u[t]], writes=[r_h[t]])

    for b in range(nseq):
        P.barrier()
        load_x(b)
        stopped = False
        for l in range(4):
            if l == 2:
                P.barrier()
                kv_w.clear()
                kv_phase(b)
                r_kv.w = None
                P.barrier()
            P.barrier()
            if dbg == ("w", l):
                wtmp = big[:, :].rearrange("p (a b) -> p a b", b=2048)
                rw_ = Res()
                P.op("sp", lambda e: e.dma_start(out=wtmp, in_=wview(wglu_b[0])), reads=[r_wcast, r_wcast2], writes=[rw_], dma=d_tok[1])
                P.op("sp", lambda e: e.dma_start(out=dbg_w.rearrange("(kc p) n -> p kc n", p=128), in_=wtmp), reads=[rw_], writes=[rw_], dma=d_tok[1])
                stopped = True
                break
            if dbg != ("m", l):
                norm(l * 2, b, u, r_u)
            if dbg == ("u", l):
                u_to_h()
                stopped = True
                break
            if dbg == ("m", l):
                pass
            elif dbg == ("gs", l):
                L = s5_layer_setup(l)
                s5_mixer(l, b, L)
                P.barrier()
                norm(l * 2, b, u, r_u)
                P.barrier()
                glu(l, b)
                stopped = True
                break
            elif dbg == ("gn", l):
                glu(l, b)
                stopped = True
                break
            elif l < 2:
                L = s5_layer_setup(l)
                if dbg == ("p", l):
                    P.barrier()
                    stopped = True
                    break
                s5_mixer(l, b, L)
                P.barrier()
                if dbg is not None and len(dbg) > 2 and dbg[2] == "fix":
                    norm(l * 2, b, big[:, :].rearrange("p (a b) -> p a b", b=S), [Res() for _ in range(4)])
                    P.barrier()
                if dbg is not None and dbg[:2] == ("pu", l):
                    for t_ in range(4):
                        for kc_ in range(8):
                            bk_ = (t_ * 8 + kc_) % 4
                            P.op("pe", lambda e, t_=t_, kc_=kc_, bk_=bk_: e.matmul(ps[bk_][:], lhsT=identb[:], rhs=u[:, kc_, T(t_)],
                                                                                   start=True, stop=True), reads=[r_u[t_]], writes=[psr[bk_]])
                            P.op("dve", lambda e, t_=t_, kc_=kc_, bk_=bk_: e.tensor_copy(out=h[:, kc_, T(t_)], in_=ps[bk_][:]),
                                 reads=[psr[bk_], r_h[t_]], writes=[r_h[t_]])
                    stopped = True
                    break
                if dbg is not None and dbg[:2] in (("z", l), ("y", l)):
                    u_to_h()
                    stopped = True
                    break
                glu(l, b)
            else:
                attention(l, b)
            if done():
                stopped = True
                break
            P.barrier()
            norm(l * 2 + 1, b, u, r_u)
            mlp(l, b)
            if done():
                stopped = True
                break
        step[0] = 0
        P.barrier()
        store_out(b, final=not stopped)
    P.barrier()
    P.op("sp", lambda e: e.dma_start(out=tok[0][0:1, 0:8], in_=x_d[0, 0:1, 0:8]), dma=d_tok[0])
    P.emit([(d_tok[0], P.cnt[d_tok[0]])] + [(k, P.cnt[k]) for k in d_otok])
    st.close()
    return nc


def _bf16(a):
    import ml_dtypes
    return np.asarray(a, dtype=np.float32).astype(ml_dtypes.bfloat16)


def make_in_maps(inp):
    f = lambda a: np.ascontiguousarray(np.asarray(a, dtype=np.float32))
    x = f(inp["x"]); c = f(inp["c"])
    adaw = np.concatenate([f(inp["ada_w"]).reshape(8, D, 3072)[i] for i in range(8)] + [f(inp["kv_ada_w"])], axis=1)
    adab_flat = np.concatenate([f(inp["ada_b"]).reshape(-1), f(inp["kv_ada_b"])])
    adab = np.ascontiguousarray(adab_flat.reshape(NCH, 128).T)
    lng_all = np.concatenate([f(inp["ln_g"]).reshape(8, D), f(inp["kv_g"]).reshape(1, D)], axis=0)
    lng = np.ascontiguousarray(lng_all.reshape(9, 8, 128).transpose(2, 0, 1))
    fing = np.ascontiguousarray(np.broadcast_to(f(inp["final_g"])[None, :], (128, D)))
    dsk = np.ascontiguousarray(f(inp["ssm_d"]).reshape(2, 8, 128).transpose(2, 0, 1))
    kk = np.arange(128)[:, None]; qq = np.arange(128)[None, :]
    mcur = np.where(kk <= qq, 0.0, -30000.0).astype(np.float32)
    mprev = np.where(kk >= qq, 0.0, -30000.0).astype(np.float32)
    common = dict(
        adaw=np.ascontiguousarray(adaw), adab=adab, lng=lng, fing=fing, dsk=dsk,
        lamre=f(inp["ssm_lam_re"]), lamim=f(inp["ssm_lam_im"]), logdt=f(inp["ssm_log_dt"]).reshape(2, 64, 1),
        bre=f(inp["ssm_b_re"]).reshape(2, 64, 1024), bim=f(inp["ssm_b_im"]).reshape(2, 64, 1024),
        cre=f(inp["ssm_c_re"]).reshape(2, 1024, 64), cim=f(inp["ssm_c_im"]).reshape(2, 1024, 64),
        identf=np.eye(128, dtype=np.float32), identb=_bf16(np.eye(128)), mcur=_bf16(mcur), mprev=_bf16(mprev),
        w1=f(inp["mlp_w1"]), w2=f(inp["mlp_w2"]), wglu=f(inp["ssm_w_glu"]), wkv=f(inp["w_kv"]),
        wq=f(inp["attn_w_q"]), wo=f(inp["attn_w_o"]),
    )
    maps = []
    for i in range(8):
        m = dict(common)
        m["x"] = np.ascontiguousarray(x[2 * i:2 * i + 2])
        m["cT"] = np.ascontiguousarray(c[2 * i:2 * i + 2].T.reshape(8, 128, 2).transpose(1, 0, 2))
        maps.append(m)
    return maps


def kernel(**inputs):
    nc = build_program()
    maps = make_in_maps(inputs)
    res = run_bass_kernel_spmd(nc, maps, core_ids=list(range(8)))
    return np.concatenate([r["out"] for r in res.results], axis=0).astype(np.float32)
```

```python
import contextlib
import numpy as np
import concourse.bass as bass
import concourse.mybir as mybir
from concourse.bass_utils import run_bass_kernel_spmd

F32 = mybir.dt.float32
BF16 = mybir.dt.bfloat16
I32 = mybir.dt.int32
AF = mybir.ActivationFunctionType
ALU = mybir.AluOpType

ENGS = ("pe", "act", "dve", "pool", "sp")
S = 2048
D = 1024
NMOD = 26624
NCH = NMOD // 128
EPS = 1e-6


class Res:
    __slots__ = ("w", "r")

    def __init__(self):
        self.w = None
        self.r = []


class Prog:
    def __init__(self, nc):
        self.nc = nc
        self.ops = {e: [] for e in ENGS}
        self.cnt = {}
        self.seen = {e: {} for e in ENGS}
        self.sems = {}
        self.nd = 0
        self.bar = None
        self.bar_done = {e: None for e in ENGS}

    def dsem(self):
        k = "d%d" % self.nd
        self.nd += 1
        self.cnt[k] = 0
        return k

    def barrier(self):
        self.bar = dict(self.cnt)

    def op(self, eng, fn, reads=(), writes=(), dma=None, after=()):
        waits = {}
        for k_, v_ in after:
            waits[k_] = v_

        def need(dep, war=False):
            if dep is None:
                return
            k, v = dep
            if k == eng and (eng == "pe" or (war and eng != "pool")):
                return
            if waits.get(k, 0) < v:
                waits[k] = v

        if self.bar is not None and self.bar_done[eng] is not self.bar:
            self.bar_done[eng] = self.bar
            for k, v in self.bar.items():
                if v > 0 and k != eng:
                    waits[k] = v
        for r in reads:
            need(r.w)
        for w in writes:
            need(w.w)
            for d in w.r:
                need(d, war=True)
        final = []
        seen = self.seen[eng]
        for k, v in waits.items():
            if seen.get(k, 0) >= v:
                continue
            seen[k] = v
            final.append((k, v))
        key, inc = (eng, 1) if dma is None else (dma, 16)
        self.cnt[key] = self.cnt.get(key, 0) + inc
        me = (key, self.cnt[key])
        self.ops[eng].append((final, fn, key, inc))
        for r in reads:
            r.r.append(me)
            if len(r.r) > 64:
                r.r = _compress(r.r)
        for w in writes:
            w.w = me
            w.r = []
        return me

    def emit(self, final_waits):
        nc = self.nc
        with contextlib.ExitStack() as st:
            for k in list(self.cnt.keys()):
                self.sems[k] = st.enter_context(nc.semaphore("s_" + k))
            block = st.enter_context(nc.Block())
            sems = self.sems

            def run(engname, e):
                for waits, fn, key, inc in self.ops[engname]:
                    for k, v in waits:
                        e.wait_ge(sems[k], v)
                    fn(e).then_inc(sems[key], inc)
                if engname == "sp":
                    for k, v in final_waits:
                        e.wait_ge(sems[k], v)

            @block.tensor
            def _(e):
                run("pe", e)

            @block.scalar
            def _(e):
                run("act", e)

            @block.vector
            def _(e):
                run("dve", e)

            @block.gpsimd
            def _(e):
                run("pool", e)

            @block.sync
            def _(e):
                run("sp", e)


def _compress(lst):
    best = {}
    for k, v in lst:
        if best.get(k, 0) < v:
            best[k] = v
    return list(best.items())


def build_program(stop_after=99, dbg=None, nseq=2):
    nc = bass.Bass("TRN2", target_bir_lowering=False)
    P = Prog(nc)

    def din(name, shape, dt=F32):
        return nc.dram_tensor(name, list(shape), dt, kind="ExternalInput").ap()

    def dscr(name, shape, dt=BF16):
        return nc.dram_tensor(name, list(shape), dt, kind="Internal").ap()

    x_d = din("x", [2, S, D])
    cT_d = din("cT", [128, 8, 2])
    adaw_d = din("adaw", [D, NMOD])
    adab_d = din("adab", [128, NCH])
    lng_d = din("lng", [128, 9, 8])
    fing_d = din("fing", [128, D])
    dsk_d = din("dsk", [128, 2, 8])
    lamre_d = din("lamre", [2, 64, 64])
    lamim_d = din("lamim", [2, 64, 64])
    logdt_d = din("logdt", [2, 64, 1])
    bre_d = din("bre", [2, 64, 1024])
    bim_d = din("bim", [2, 64, 1024])
    cre_d = din("cre", [2, 1024, 64])
    cim_d = din("cim", [2, 1024, 64])
    identf_d = din("identf", [128, 128])
    identb_d = din("identb", [128, 128], BF16)
    mcur_d = din("mcur", [128, 128], BF16)
    mprev_d = din("mprev", [128, 128], BF16)
    w1_d = din("w1", [4, D, 4096])
    w2_d = din("w2", [4, 4096, D])
    wglu_d = din("wglu", [2, D, 2048])
    wkv_d = din("wkv", [D, 6144])
    wq_d = din("wq", [2, D, 3072])
    wo_d = din("wo", [2, D, D])
    out_d = nc.dram_tensor("out", [2, S, D], F32, kind="ExternalOutput").ap()
    if dbg is not None and dbg[0] == "w":
        dbg_w = nc.dram_tensor("dbg_w", [D, 2048], BF16, kind="ExternalOutput").ap()
    if dbg is not None and dbg[0] == "p":
        dbg_l = nc.dram_tensor("dbg_l", [64, 256], F32, kind="ExternalOutput").ap()
        dbg_bb = nc.dram_tensor("dbg_bb", [64, 16, 2, 64], BF16, kind="ExternalOutput").ap()

    w1_b = dscr("w1b", [4, D, 4096])
    w2_b = dscr("w2b", [4, 4096, D])
    wglu_b = dscr("wglub", [2, D, 2048])
    wkv_b = dscr("wkvb", [D, 6144])
    wq_b = dscr("wqb", [2, D, 3072])
    wo_b = dscr("wob", [2, D, D])
    kt_s = dscr("kts", [2, 24, 128, S])
    v_s = dscr("vs", [2, S, 3072])
    bb_s = dscr("bbs", [2, 64, 16, 128])
    bb_s1 = dscr("bbs1", [2, 64, 16, 128])

    st = contextlib.ExitStack()

    def sb(name, shape, dt):
        return st.enter_context(nc.sbuf_tensor(name, list(shape), dt))

    h = sb("h", [128, 8, S], F32)
    u = sb("u", [128, 8, S], BF16)
    big = sb("big", [128, 16384], BF16)
    slabs = [sb("slab%d" % i, [128, 8, 512], BF16) for i in range(3)]
    sq = [sb("sq%d" % i, [128, 512], BF16) for i in range(2)]
    rs = sb("rs", [128, 512], F32)
    rstd = rs
    tmpf = [sb("tmpf%d" % i, [128, 512], F32) for i in range(2)]
    identf = sb("identf_s", [128, 128], F32)
    identb = sb("identb_s", [128, 128], BF16)
    onesb = sb("onesb", [128, 128], BF16)
    mcur = sb("mcur_s", [128, 128], BF16)
    mprev = sb("mprev_s", [128, 128], BF16)
    mod = sb("mod", [128, NCH, 2], F32)
    adab = sb("adab_s", [128, NCH], F32)
    lng = sb("lng_s", [128, 9, 8], F32)
    Gm = sb("Gm", [128, 9, 8, 2], F32)
    dsk = sb("dsk_s", [128, 2, 8], F32)
    cT = sb("cT_s", [128, 8, 2], F32)
    scT = sb("scT", [128, 8, 2], F32)
    arena = sb("arena", [128, 10752], F32)

    def carve(off, shape, dt, parts=128):
        nel = int(np.prod(shape[1:]))
        nb = nel * (2 if dt == BF16 else 4)
        assert off % 4 == 0 and off + nb <= 10752 * 4, (off, nb)
        ap = arena[0:parts, off // 4:(off + nb + 3) // 4]
        if dt == BF16:
            ap = ap.bitcast(BF16)
        elif dt == I32:
            ap = ap.bitcast(I32)
        if len(shape) == 3:
            ap = ap.rearrange("p (a b) -> p a b", b=shape[2])
        elif len(shape) == 4:
            ap = ap.rearrange("p (a b c) -> p a b c", b=shape[2], c=shape[3])
        return ap

    ps = [st.enter_context(nc.psum_tensor("ps%d" % i, [128, 512], F32)) for i in range(8)]
    psr = [Res() for _ in range(8)]

    r_h = [Res() for _ in range(4)]
    r_u = [Res() for _ in range(4)]
    r_big = Res()
    r_slab = [Res() for _ in range(3)]
    d_slab = [P.dsem() for _ in range(3)]
    r_sq = [Res(), Res()]
    r_rs = Res()
    r_rstd = Res()
    r_tmpf = [Res(), Res()]
    r_const = Res()
    r_mod = Res()
    d_const = P.dsem()
    d_constp = P.dsem()
    d_misc = [P.dsem() for _ in range(8)]
    misc_i = [0]
    slab_i = [0]
    rot = {"a": 0}

    def next_misc():
        k = d_misc[misc_i[0] % len(d_misc)]
        misc_i[0] += 1
        return k

    for (dst, src) in [(identf, identf_d), (adab, adab_d), (lng, lng_d), (dsk, dsk_d), (cT, cT_d)]:
        P.op("sp", lambda e, dst=dst, src=src: e.dma_start(out=dst[:], in_=src), writes=[r_const], dma=d_const)
    for (dst, src) in [(identb, identb_d), (mcur, mcur_d), (mprev, mprev_d)]:
        P.op("pool", lambda e, dst=dst, src=src: e.dma_start(out=dst[:], in_=src), writes=[r_const], dma=d_constp)
    r_const.w = (d_const, P.cnt[d_const])
    P.op("dve", lambda e: e.memset(onesb[:], 1.0), writes=[r_const])
    r_const.w = None
    d_castAB = [P.dsem(), P.dsem()]
    cast_n = [0]
    r_wcast = Res()
    r_wcast2 = Res()

    def cast2d(dst, src, rows, cols):
        for r0 in range(0, rows, 128):
            for c0 in range(0, cols, 1024):
                c1 = min(cols, c0 + 1024)
                g_ = cast_n[0] // 24
                dk = d_castAB[g_ % 2]
                aft = [(dk, P.cnt[dk])] if (cast_n[0] % 24 == 0 and g_ >= 2) else []
                cast_n[0] += 1
                P.op("pool", lambda e, r0=r0, c0=c0, c1=c1: e.dma_start(
                    out=dst[r0:r0 + 128, c0:c1], in_=src[r0:r0 + 128, c0:c1], max_dma_last_dim=4096),
                    dma=dk, after=aft)

    for l in range(2):
        cast2d(wglu_b[l], wglu_d[l], D, 2048)
    for l in range(4):
        cast2d(w1_b[l], w1_d[l], D, 4096)
        cast2d(w2_b[l], w2_d[l], 4096, D)
    cast2d(wkv_b, wkv_d, D, 6144)
    for l in range(2):
        cast2d(wq_b[l], wq_d[l], D, 3072)
        cast2d(wo_b[l], wo_d[l], D, D)
    r_wcast.w = (d_castAB[0], P.cnt[d_castAB[0]])
    r_wcast2.w = (d_castAB[1], P.cnt[d_castAB[1]])

    P.barrier()
    P.op("act", lambda e: e.activation(out=scT[:], in_=cT[:], func=AF.Silu), writes=[r_const])
    aslab = [carve(0, [128, 8, 512], F32), carve(16384, [128, 8, 512], F32)]
    r_aslab = [Res(), Res()]
    d_aslab = [P.dsem(), P.dsem()]
    adaw_v = adaw_d.rearrange("(kc p) n -> p kc n", p=128)
    nsl = NMOD // 512
    for s_ in range(nsl):
        bi = s_ % 2
        P.op("sp", lambda e, s_=s_, bi=bi: e.dma_start(out=aslab[bi], in_=adaw_v[:, :, s_ * 512:(s_ + 1) * 512]),
             writes=[r_aslab[bi]], dma=d_aslab[bi])
        for j in range(4):
            ch = s_ * 4 + j
            for kc in range(8):
                P.op("pe", lambda e, bi=bi, j=j, kc=kc, ch=ch: e.matmul(
                    ps[7][:, 2 * ch:2 * ch + 2], lhsT=aslab[bi][:, kc, j * 128:(j + 1) * 128], rhs=scT[:, kc, :],
                    start=(kc == 0), stop=(kc == 7)), reads=[r_aslab[bi], r_const], writes=[psr[7]])
    P.op("dve", lambda e: e.tensor_tensor(out=mod[:], in0=ps[7][:, 0:2 * NCH].rearrange("p (c b) -> p c b", b=2),
                                          in1=adab[:].unsqueeze(2).to_broadcast([128, NCH, 2]), op=ALU.add),
         reads=[psr[7]], writes=[r_mod])
    for n in range(9):
        base = (n * 3072 + 1024) // 128 if n < 8 else (24576 + 1024) // 128
        P.op("dve", lambda e, n=n, base=base: e.scalar_tensor_tensor(
            out=Gm[:, n, :, :], in0=mod[:, base:base + 8, :], scalar=1.0,
            in1=lng[:, n, :].unsqueeze(2).to_broadcast([128, 8, 2]), op0=ALU.add, op1=ALU.mult),
            reads=[r_mod], writes=[r_mod])

    def mod_shift(n, kc, b):
        base = (n * 3072) // 128 if n < 8 else 24576 // 128
        return mod[:, base + kc, b:b + 1]

    def mod_gate(n, kc, b):
        base = (n * 3072 + 2048) // 128
        return mod[:, base + kc, b:b + 1]

    def T(t):
        return slice(t * 512, (t + 1) * 512)

    def norm(n, b, out_buf, r_out):
        for t in range(4):
            for kc in range(8):
                i = kc % 2
                P.op("act", lambda e, kc=kc, i=i, t=t: e.activation(out=sq[i][:], in_=h[:, kc, T(t)], func=AF.Square),
                     reads=[r_h[t]], writes=[r_sq[i]])
                P.op("pe", lambda e, kc=kc, i=i: e.matmul(ps[6][:], lhsT=onesb[:], rhs=sq[i][:],
                                                          start=(kc == 0), stop=(kc == 7)),
                     reads=[r_sq[i]], writes=[psr[6]])
            P.op("act", lambda e: e.activation(out=rs[:], in_=ps[6][:], func=AF.Sqrt, bias=EPS, scale=1.0 / D),
                 reads=[psr[6]], writes=[r_rs, r_rstd])
            P.op("dve", lambda e: e.reciprocal(out=rstd[:], in_=rs[:]), reads=[r_rs], writes=[r_rs, r_rstd])
            for kc in range(8):
                i = kc % 2
                P.op("pool", lambda e, kc=kc, i=i, t=t: e.tensor_tensor(out=tmpf[i][:], in0=h[:, kc, T(t)], in1=rstd[:],
                                                                         op=ALU.mult),
                     reads=[r_h[t], r_rstd], writes=[r_tmpf[i]])
                P.op("act", lambda e, kc=kc, i=i, t=t: e.activation(
                    out=out_buf[:, kc, T(t)], in_=tmpf[i][:], func=AF.Identity,
                    bias=mod_shift(n, kc, b), scale=Gm[:, n, kc, b:b + 1]),
                    reads=[r_tmpf[i], r_mod], writes=[r_out[t]])

    def load_slab(view):
        i = slab_i[0] % 3
        slab_i[0] += 1
        P.op("sp", lambda e, i=i, view=view: e.dma_start(out=slabs[i][:, 0:view.shape[1], 0:view.shape[2]], in_=view),
             reads=[r_wcast, r_wcast2], writes=[r_slab[i]], dma=d_slab[i])
        return i

    def gemm_fm(src, r_src, KC, wv, colgroups, epi, tiles=range(4)):
        for t in tiles:
            for grp in colgroups:
                c0 = grp[0]
                banks = []
                for _ in grp:
                    banks.append(rot["a"] % 6)
                    rot["a"] += 1
                for kg in range(KC // 8):
                    si = load_slab(wv[:, kg * 8:(kg + 1) * 8, c0:grp[-1] + 128])
                    for j, c in enumerate(grp):
                        for kc in range(8):
                            kk = kg * 8 + kc
                            P.op("pe", lambda e, si=si, j=j, c=c, kc=kc, kk=kk, bk=banks[j], t=t, c0=c0: e.matmul(
                                ps[bk][:], lhsT=slabs[si][:, kc, c - c0:c - c0 + 128], rhs=src[:, kk, T(t)],
                                start=(kk == 0), stop=(kk == KC - 1)),
                                reads=[r_slab[si], r_src[t] if isinstance(r_src, list) else r_src], writes=[psr[banks[j]]])
                for j, c in enumerate(grp):
                    epi(t, c // 128, ps[banks[j]], psr[banks[j]])

    def wview(w2d):
        return w2d.rearrange("(kc p) n -> p kc n", p=128)

    def resid_epi(n, b):
        def epi(t, oc, p_ap, p_res):
            P.op("dve", lambda e, t=t, oc=oc, p_ap=p_ap: e.scalar_tensor_tensor(
                out=h[:, oc, T(t)], in0=p_ap[:], scalar=mod_gate(n, oc, b), in1=h[:, oc, T(t)],
                op0=ALU.mult, op1=ALU.add), reads=[p_res, r_mod, r_h[t]], writes=[r_h[t]])
        return epi

    hid = big[:, :].rearrange("p (a b) -> p a b", b=512)
    r_hid = Res()
    relu_t = [carve(0, [128, 512], BF16), carve(1024, [128, 512], BF16)]
    r_relu = [Res(), Res()]

    def mlp(l, b):
        n = l * 2 + 1
        w1v = wview(w1_b[l])
        w2v = wview(w2_b[l])
        for t in range(4):
            def epi1(t_, hc, p_ap, p_res):
                i = hc % 2
                P.op("act", lambda e, i=i, p_ap=p_ap: e.activation(out=relu_t[i], in_=p_ap[:], func=AF.Relu),
                     reads=[p_res], writes=[r_relu[i]])
                P.op("pool", lambda e, i=i, hc=hc: e.tensor_tensor(out=hid[:, hc, :], in0=relu_t[i], in1=relu_t[i],
                                                                    op=ALU.mult),
                     reads=[r_relu[i]], writes=[r_hid])
            gemm_fm(u, r_u, 8, w1v, [[c0 + 128 * j for j in range(4)] for c0 in range(0, 4096, 512)], epi1, tiles=[t])

            class HidSrc:
                def __getitem__(self, idx):
                    return hid[idx[0], idx[1], :]
            gemm_fm(HidSrc(), r_hid, 32, w2v, [[c0 + 128 * j for j in range(4)] for c0 in (0, 512)],
                    lambda t_, oc, p_ap, p_res, t=t: resid_epi(n, b)(t, oc, p_ap, p_res), tiles=[t])

    tok = [carve(0, [128, D], F32), carve(4096, [128, D], F32)]
    r_tok = [Res(), Res()]
    d_tok = [P.dsem(), P.dsem()]
    otok = [carve(8192, [128, D], F32), carve(12288, [128, D], F32)]
    r_otok = [Res(), Res()]
    d_otok = [P.dsem(), P.dsem()]
    ss1 = carve(16384, [128, 1], F32)
    ss2 = carve(16400, [128, 1], F32)
    ss3 = carve(16416, [128, 1], F32)
    junk = carve(16448, [128, D], F32)
    fing = carve(20544, [128, D], F32)
    r_ss = Res()
    last_out = []

    def load_x(b):
        for tb in range(16):
            i = tb % 2
            P.op("sp", lambda e, tb=tb, i=i: e.dma_start(out=tok[i], in_=x_d[b, tb * 128:(tb + 1) * 128, :]),
                 writes=[r_tok[i]], dma=d_tok[i])
            for half in range(2):
                bk = rot["a"] % 6
                rot["a"] += 1
                for j in range(4):
                    kc = half * 4 + j
                    P.op("pe", lambda e, i=i, kc=kc, j=j, bk=bk: e.transpose(
                        out=ps[bk][:, j * 128:(j + 1) * 128], in_=tok[i][:, kc * 128:(kc + 1) * 128], identity=identf[:]),
                        reads=[r_tok[i]], writes=[psr[bk]])
                P.op("act", lambda e, half=half, bk=bk, tb=tb: e.activation(
                    out=h[:, half * 4:half * 4 + 4, tb * 128:(tb + 1) * 128],
                    in_=ps[bk][:].rearrange("p (a b) -> p a b", b=128), func=AF.Copy),
                    reads=[psr[bk]], writes=[r_h[tb // 4]])

    def store_out(b, final=True):
        P.op("sp", lambda e: e.dma_start(out=fing, in_=fing_d), writes=[r_ss], dma=d_tok[0])
        for tb in range(16):
            i = tb % 2
            for half in range(2):
                for j in range(4):
                    kc = half * 4 + j
                    P.op("pe", lambda e, kc=kc, j=j, half=half, tb=tb: e.transpose(
                        out=ps[half][:, j * 128:(j + 1) * 128], in_=h[:, kc, tb * 128:(tb + 1) * 128], identity=identf[:]),
                        reads=[r_h[tb // 4]], writes=[psr[half]])
            if final:
                for half in range(2):
                    P.op("act", lambda e, half=half: e.activation(
                        out=junk[:, half * 512:(half + 1) * 512], in_=ps[half][:], func=AF.Square,
                        accum_out=(ss1 if half == 0 else ss2)), reads=[psr[half]], writes=[r_ss])
                P.op("dve", lambda e: e.tensor_tensor(out=ss3, in0=ss1, in1=ss2, op=ALU.add), reads=[r_ss], writes=[r_ss])
                P.op("act", lambda e: e.activation(out=ss1, in_=ss3, func=AF.Sqrt, bias=EPS, scale=1.0 / D),
                     reads=[r_ss], writes=[r_ss])
                P.op("dve", lambda e: e.reciprocal(out=ss2, in_=ss1), reads=[r_ss], writes=[r_ss])
                for half in range(2):
                    P.op("dve", lambda e, half=half, i=i: e.scalar_tensor_tensor(
                        out=otok[i][:, half * 512:(half + 1) * 512], in0=ps[half][:], scalar=ss2,
                        in1=fing[:, half * 512:(half + 1) * 512], op0=ALU.mult, op1=ALU.mult),
                        reads=[psr[half], r_ss], writes=[r_otok[i]])
            else:
                for half in range(2):
                    P.op("act", lambda e, half=half, i=i: e.activation(
                        out=otok[i][:, half * 512:(half + 1) * 512], in_=ps[half][:], func=AF.Copy),
                        reads=[psr[half]], writes=[r_otok[i]])
            me = P.op("sp", lambda e, i=i, tb=tb: e.dma_start(out=out_d[b, tb * 128:(tb + 1) * 128, :], in_=otok[i]),
                      reads=[r_otok[i]], dma=d_otok[i])
            r_otok[i].r.append(me)
            last_out.append(me)

    def s5_prep(l):
        P.barrier()
        o = [0]

        def A(shape, dt=F32, parts=64):
            nel = int(np.prod(shape[1:]))
            nb = ((nel * (2 if dt == BF16 else 4) + 3) // 4) * 4
            ap = carve(o[0], shape, dt, parts)
            o[0] += nb
            return ap
        lre = A([64, 64]); lim = A([64, 64]); ldt = A([64, 1]); dt_ = A([64, 1])
        a_ = A([64, 64]); th = A([64, 64]); ea = A([64, 64]); yy = A([64, 64]); ki = A([64, 64], I32)
        kf = A([64, 64]); ff = A([64, 64]); sn = A([64, 64]); cs = A([64, 64])
        lbr = A([64, 64]); lbi = A([64, 64]); den = A([64, 64]); nr = A([64, 64]); ni = A([64, 64])
        qr = A([64, 64]); qi = A([64, 64]); t1 = A([64, 64]); t2 = A([64, 64])
        br = A([64, 64, 16]); bi_ = A([64, 64, 16])
        bbr = A([64, 64, 16]); bbi = A([64, 64, 16]); t3 = A([64, 64, 16])
        bout = A([64, 16, 2, 64], BF16)
        l2r = A([64, 64]); l2i = A([64, 64])
        R = Res()
        dd = next_misc()
        for dst, src in [(lre, lamre_d[l]), (lim, lamim_d[l]), (ldt, logdt_d[l])]:
            P.op("sp", lambda e, dst=dst, src=src: e.dma_start(out=dst, in_=src), writes=[R], dma=dd)
        P.op("sp", lambda e: e.dma_start(out=br, in_=bre_d[l].rearrange("g (p c) -> g p c", c=16)), writes=[R], dma=dd)
        P.op("sp", lambda e: e.dma_start(out=bi_, in_=bim_d[l].rearrange("g (p c) -> g p c", c=16)), writes=[R], dma=dd)

        def V(fn):
            P.op("dve", fn, reads=[R], writes=[R])

        def ACT(fn):
            P.op("act", fn, reads=[R], writes=[R])
        ACT(lambda e: e.activation(out=dt_, in_=ldt, func=AF.Exp))
        V(lambda e: e.tensor_scalar(out=a_, in0=lre, scalar1=dt_, scalar2=None, op0=ALU.mult))
        V(lambda e: e.tensor_scalar(out=th, in0=lim, scalar1=dt_, scalar2=None, op0=ALU.mult))
        ACT(lambda e: e.activation(out=ea, in_=a_, func=AF.Exp))

        def sin_of(dst, offs):
            V(lambda e: e.tensor_scalar(out=yy, in0=th, scalar1=1.0 / (2 * np.pi), scalar2=offs, op0=ALU.mult, op1=ALU.add))
            V(lambda e: e.tensor_copy(out=ki, in_=yy))
            V(lambda e: e.tensor_copy(out=kf, in_=ki))
            V(lambda e: e.tensor_tensor(out=ff, in0=yy, in1=kf, op=ALU.subtract))
            V(lambda e: e.scalar_tensor_tensor(out=ff, in0=ff, scalar=0.0, in1=ff, op0=ALU.is_lt, op1=ALU.add))
            V(lambda e: e.tensor_scalar(out=ff, in0=ff, scalar1=2 * np.pi, scalar2=-np.pi, op0=ALU.mult, op1=ALU.add))
            V(lambda e: e.tensor_scalar(out=ff, in0=ff, scalar1=-3.14159, scalar2=3.14159, op0=ALU.max, op1=ALU.min))
            ACT(lambda e: e.activation(out=dst, in_=ff, func=AF.Sin))
        sin_of(sn, 0.5)
        sin_of(cs, 0.75)
        V(lambda e: e.tensor_tensor(out=lbr, in0=ea, in1=cs, op=ALU.mult))
        V(lambda e: e.tensor_tensor(out=lbi, in0=ea, in1=sn, op=ALU.mult))
        V(lambda e: e.tensor_scalar(out=nr, in0=lbr, scalar1=-1.0, scalar2=None, op0=ALU.add))
        V(lambda e: e.tensor_tensor(out=t1, in0=lre, in1=lre, op=ALU.mult))
        V(lambda e: e.tensor_tensor(out=t2, in0=lim, in1=lim, op=ALU.mult))
        V(lambda e: e.tensor_tensor(out=den, in0=t1, in1=t2, op=ALU.add))
        V(lambda e: e.reciprocal(out=den, in_=den))
        V(lambda e: e.tensor_tensor(out=t1, in0=nr, in1=lre, op=ALU.mult))
        V(lambda e: e.tensor_tensor(out=t2, in0=lbi, in1=lim, op=ALU.mult))
        V(lambda e: e.tensor_tensor(out=qr, in0=t1, in1=t2, op=ALU.add))
        V(lambda e: e.tensor_tensor(out=qr, in0=qr, in1=den, op=ALU.mult))
        V(lambda e: e.tensor_tensor(out=t1, in0=lbi, in1=lre, op=ALU.mult))
        V(lambda e: e.tensor_tensor(out=t2, in0=nr, in1=lim, op=ALU.mult))
        V(lambda e: e.tensor_tensor(out=qi, in0=t1, in1=t2, op=ALU.subtract))
        V(lambda e: e.tensor_tensor(out=qi, in0=qi, in1=den, op=ALU.mult))
        qrb = qr.unsqueeze(2).to_broadcast([64, 64, 16])
        qib = qi.unsqueeze(2).to_broadcast([64, 64, 16])
        V(lambda e: e.tensor_tensor(out=bbr, in0=br, in1=qrb, op=ALU.mult))
        V(lambda e: e.tensor_tensor(out=t3, in0=bi_, in1=qib, op=ALU.mult))
        V(lambda e: e.tensor_tensor(out=bbr, in0=bbr, in1=t3, op=ALU.subtract))
        V(lambda e: e.tensor_tensor(out=bbi, in0=bi_, in1=qrb, op=ALU.mult))
        V(lambda e: e.tensor_tensor(out=t3, in0=br, in1=qib, op=ALU.mult))
        V(lambda e: e.tensor_tensor(out=bbi, in0=bbi, in1=t3, op=ALU.add))
        V(lambda e: e.tensor_copy(out=bout[:, :, 0, :], in_=bbr.rearrange("g p c -> g c p")))
        V(lambda e: e.tensor_copy(out=bout[:, :, 1, :], in_=bbi.rearrange("g p c -> g c p")))
        P.op("sp", lambda e: e.dma_start(out=bb_s[l].rearrange("g c (r p) -> g c r p", r=2), in_=bout), reads=[R], writes=[R], dma=dd)
        lrb = lbr.unsqueeze(2).to_broadcast([64, 64, 16])
        lib = lbi.unsqueeze(2).to_broadcast([64, 64, 16])
        V(lambda e: e.tensor_tensor(out=br, in0=bbr, in1=lrb, op=ALU.mult))
        V(lambda e: e.tensor_tensor(out=t3, in0=bbi, in1=lib, op=ALU.mult))
        V(lambda e: e.tensor_tensor(out=br, in0=br, in1=t3, op=ALU.subtract))
        V(lambda e: e.tensor_tensor(out=bi_, in0=bbi, in1=lrb, op=ALU.mult))
        V(lambda e: e.tensor_tensor(out=t3, in0=bbr, in1=lib, op=ALU.mult))
        V(lambda e: e.tensor_tensor(out=bi_, in0=bi_, in1=t3, op=ALU.add))
        V(lambda e: e.tensor_copy(out=bout[:, :, 0, :], in_=br.rearrange("g p c -> g c p")))
        V(lambda e: e.tensor_copy(out=bout[:, :, 1, :], in_=bi_.rearrange("g p c -> g c p")))
        P.op("sp", lambda e: e.dma_start(out=bb_s1[l].rearrange("g c (r p) -> g c r p", r=2), in_=bout), reads=[R], writes=[R], dma=dd)
        V(lambda e: e.tensor_tensor(out=t1, in0=lbr, in1=lbr, op=ALU.mult))
        V(lambda e: e.tensor_tensor(out=t2, in0=lbi, in1=lbi, op=ALU.mult))
        V(lambda e: e.tensor_tensor(out=l2r, in0=t1, in1=t2, op=ALU.subtract))
        V(lambda e: e.tensor_tensor(out=t1, in0=lbr, in1=lbi, op=ALU.mult))
        V(lambda e: e.tensor_scalar(out=l2i, in0=t1, scalar1=2.0, scalar2=None, op0=ALU.mult))
        if dbg is not None and dbg[0] == "p":
            for i_, src_ in enumerate((lbr, lbi, qr, qi)):
                P.op("sp", lambda e, i_=i_, src_=src_: e.dma_start(out=dbg_l[:, i_ * 64:(i_ + 1) * 64], in_=src_), reads=[R], writes=[R], dma=dd)
            P.op("sp", lambda e: e.dma_start(out=dbg_bb, in_=bout), reads=[R], writes=[R], dma=dd)
        return l2r, l2i, R

    def s5_layer_setup(l):
        lbr, lbi, R = s5_prep(l)
        cw = carve(32768, [128, 2, 1024], BF16)
        ca = carve(36864, [128, 2, 32], F32)
        cb = carve(37120, [128, 2, 32], F32)
        l2 = [carve(38912, [64, 128], F32, 64), carve(39424, [64, 128], F32, 64)]
        R2 = Res()
        dd = next_misc()
        for src, idx in ((lbr, 0), (lbi, 1)):
            for hf in range(2):
                P.op("dve", lambda e, src=src, idx=idx, hf=hf: e.tensor_copy(out=l2[idx][:, hf * 64:(hf + 1) * 64], in_=src),
                     reads=[R], writes=[R])
            P.op("pe", lambda e, idx=idx: e.transpose(out=ps[6][:, idx * 64:(idx + 1) * 64], in_=l2[idx],
                                                      identity=identf[0:64, 0:64]),
                 reads=[R], writes=[psr[6]])
        for hv in range(2):
            hp_ = slice(64 * hv, 64 * hv + 64)
            gcol = slice(32 * hv, 32 * hv + 32)
            gcol2 = slice(64 + 32 * hv, 64 + 32 * hv + 32)
            P.op("dve", lambda e, hp_=hp_, gcol=gcol: e.tensor_copy(out=ca[hp_, 0, :], in_=ps[6][hp_, gcol]), reads=[psr[6]], writes=[R2])
            P.op("dve", lambda e, hp_=hp_, gcol=gcol: e.tensor_copy(out=ca[hp_, 1, :], in_=ps[6][hp_, gcol]), reads=[psr[6]], writes=[R2])
            P.op("dve", lambda e, hp_=hp_, gcol2=gcol2: e.tensor_copy(out=cb[hp_, 1, :], in_=ps[6][hp_, gcol2]), reads=[psr[6]], writes=[R2])
            P.op("dve", lambda e, hp_=hp_, gcol2=gcol2: e.tensor_scalar(out=cb[hp_, 0, :], in0=ps[6][hp_, gcol2], scalar1=-1.0,
                                                                    scalar2=None, op0=ALU.mult), reads=[psr[6]], writes=[R2])
        P.barrier()
        bbpad = [carve(0, [128, 64, 128], BF16), carve(16384, [128, 64, 128], BF16)]
        cnat = carve(38912, [128, 8, 128], F32)
        for ti_, bsrc in enumerate((bb_s, bb_s1)):
            P.op("pool", lambda e, ti_=ti_: e.memset(bbpad[ti_], 0.0), writes=[R2])
            for g in range(64):
                gl = g % 8
                P.op("sp", lambda e, g=g, gl=gl, ti_=ti_, bsrc=bsrc: e.dma_start(out=bbpad[ti_][16 * gl:16 * gl + 16, g, :], in_=bsrc[l, g]),
                     reads=[R, R2], writes=[R2], dma=dd)
        for idx, cd, sgn in ((0, cre_d, 1.0), (1, cim_d, -1.0)):
            for hf in range(2):
                P.op("sp", lambda e, cd=cd, hf=hf: e.dma_start(out=cnat[:, :, hf * 64:(hf + 1) * 64],
                                                              in_=cd[l].rearrange("(a q) p -> q a p", q=128)),
                     reads=[R2], writes=[R2], dma=dd)
            for half in range(2):
                for j in range(4):
                    a = half * 4 + j
                    P.op("pe", lambda e, a=a, j=j: e.transpose(out=ps[5][:, j * 128:(j + 1) * 128], in_=cnat[:, a, :],
                                                                identity=identf[:]), reads=[R2], writes=[psr[5]])
                P.op("dve", lambda e, idx=idx, half=half, sgn=sgn: e.tensor_scalar(
                    out=cw[:, idx, half * 512:(half + 1) * 512], in0=ps[5][:, :], scalar1=sgn, scalar2=None,
                    op0=ALU.mult), reads=[psr[5]], writes=[R2])
        P.barrier()
        return dict(bbpad=bbpad, cw=cw, ca=ca, cb=cb, R=R2)

    TC = 32

    def s5_mixer(l, b, L):
        R2 = L["R"]
        bbpad, cw, ca, cb = L["bbpad"], L["cw"], L["ca"], L["cb"]
        bigf = big[:, :].bitcast(F32)
        Vbs = [bigf[:, i * 2048:(i + 1) * 2048].rearrange("p (r g t) -> p r g t", r=2, g=32) for i in range(2)]
        Xb = bigf[:, 4096:5120].bitcast(BF16).rearrange("p (r g t) -> p r g t", r=2, g=32)
        ytok = bigf[0:TC, 5120:6144]
        zp = bigf[:, 6144:6400].rearrange("p (k t) -> p k t", t=TC)
        g1 = bigf[:, 6400:6656].rearrange("p (k t) -> p k t", t=TC)
        g2 = bigf[:, 6656:6912].rearrange("p (k t) -> p k t", t=TC)
        Zst = carve(37376, [128, 2, 32, 2], F32)
        m1 = carve(37888, [128, 2, 32, 2], F32)
        m2 = carve(38400, [128, 2, 32, 2], F32)
        rVs = [Res(), Res()]
        rX = Res(); rY = Res(); rZ = Res(); rZs = Res(); rZ2 = Res()
        rm1 = Res(); rm2a = Res(); rm2b = Res()
        r_ul = Res()
        P.op("dve", lambda e: e.memset(Zst, 0.0), writes=[rZs])
        nck = S // TC
        per_bank = 512 // (2 * TC)

        def inproj(ck):
            Vb = Vbs[ck % 2]
            rV = rVs[ck % 2]
            t0 = ck * TC
            t4 = t0 // 512
            for gg0 in range(0, 32, per_bank):
                bk = rot["a"] % 5
                rot["a"] += 1
                for hv in range(2):
                    for j in range(per_bank):
                        g = 32 * hv + gg0 + j
                        for ri in range(2):
                            c_ = (j * 2 + ri) * TC
                            P.op("pe", lambda e, g=g, ri=ri, c_=c_, bk=bk, hv=hv, t0=t0: e.matmul(
                                ps[bk][64 * hv:64 * hv + 64, c_:c_ + TC], lhsT=bbpad[0][:, g, ri * 64:(ri + 1) * 64],
                                rhs=u[:, g // 8, t0:t0 + TC], start=True, stop=False),
                                reads=[R2, r_u[t4]], writes=[psr[bk]])
                            lo = 1 if ck == 0 else 0
                            P.op("pe", lambda e, g=g, ri=ri, c_=c_, bk=bk, hv=hv, t0=t0, lo=lo: e.matmul(
                                ps[bk][64 * hv:64 * hv + 64, c_ + lo:c_ + TC], lhsT=bbpad[1][:, g, ri * 64:(ri + 1) * 64],
                                rhs=u[:, g // 8, t0 - 1 + lo:t0 - 1 + TC], start=False, stop=True),
                                reads=[R2, r_u[t4], r_ul], writes=[psr[bk]])
                P.op("act", lambda e, bk=bk, gg0=gg0, Vb=Vb: e.activation(
                    out=Vb[:, :, gg0:gg0 + per_bank, :].rearrange("p r g t -> p g r t"),
                    in_=ps[bk][:, 0:per_bank * 2 * TC].rearrange("p (g r t) -> p g r t", r=2, t=TC), func=AF.Copy),
                    reads=[psr[bk]], writes=[rV])

        def scan(ck):
            Vb = Vbs[ck % 2]
            rV = rVs[ck % 2]
            for t in range(0, TC, 2):
                if t == 0:
                    prev = Zst[:, :, :, :]
                    pf = [Zst[:, 1, :, :], Zst[:, 0, :, :]]
                    rp = rZs
                else:
                    prev = Vb[:, :, :, t - 2:t]
                    pf = [Vb[:, 1, :, t - 2:t], Vb[:, 0, :, t - 2:t]]
                    rp = rV
                cur = Vb[:, :, :, t:t + 2]
                cab = ca.unsqueeze(3).to_broadcast([128, 2, 32, 2])
                cb0 = cb[:, 0, :].unsqueeze(2).to_broadcast([128, 32, 2])
                cb1 = cb[:, 1, :].unsqueeze(2).to_broadcast([128, 32, 2])
                P.op("dve", lambda e, prev=prev, cab=cab: e.tensor_tensor(out=m1, in0=prev, in1=cab, op=ALU.mult),
                     reads=[rp, R2], writes=[rm1])
                P.op("dve", lambda e, pf=pf, cb0=cb0: e.tensor_tensor(out=m2[:, 0, :, :], in0=pf[0], in1=cb0, op=ALU.mult),
                     reads=[rp, R2], writes=[rm2a])
                P.op("dve", lambda e, pf=pf, cb1=cb1: e.tensor_tensor(out=m2[:, 1, :, :], in0=pf[1], in1=cb1, op=ALU.mult),
                     reads=[rp, R2], writes=[rm2b])
                P.op("dve", lambda e, cur=cur: e.tensor_tensor(out=cur, in0=cur, in1=m1, op=ALU.add),
                     reads=[rm1, rV], writes=[rV])
                P.op("dve", lambda e, cur=cur: e.tensor_tensor(out=cur, in0=cur, in1=m2, op=ALU.add),
                     reads=[rm2a, rm2b, rV], writes=[rV])
            P.op("dve", lambda e, Vb=Vb: e.tensor_copy(out=Zst, in_=Vb[:, :, :, TC - 2:TC]), reads=[rV], writes=[rZs])

        def finalize(ck):
            Vb = Vbs[ck % 2]
            rV = rVs[ck % 2]
            tsl = slice(ck * TC, (ck + 1) * TC)
            t4 = ck * TC // 512
            P.op("pool", lambda e, Vb=Vb: e.tensor_copy(out=Xb, in_=Vb), reads=[rV], writes=[rX])
            for half in range(2):
                for gg in range(32):
                    g = half * 32 + gg
                    hs_ = slice(64 * half, 64 * half + 64)
                    for ri in range(2):
                        P.op("pe", lambda e, g=g, gg=gg, ri=ri, hs_=hs_: e.matmul(
                            ps[5][0:TC, gg * 16:(gg + 1) * 16], lhsT=Xb[hs_, ri, gg, :], rhs=cw[hs_, ri, g * 16:(g + 1) * 16],
                            start=(ri == 0), stop=(ri == 1)), reads=[rX, R2], writes=[psr[5]])
                P.op("act", lambda e, half=half: e.activation(out=ytok[:, half * 512:(half + 1) * 512], in_=ps[5][0:TC, :],
                                                              func=AF.Copy), reads=[psr[5]], writes=[rY])
            for kc in range(8):
                P.op("pe", lambda e, kc=kc: e.transpose(out=ps[6][:, kc * TC:(kc + 1) * TC], in_=ytok[:, kc * 128:(kc + 1) * 128],
                                                        identity=identf[0:TC, 0:TC]), reads=[rY], writes=[psr[6]])
            P.op("pool", lambda e, tsl=tsl: e.tensor_tensor(out=zp, in0=u[:, :, tsl],
                                                           in1=dsk[:, l, :].unsqueeze(2).to_broadcast([128, 8, TC]), op=ALU.mult),
                 reads=[r_u[t4]], writes=[rZ])
            P.op("act", lambda e: e.activation(out=g1, in_=ps[6][:, 0:8 * TC].rearrange("p (k t) -> p k t", t=TC), func=AF.Copy),
                 reads=[psr[6]], writes=[rZ])
            P.op("pool", lambda e: e.tensor_tensor(out=zp, in0=zp, in1=g1, op=ALU.add), reads=[rZ], writes=[rZ])
            P.op("pool", lambda e: e.tensor_tensor(out=g1, in0=zp, in1=zp, op=ALU.mult), reads=[rZ], writes=[rZ])
            P.op("pool", lambda e: e.tensor_scalar(out=g1, in0=g1, scalar1=0.044715, scalar2=1.0, op0=ALU.mult, op1=ALU.add),
                 reads=[rZ], writes=[rZ])
            P.op("pool", lambda e: e.tensor_tensor(out=g1, in0=g1, in1=zp, op=ALU.mult), reads=[rZ], writes=[rZ])
            P.op("pool", lambda e: e.tensor_scalar(out=g1, in0=g1, scalar1=-30.0, scalar2=None, op0=ALU.max), reads=[rZ], writes=[rZ])
            P.op("act", lambda e: e.activation(out=g2, in_=g1, func=AF.Sigmoid, scale=1.5957691216057308),
                 reads=[rZ], writes=[rZ])
            P.op("pool", lambda e, tsl=tsl: e.tensor_tensor(out=u[:, :, tsl], in0=zp, in1=g2, op=ALU.mult),
                 reads=[rZ, r_u[t4]], writes=[rZ2, r_ul])

        for ck in range(nck):
            inproj(ck)
            if ck > 0:
                finalize(ck - 1)
            scan(ck)
        finalize(nck - 1)

    def glu(l, b):
        n = l * 2
        wv = wview(wglu_b[l])
        sg = [carve(0, [128, 512], F32), carve(2048, [128, 512], F32)]
        yv = [carve(4096, [128, 512], F32), carve(6144, [128, 512], F32)]
        r_sg = [Res(), Res()]
        r_yv = [Res(), Res()]
        for t in range(4):
            for c0 in (0, 512):
                pend = {}

                def epi(t_, oc, p_ap, p_res, pend=pend, t=t):
                    if oc < 8:
                        pend[oc] = (p_ap, p_res)
                        if dbg is not None and dbg[0] in ("gi", "gn", "gs"):
                            P.op("dve", lambda e, p_ap=p_ap, oc=oc, t=t: e.tensor_copy(out=h[:, oc, T(t)], in_=p_ap[:]),
                                 reads=[p_res, r_h[t]], writes=[r_h[t]])
                        return
                    if dbg is not None and dbg[0] in ("gi", "gn", "gs"):
                        return
                    ov = oc - 8
                    i = ov % 2
                    vp, vr = pend[ov]
                    if dbg is not None and dbg[0] in ("gv", "gg"):
                        src_, sr_ = (vp, vr) if dbg[0] == "gv" else (p_ap, p_res)
                        P.op("dve", lambda e, src_=src_, ov=ov, t=t: e.tensor_copy(out=h[:, ov, T(t)], in_=src_[:]),
                             reads=[sr_, vr, p_res, r_h[t]], writes=[r_h[t]])
                        return
                    P.op("act", lambda e, i=i, p_ap=p_ap: e.activation(out=sg[i], in_=p_ap[:], func=AF.Sigmoid),
                         reads=[p_res], writes=[r_sg[i]])
                    P.op("dve", lambda e, i=i, vp=vp: e.tensor_tensor(out=yv[i], in0=vp[:], in1=sg[i], op=ALU.mult),
                         reads=[vr, r_sg[i]], writes=[r_yv[i]])
                    P.op("dve", lambda e, i=i, ov=ov, t=t: e.scalar_tensor_tensor(
                        out=h[:, ov, T(t)], in0=yv[i], scalar=mod_gate(n, ov, b), in1=h[:, ov, T(t)],
                        op0=ALU.mult, op1=ALU.add), reads=[r_yv[i], r_mod, r_h[t]], writes=[r_h[t]])
                for sub in (0, 256):
                    gemm_fm(u, r_u, 8, wv, [[c0 + sub, c0 + sub + 128]], epi, tiles=[t])
                    gemm_fm(u, r_u, 8, wv, [[1024 + c0 + sub, 1024 + c0 + sub + 128]], epi, tiles=[t])

    def kv_phase(b):
        norm(8, b, u, r_u)
        wv = wview(wkv_b)
        stg = [carve(i * 1024, [128, 512], BF16) for i in range(4)]
        r_stg = [Res() for _ in range(4)]
        d_stg = [P.dsem() for _ in range(4)]
        cnt = [0]

        def epi(t, oc, p_ap, p_res):
            i = cnt[0] % 4
            cnt[0] += 1
            P.op("act", lambda e, i=i, p_ap=p_ap: e.activation(out=stg[i], in_=p_ap[:], func=AF.Copy),
                 reads=[p_res], writes=[r_stg[i]])
            me = P.op("sp", lambda e, i=i, oc=oc, t=t: e.dma_start(out=kt_s[b, oc, :, T(t)], in_=stg[i]),
                      reads=[r_stg[i]], dma=d_stg[i])
            r_stg[i].r.append(me)
            r_kv.w = me if r_kv.w is None or True else r_kv.w
            kv_w.append(me)
        gemm_fm(u, r_u, 8, wv, [[c0 + 128 * j for j in range(4)] for c0 in range(0, 3072, 512)], epi)
        for c0 in range(3072, 6144, 512):
            si = load_slab(wv[:, 0:8, c0:c0 + 512])
            for tb in range(16):
                bk = rot["a"] % 6
                rot["a"] += 1
                for kc in range(8):
                    P.op("pe", lambda e, si=si, kc=kc, tb=tb, bk=bk: e.matmul(
                        ps[bk][:], lhsT=u[:, kc, tb * 128:(tb + 1) * 128], rhs=slabs[si][:, kc, :],
                        start=(kc == 0), stop=(kc == 7)), reads=[r_slab[si], r_u[tb // 4]], writes=[psr[bk]])
                i = cnt[0] % 4
                cnt[0] += 1
                P.op("act", lambda e, i=i, bk=bk: e.activation(out=stg[i], in_=ps[bk][:], func=AF.Copy),
                     reads=[psr[bk]], writes=[r_stg[i]])
                me = P.op("sp", lambda e, i=i, tb=tb, c0=c0: e.dma_start(
                    out=v_s[b, tb * 128:(tb + 1) * 128, c0 - 3072:c0 - 3072 + 512], in_=stg[i]),
                    reads=[r_stg[i]], dma=d_stg[i])
                r_stg[i].r.append(me)
                kv_w.append(me)

    r_kv = Res()
    kv_w = []
    DIL = (1, 4, 16)

    def sl_(start, n, step):
        return slice(start, start + (n - 1) * step + 1, step)

    def attention(l, b):
        j_ = l - 2
        n = l * 2
        qT = carve(0, [128, 3, S], BF16)
        kT = carve(12288, [128, 3, S], BF16)
        Vt = carve(24576, [128, 3, 16, 128], BF16)
        pT = [carve(36864, [128, 512], BF16), carve(37888, [128, 512], BF16)]
        rec = carve(38912, [128, 1024], F32)
        oT = big[:, :].rearrange("p (a b) -> p a b", b=S)
        r_q = Res(); r_k = Res(); r_v = Res(); r_p = [Res(), Res()]; r_rec = Res(); r_qs = Res(); r_o = Res()
        d_k = next_misc(); d_v = next_misc(); d_q = next_misc()
        wqv = wview(wq_b[j_])
        pcount = [0]
        for hp in range(8):
            for br in range(3):
                P.op("sp", lambda e, br=br, hp=hp: e.dma_start(out=kT[:, br, :], in_=kt_s[b, br * 8 + hp]),
                     reads=[r_kv], writes=[r_k], dma=d_k)
                d = DIL[br]
                vsrc = v_s[b].rearrange("(n p r) f -> p r n f", p=128, r=d)[:, :, :, br * 1024 + hp * 128: br * 1024 + hp * 128 + 128]
                for r in range(d):
                    nb_ = 16 // d
                    P.op("sp", lambda e, br=br, r=r, nb_=nb_, vsrc=vsrc: e.dma_start(
                        out=Vt[:, br, r * nb_:(r + 1) * nb_, :], in_=vsrc[:, r, :, :]),
                        reads=[r_kv], writes=[r_v], dma=d_v)
            for br in range(3):
                c0 = br * 1024 + hp * 128
                si = load_slab(wqv[:, :, c0:c0 + 128])
                for t in range(4):
                    bk = rot["a"] % 4
                    rot["a"] += 1
                    for kc in range(8):
                        P.op("pe", lambda e, kc=kc, t=t, bk=bk, si=si: e.matmul(ps[bk][:], lhsT=slabs[si][:, kc, 0:128], rhs=u[:, kc, T(t)],
                                                                         start=(kc == 0), stop=(kc == 7)),
                             reads=[r_slab[si], r_u[t]], writes=[psr[bk]])
                    P.op("act", lambda e, br=br, t=t, bk=bk: e.activation(out=qT[:, br, T(t)], in_=ps[bk][:], func=AF.Copy,
                                                                          scale=0.125), reads=[psr[bk]], writes=[r_q])
            for qh in range(2):
                NB = (4, 5)
                DB = (6, 7)
                first = {}
                for hh in range(2):
                    hs = slice(64 * hh, 64 * hh + 64)
                    tl = []
                    for nn in range(8):
                        nblk = qh * 8 + nn
                        for kb, msk in ((nblk - 1, mprev), (nblk, mcur)):
                            if kb < 0:
                                continue
                            tl.append((0, slice(kb * 128, kb * 128 + 128), slice(nblk * 128, nblk * 128 + 128),
                                       msk[:, :], kb, nn // 4, slice((nn % 4) * 128, (nn % 4) * 128 + 128), 128))
                    for r in range(4):
                        for nn in range(2):
                            nblk = qh * 2 + nn
                            for kb, msk in ((nblk - 1, mprev), (nblk, mcur)):
                                if kb < 0:
                                    continue
                                tl.append((1, sl_(r + 512 * kb, 128, 4),
                                           sl_(r + 512 * nblk, 128, 4), msk[:, :], r * 4 + kb,
                                           nn, sl_(r, 128, 4), 128))
                    for r in range(16):
                        for mm in range(2):
                            m_ = qh * 2 + mm
                            tl.append((2, sl_(r, 128, 16), sl_(r + 512 * m_, 32, 16),
                                       mcur[:, 32 * m_:32 * m_ + 32], r, mm, sl_(r, 32, 16), 32))
                    i0 = 0
                    while i0 < len(tl):
                        grp = []
                        w_ = 0
                        while i0 < len(tl) and w_ + tl[i0][7] <= 512:
                            grp.append((tl[i0], w_))
                            w_ += tl[i0][7]
                            i0 += 1
                        bk = pcount[0] % 2 * 1 + 0
                        bk = pcount[0] % 4
                        pi = pcount[0] % 2
                        pcount[0] += 1
                        for (br, kap, qap, mk, vb, ob, oc_, nc_), off in grp:
                            P.op("pe", lambda e, br=br, kap=kap, qap=qap, off=off, nc_=nc_, bk=bk, hs=hs: e.matmul(
                                ps[bk][:, off:off + nc_], lhsT=kT[hs, br, kap], rhs=qT[hs, br, qap], start=True, stop=False),
                                reads=[r_k, r_q], writes=[psr[bk]])
                            P.op("pe", lambda e, mk=mk, off=off, nc_=nc_, bk=bk: e.matmul(
                                ps[bk][:, off:off + nc_], lhsT=identb[:], rhs=mk, start=False, stop=True),
                                writes=[psr[bk]])
                        P.op("act", lambda e, bk=bk, pi=pi, w_=w_: e.activation(out=pT[pi][:, 0:w_], in_=ps[bk][:, 0:w_], func=AF.Exp),
                             reads=[psr[bk]], writes=[r_p[pi]])
                        for (br, kap, qap, mk, vb, ob, oc_, nc_), off in grp:
                            for (bank, lhs) in ((NB[ob], Vt[:, br, vb, hs]), (DB[ob], onesb[:, 0:64])):
                                key = (bank, hh)
                                st_ = key not in first
                                first[key] = 1
                                P.op("pe", lambda e, bank=bank, lhs=lhs, oc_=oc_, pi=pi, off=off, nc_=nc_, st_=st_, hs=hs: e.matmul(
                                    ps[bank][hs, oc_], lhsT=lhs, rhs=pT[pi][:, off:off + nc_], start=st_, stop=False,
                                    skip_group_check=True),
                                    reads=[r_p[pi], r_v], writes=[psr[bank]])
                for ob in range(2):
                    P.op("dve", lambda e, ob=ob: e.reciprocal(out=rec[:, ob * 512:(ob + 1) * 512], in_=ps[DB[ob]][:]),
                         reads=[psr[DB[ob]]], writes=[r_rec])
                    P.op("dve", lambda e, ob=ob, hp=hp, qh=qh: e.tensor_tensor(
                        out=oT[:, hp, qh * 1024 + ob * 512: qh * 1024 + (ob + 1) * 512], in0=ps[NB[ob]][:],
                        in1=rec[:, ob * 512:(ob + 1) * 512], op=ALU.mult),
                        reads=[psr[NB[ob]], r_rec], writes=[r_o])
        rot["a"] = 0
        gemm_fm(oT, r_o, 8, wview(wo_b[j_]), [[c0 + 128 * j for j in range(4)] for c0 in (0, 512)], resid_epi(n, b))

    step = [0]

    def done():
        step[0] += 1
        return step[0] >= stop_after

    def u_to_h():
        P.barrier()
        for t in range(4):
            for kc in range(8):
                P.op("act", lambda e, t=t, kc=kc: e.activation(out=h[:, kc, T(t)], in_=u[:, kc, T(t)], func=AF.Copy),
                     reads=[r_u[t]], writes=[r_h[t]])

    for b in range(nseq):
        P.barrier()
        load_x(b)
        stopped = False
        for l in range(4):
            if l == 2:
                P.barrier()
                kv_w.clear()
                kv_phase(b)
                r_kv.w = None
                P.barrier()
            P.barrier()
            if dbg == ("w", l):
                wtmp = big[:, :].rearrange("p (a b) -> p a b", b=2048)
                rw_ = Res()
                P.op("sp", lambda e: e.dma_start(out=wtmp, in_=wview(wglu_b[0])), reads=[r_wcast, r_wcast2], writes=[rw_], dma=d_tok[1])
                P.op("sp", lambda e: e.dma_start(out=dbg_w.rearrange("(kc p) n -> p kc n", p=128), in_=wtmp), reads=[rw_], writes=[rw_], dma=d_tok[1])
                stopped = True
                break
            if dbg != ("m", l):
                norm(l * 2, b, u, r_u)
            if dbg == ("u", l):
                u_to_h()
                stopped = True
                break
            if dbg == ("m", l):
                pass
            elif dbg == ("gs", l):
                L = s5_layer_setup(l)
                s5_mixer(l, b, L)
                P.barrier()
                norm(l * 2, b, u, r_u)
                P.barrier()
                glu(l, b)
                stopped = True
                break
            elif dbg == ("gn", l):
                glu(l, b)
                stopped = True
                break
            elif l < 2:
                L = s5_layer_setup(l)
                if dbg == ("p", l):
                    P.barrier()
                    stopped = True
                    break
                s5_mixer(l, b, L)
                P.barrier()
                if dbg is not None and len(dbg) > 2 and dbg[2] == "fix":
                    norm(l * 2, b, big[:, :].rearrange("p (a b) -> p a b", b=S), [Res() for _ in range(4)])
                    P.barrier()
                if dbg is not None and dbg[:2] == ("pu", l):
                    for t_ in range(4):
                        for kc_ in range(8):
                            bk_ = (t_ * 8 + kc_) % 4
                            P.op("pe", lambda e, t_=t_, kc_=kc_, bk_=bk_: e.matmul(ps[bk_][:], lhsT=identb[:], rhs=u[:, kc_, T(t_)],
                                                                                   start=True, stop=True), reads=[r_u[t_]], writes=[psr[bk_]])
                            P.op("dve", lambda e, t_=t_, kc_=kc_, bk_=bk_: e.tensor_copy(out=h[:, kc_, T(t_)], in_=ps[bk_][:]),
                                 reads=[psr[bk_], r_h[t_]], writes=[r_h[t_]])
                    stopped = True
                    break
                if dbg is not None and dbg[:2] in (("z", l), ("y", l)):
                    u_to_h()
                    stopped = True
                    break
                glu(l, b)
            else:
                attention(l, b)
            if done():
                stopped = True
                break
            P.barrier()
            norm(l * 2 + 1, b, u, r_u)
            mlp(l, b)
            if done():
                stopped = True
                break
        step[0] = 0
        P.barrier()
        store_out(b, final=not stopped)
    P.barrier()
    P.op("sp", lambda e: e.dma_start(out=tok[0][0:1, 0:8], in_=x_d[0, 0:1, 0:8]), dma=d_tok[0])
    P.emit([(d_tok[0], P.cnt[d_tok[0]])] + [(k, P.cnt[k]) for k in d_otok])
    st.close()
    return nc


def _bf16(a):
    import ml_dtypes
    return np.asarray(a, dtype=np.float32).astype(ml_dtypes.bfloat16)


def make_in_maps(inp):
    f = lambda a: np.ascontiguousarray(np.asarray(a, dtype=np.float32))
    x = f(inp["x"]); c = f(inp["c"])
    adaw = np.concatenate([f(inp["ada_w"]).reshape(8, D, 3072)[i] for i in range(8)] + [f(inp["kv_ada_w"])], axis=1)
    adab_flat = np.concatenate([f(inp["ada_b"]).reshape(-1), f(inp["kv_ada_b"])])
    adab = np.ascontiguousarray(adab_flat.reshape(NCH, 128).T)
    lng_all = np.concatenate([f(inp["ln_g"]).reshape(8, D), f(inp["kv_g"]).reshape(1, D)], axis=0)
    lng = np.ascontiguousarray(lng_all.reshape(9, 8, 128).transpose(2, 0, 1))
    fing = np.ascontiguousarray(np.broadcast_to(f(inp["final_g"])[None, :], (128, D)))
    dsk = np.ascontiguousarray(f(inp["ssm_d"]).reshape(2, 8, 128).transpose(2, 0, 1))
    kk = np.arange(128)[:, None]; qq = np.arange(128)[None, :]
    mcur = np.where(kk <= qq, 0.0, -30000.0).astype(np.float32)
    mprev = np.where(kk >= qq, 0.0, -30000.0).astype(np.float32)
    common = dict(
        adaw=np.ascontiguousarray(adaw), adab=adab, lng=lng, fing=fing, dsk=dsk,
        lamre=f(inp["ssm_lam_re"]), lamim=f(inp["ssm_lam_im"]), logdt=f(inp["ssm_log_dt"]).reshape(2, 64, 1),
        bre=f(inp["ssm_b_re"]).reshape(2, 64, 1024), bim=f(inp["ssm_b_im"]).reshape(2, 64, 1024),
        cre=f(inp["ssm_c_re"]).reshape(2, 1024, 64), cim=f(inp["ssm_c_im"]).reshape(2, 1024, 64),
        identf=np.eye(128, dtype=np.float32), identb=_bf16(np.eye(128)), mcur=_bf16(mcur), mprev=_bf16(mprev),
        w1=f(inp["mlp_w1"]), w2=f(inp["mlp_w2"]), wglu=f(inp["ssm_w_glu"]), wkv=f(inp["w_kv"]),
        wq=f(inp["attn_w_q"]), wo=f(inp["attn_w_o"]),
    )
    maps = []
    for i in range(8):
        m = dict(common)
        m["x"] = np.ascontiguousarray(x[2 * i:2 * i + 2])
        m["cT"] = np.ascontiguousarray(c[2 * i:2 * i + 2].T.reshape(8, 128, 2).transpose(1, 0, 2))
        maps.append(m)
    return maps


def kernel(**inputs):
    nc = build_program()
    maps = make_in_maps(inputs)
    res = run_bass_kernel_spmd(nc, maps, core_ids=list(range(8)))
    return np.concatenate([r["out"] for r in res.results], axis=0).astype(np.float32)
```

```python
import contextlib
import numpy as np
import concourse.bass as bass
import concourse.mybir as mybir
from concourse.bass_utils import run_bass_kernel_spmd

F32 = mybir.dt.float32
BF16 = mybir.dt.bfloat16
I32 = mybir.dt.int32
AF = mybir.ActivationFunctionType
ALU = mybir.AluOpType

ENGS = ("pe", "act", "dve", "pool", "sp")
S = 2048
D = 1024
NMOD = 26624
NCH = NMOD // 128
EPS = 1e-6


class Res:
    __slots__ = ("w", "r")

    def __init__(self):
        self.w = None
        self.r = []


class Prog:
    def __init__(self, nc):
        self.nc = nc
        self.ops = {e: [] for e in ENGS}
        self.cnt = {}
        self.seen = {e: {} for e in ENGS}
        self.sems = {}
        self.nd = 0
        self.bar = None
        self.bar_done = {e: None for e in ENGS}

    def dsem(self):
        k = "d%d" % self.nd
        self.nd += 1
        self.cnt[k] = 0
        return k

    def barrier(self):
        self.bar = dict(self.cnt)

    def op(self, eng, fn, reads=(), writes=(), dma=None, after=()):
        waits = {}
        for k_, v_ in after:
            waits[k_] = v_

        def need(dep, war=False):
            if dep is None:
                return
            k, v = dep
            if k == eng and (eng == "pe" or (war and eng != "pool")):
                return
            if waits.get(k, 0) < v:
                waits[k] = v

        if self.bar is not None and self.bar_done[eng] is not self.bar:
            self.bar_done[eng] = self.bar
            for k, v in self.bar.items():
                if v > 0 and k != eng:
                    waits[k] = v
        for r in reads:
            need(r.w)
        for w in writes:
            need(w.w)
            for d in w.r:
                need(d, war=True)
        final = []
        seen = self.seen[eng]
        for k, v in waits.items():
            if seen.get(k, 0) >= v:
                continue
            seen[k] = v
            final.append((k, v))
        key, inc = (eng, 1) if dma is None else (dma, 16)
        self.cnt[key] = self.cnt.get(key, 0) + inc
        me = (key, self.cnt[key])
        self.ops[eng].append((final, fn, key, inc))
        for r in reads:
            r.r.append(me)
            if len(r.r) > 64:
                r.r = _compress(r.r)
        for w in writes:
            w.w = me
            w.r = []
        return me

    def emit(self, final_waits):
        nc = self.nc
        with contextlib.ExitStack() as st:
            for k in list(self.cnt.keys()):
                self.sems[k] = st.enter_context(nc.semaphore("s_" + k))
            block = st.enter_context(nc.Block())
            sems = self.sems

            def run(engname, e):
                for waits, fn, key, inc in self.ops[engname]:
                    for k, v in waits:
                        e.wait_ge(sems[k], v)
                    fn(e).then_inc(sems[key], inc)
                if engname == "sp":
                    for k, v in final_waits:
                        e.wait_ge(sems[k], v)

            @block.tensor
            def _(e):
                run("pe", e)

            @block.scalar
            def _(e):
                run("act", e)

            @block.vector
            def _(e):
                run("dve", e)

            @block.gpsimd
            def _(e):
                run("pool", e)

            @block.sync
            def _(e):
                run("sp", e)


def _compress(lst):
    best = {}
    for k, v in lst:
        if best.get(k, 0) < v:
            best[k] = v
    return list(best.items())


def build_program(stop_after=99, dbg=None, nseq=2):
    nc = bass.Bass("TRN2", target_bir_lowering=False)
    P = Prog(nc)

    def din(name, shape, dt=F32):
        return nc.dram_tensor(name, list(shape), dt, kind="ExternalInput").ap()

    def dscr(name, shape, dt=BF16):
        return nc.dram_tensor(name, list(shape), dt, kind="Internal").ap()

    x_d = din("x", [2, S, D])
    cT_d = din("cT", [128, 8, 2])
    adaw_d = din("adaw", [D, NMOD])
    adab_d = din("adab", [128, NCH])
    lng_d = din("lng", [128, 9, 8])
    fing_d = din("fing", [128, D])
    dsk_d = din("dsk", [128, 2, 8])
    lamre_d = din("lamre", [2, 64, 64])
    lamim_d = din("lamim", [2, 64, 64])
    logdt_d = din("logdt", [2, 64, 1])
    bre_d = din("bre", [2, 64, 1024])
    bim_d = din("bim", [2, 64, 1024])
    cre_d = din("cre", [2, 1024, 64])
    cim_d = din("cim", [2, 1024, 64])
    identf_d = din("identf", [128, 128])
    identb_d = din("identb", [128, 128], BF16)
    mcur_d = din("mcur", [128, 128], BF16)
    mprev_d = din("mprev", [128, 128], BF16)
    w1_d = din("w1", [4, D, 4096])
    w2_d = din("w2", [4, 4096, D])
    wglu_d = din("wglu", [2, D, 2048])
    wkv_d = din("wkv", [D, 6144])
    wq_d = din("wq", [2, D, 3072])
    wo_d = din("wo", [2, D, D])
    out_d = nc.dram_tensor("out", [2, S, D], F32, kind="ExternalOutput").ap()
    if dbg is not None and dbg[0] == "w":
        dbg_w = nc.dram_tensor("dbg_w", [D, 2048], BF16, kind="ExternalOutput").ap()
    if dbg is not None and dbg[0] == "p":
        dbg_l = nc.dram_tensor("dbg_l", [64, 256], F32, kind="ExternalOutput").ap()
        dbg_bb = nc.dram_tensor("dbg_bb", [64, 16, 2, 64], BF16, kind="ExternalOutput").ap()

    w1_b = dscr("w1b", [4, D, 4096])
    w2_b = dscr("w2b", [4, 4096, D])
    wglu_b = dscr("wglub", [2, D, 2048])
    wkv_b = dscr("wkvb", [D, 6144])
    wq_b = dscr("wqb", [2, D, 3072])
    wo_b = dscr("wob", [2, D, D])
    kt_s = dscr("kts", [2, 24, 128, S])
    v_s = dscr("vs", [2, S, 3072])
    bb_s = dscr("bbs", [2, 64, 16, 128])
    bb_s1 = dscr("bbs1", [2, 64, 16, 128])

    st = contextlib.ExitStack()

    def sb(name, shape, dt):
        return st.enter_context(nc.sbuf_tensor(name, list(shape), dt))

    h = sb("h", [128, 8, S], F32)
    u = sb("u", [128, 8, S], BF16)
    big = sb("big", [128, 16384], BF16)
    slabs = [sb("slab%d" % i, [128, 8, 512], BF16) for i in range(3)]
    sq = [sb("sq%d" % i, [128, 512], BF16) for i in range(2)]
    rs = sb("rs", [128, 512], F32)
    rstd = rs
    tmpf = [sb("tmpf%d" % i, [128, 512], F32) for i in range(2)]
    identf = sb("identf_s", [128, 128], F32)
    identb = sb("identb_s", [128, 128], BF16)
    onesb = sb("onesb", [128, 128], BF16)
    mcur = sb("mcur_s", [128, 128], BF16)
    mprev = sb("mprev_s", [128, 128], BF16)
    mod = sb("mod", [128, NCH, 2], F32)
    adab = sb("adab_s", [128, NCH], F32)
    lng = sb("lng_s", [128, 9, 8], F32)
    Gm = sb("Gm", [128, 9, 8, 2], F32)
    dsk = sb("dsk_s", [128, 2, 8], F32)
    cT = sb("cT_s", [128, 8, 2], F32)
    scT = sb("scT", [128, 8, 2], F32)
    arena = sb("arena", [128, 10752], F32)

    def carve(off, shape, dt, parts=128):
        nel = int(np.prod(shape[1:]))
        nb = nel * (2 if dt == BF16 else 4)
        assert off % 4 == 0 and off + nb <= 10752 * 4, (off, nb)
        ap = arena[0:parts, off // 4:(off + nb + 3) // 4]
        if dt == BF16:
            ap = ap.bitcast(BF16)
        elif dt == I32:
            ap = ap.bitcast(I32)
        if len(shape) == 3:
            ap = ap.rearrange("p (a b) -> p a b", b=shape[2])
        elif len(shape) == 4:
            ap = ap.rearrange("p (a b c) -> p a b c", b=shape[2], c=shape[3])
        return ap

    ps = [st.enter_context(nc.psum_tensor("ps%d" % i, [128, 512], F32)) for i in range(8)]
    psr = [Res() for _ in range(8)]

    r_h = [Res() for _ in range(4)]
    r_u = [Res() for _ in range(4)]
    r_big = Res()
    r_slab = [Res() for _ in range(3)]
    d_slab = [P.dsem() for _ in range(3)]
    r_sq = [Res(), Res()]
    r_rs = Res()
    r_rstd = Res()
    r_tmpf = [Res(), Res()]
    r_const = Res()
    r_mod = Res()
    d_const = P.dsem()
    d_constp = P.dsem()
    d_misc = [P.dsem() for _ in range(8)]
    misc_i = [0]
    slab_i = [0]
    rot = {"a": 0}

    def next_misc():
        k = d_misc[misc_i[0] % len(d_misc)]
        misc_i[0] += 1
        return k

    for (dst, src) in [(identf, identf_d), (adab, adab_d), (lng, lng_d), (dsk, dsk_d), (cT, cT_d)]:
        P.op("sp", lambda e, dst=dst, src=src: e.dma_start(out=dst[:], in_=src), writes=[r_const], dma=d_const)
    for (dst, src) in [(identb, identb_d), (mcur, mcur_d), (mprev, mprev_d)]:
        P.op("pool", lambda e, dst=dst, src=src: e.dma_start(out=dst[:], in_=src), writes=[r_const], dma=d_constp)
    r_const.w = (d_const, P.cnt[d_const])
    P.op("dve", lambda e: e.memset(onesb[:], 1.0), writes=[r_const])
    r_const.w = None
    d_castAB = [P.dsem(), P.dsem()]
    cast_n = [0]
    r_wcast = Res()
    r_wcast2 = Res()

    def cast2d(dst, src, rows, cols):
        for r0 in range(0, rows, 128):
            for c0 in range(0, cols, 1024):
                c1 = min(cols, c0 + 1024)
                g_ = cast_n[0] // 24
                dk = d_castAB[g_ % 2]
                aft = [(dk, P.cnt[dk])] if (cast_n[0] % 24 == 0 and g_ >= 2) else []
                cast_n[0] += 1
                P.op("pool", lambda e, r0=r0, c0=c0, c1=c1: e.dma_start(
                    out=dst[r0:r0 + 128, c0:c1], in_=src[r0:r0 + 128, c0:c1], max_dma_last_dim=4096),
                    dma=dk, after=aft)

    for l in range(2):
        cast2d(wglu_b[l], wglu_d[l], D, 2048)
    for l in range(4):
        cast2d(w1_b[l], w1_d[l], D, 4096)
        cast2d(w2_b[l], w2_d[l], 4096, D)
    cast2d(wkv_b, wkv_d, D, 6144)
    for l in range(2):
        cast2d(wq_b[l], wq_d[l], D, 3072)
        cast2d(wo_b[l], wo_d[l], D, D)
    r_wcast.w = (d_castAB[0], P.cnt[d_castAB[0]])
    r_wcast2.w = (d_castAB[1], P.cnt[d_castAB[1]])

    P.barrier()
    P.op("act", lambda e: e.activation(out=scT[:], in_=cT[:], func=AF.Silu), writes=[r_const])
    aslab = [carve(0, [128, 8, 512], F32), carve(16384, [128, 8, 512], F32)]
    r_aslab = [Res(), Res()]
    d_aslab = [P.dsem(), P.dsem()]
    adaw_v = adaw_d.rearrange("(kc p) n -> p kc n", p=128)
    nsl = NMOD // 512
    for s_ in range(nsl):
        bi = s_ % 2
        P.op("sp", lambda e, s_=s_, bi=bi: e.dma_start(out=aslab[bi], in_=adaw_v[:, :, s_ * 512:(s_ + 1) * 512]),
             writes=[r_aslab[bi]], dma=d_aslab[bi])
        for j in range(4):
            ch = s_ * 4 + j
            for kc in range(8):
                P.op("pe", lambda e, bi=bi, j=j, kc=kc, ch=ch: e.matmul(
                    ps[7][:, 2 * ch:2 * ch + 2], lhsT=aslab[bi][:, kc, j * 128:(j + 1) * 128], rhs=scT[:, kc, :],
                    start=(kc == 0), stop=(kc == 7)), reads=[r_aslab[bi], r_const], writes=[psr[7]])
    P.op("dve", lambda e: e.tensor_tensor(out=mod[:], in0=ps[7][:, 0:2 * NCH].rearrange("p (c b) -> p c b", b=2),
                                          in1=adab[:].unsqueeze(2).to_broadcast([128, NCH, 2]), op=ALU.add),
         reads=[psr[7]], writes=[r_mod])
    for n in range(9):
        base = (n * 3072 + 1024) // 128 if n < 8 else (24576 + 1024) // 128
        P.op("dve", lambda e, n=n, base=base: e.scalar_tensor_tensor(
            out=Gm[:, n, :, :], in0=mod[:, base:base + 8, :], scalar=1.0,
            in1=lng[:, n, :].unsqueeze(2).to_broadcast([128, 8, 2]), op0=ALU.add, op1=ALU.mult),
            reads=[r_mod], writes=[r_mod])

    def mod_shift(n, kc, b):
        base = (n * 3072) // 128 if n < 8 else 24576 // 128
        return mod[:, base + kc, b:b + 1]

    def mod_gate(n, kc, b):
        base = (n * 3072 + 2048) // 128
        return mod[:, base + kc, b:b + 1]

    def T(t):
        return slice(t * 512, (t + 1) * 512)

    def norm(n, b, out_buf, r_out):
        for t in range(4):
            for kc in range(8):
                i = kc % 2
                P.op("act", lambda e, kc=kc, i=i, t=t: e.activation(out=sq[i][:], in_=h[:, kc, T(t)], func=AF.Square),
                     reads=[r_h[t]], writes=[r_sq[i]])
                P.op("pe", lambda e, kc=kc, i=i: e.matmul(ps[6][:], lhsT=onesb[:], rhs=sq[i][:],
                                                          start=(kc == 0), stop=(kc == 7)),
                     reads=[r_sq[i]], writes=[psr[6]])
            P.op("act", lambda e: e.activation(out=rs[:], in_=ps[6][:], func=AF.Sqrt, bias=EPS, scale=1.0 / D),
                 reads=[psr[6]], writes=[r_rs, r_rstd])
            P.op("dve", lambda e: e.reciprocal(out=rstd[:], in_=rs[:]), reads=[r_rs], writes=[r_rs, r_rstd])
            for kc in range(8):
                i = kc % 2
                P.op("pool", lambda e, kc=kc, i=i, t=t: e.tensor_tensor(out=tmpf[i][:], in0=h[:, kc, T(t)], in1=rstd[:],
                                                                         op=ALU.mult),
                     reads=[r_h[t], r_rstd], writes=[r_tmpf[i]])
                P.op("act", lambda e, kc=kc, i=i, t=t: e.activation(
                    out=out_buf[:, kc, T(t)], in_=tmpf[i][:], func=AF.Identity,
                    bias=mod_shift(n, kc, b), scale=Gm[:, n, kc, b:b + 1]),
                    reads=[r_tmpf[i], r_mod], writes=[r_out[t]])

    def load_slab(view):
        i = slab_i[0] % 3
        slab_i[0] += 1
        P.op("sp", lambda e, i=i, view=view: e.dma_start(out=slabs[i][:, 0:view.shape[1], 0:view.shape[2]], in_=view),
             reads=[r_wcast, r_wcast2], writes=[r_slab[i]], dma=d_slab[i])
        return i

    def gemm_fm(src, r_src, KC, wv, colgroups, epi, tiles=range(4)):
        for t in tiles:
            for grp in colgroups:
                c0 = grp[0]
                banks = []
                for _ in grp:
                    banks.append(rot["a"] % 6)
                    rot["a"] += 1
                for kg in range(KC // 8):
                    si = load_slab(wv[:, kg * 8:(kg + 1) * 8, c0:grp[-1] + 128])
                    for j, c in enumerate(grp):
                        for kc in range(8):
                            kk = kg * 8 + kc
                            P.op("pe", lambda e, si=si, j=j, c=c, kc=kc, kk=kk, bk=banks[j], t=t, c0=c0: e.matmul(
                                ps[bk][:], lhsT=slabs[si][:, kc, c - c0:c - c0 + 128], rhs=src[:, kk, T(t)],
                                start=(kk == 0), stop=(kk == KC - 1)),
                                reads=[r_slab[si], r_src[t] if isinstance(r_src, list) else r_src], writes=[psr[banks[j]]])
                for j, c in enumerate(grp):
                    epi(t, c // 128, ps[banks[j]], psr[banks[j]])

    def wview(w2d):
        return w2d.rearrange("(kc p) n -> p kc n", p=128)

    def resid_epi(n, b):
        def epi(t, oc, p_ap, p_res):
            P.op("dve", lambda e, t=t, oc=oc, p_ap=p_ap: e.scalar_tensor_tensor(
                out=h[:, oc, T(t)], in0=p_ap[:], scalar=mod_gate(n, oc, b), in1=h[:, oc, T(t)],
                op0=ALU.mult, op1=ALU.add), reads=[p_res, r_mod, r_h[t]], writes=[r_h[t]])
        return epi

    hid = big[:, :].rearrange("p (a b) -> p a b", b=512)
    r_hid = Res()
    relu_t = [carve(0, [128, 512], BF16), carve(1024, [128, 512], BF16)]
    r_relu = [Res(), Res()]

    def mlp(l, b):
        n = l * 2 + 1
        w1v = wview(w1_b[l])
        w2v = wview(w2_b[l])
        for t in range(4):
            def epi1(t_, hc, p_ap, p_res):
                i = hc % 2
                P.op("act", lambda e, i=i, p_ap=p_ap: e.activation(out=relu_t[i], in_=p_ap[:], func=AF.Relu),
                     reads=[p_res], writes=[r_relu[i]])
                P.op("pool", lambda e, i=i, hc=hc: e.tensor_tensor(out=hid[:, hc, :], in0=relu_t[i], in1=relu_t[i],
                                                                    op=ALU.mult),
                     reads=[r_relu[i]], writes=[r_hid])
            gemm_fm(u, r_u, 8, w1v, [[c0 + 128 * j for j in range(4)] for c0 in range(0, 4096, 512)], epi1, tiles=[t])

            class HidSrc:
                def __getitem__(self, idx):
                    return hid[idx[0], idx[1], :]
            gemm_fm(HidSrc(), r_hid, 32, w2v, [[c0 + 128 * j for j in range(4)] for c0 in (0, 512)],
                    lambda t_, oc, p_ap, p_res, t=t: resid_epi(n, b)(t, oc, p_ap, p_res), tiles=[t])

    tok = [carve(0, [128, D], F32), carve(4096, [128, D], F32)]
    r_tok = [Res(), Res()]
    d_tok = [P.dsem(), P.dsem()]
    otok = [carve(8192, [128, D], F32), carve(12288, [128, D], F32)]
    r_otok = [Res(), Res()]
    d_otok = [P.dsem(), P.dsem()]
    ss1 = carve(16384, [128, 1], F32)
    ss2 = carve(16400, [128, 1], F32)
    ss3 = carve(16416, [128, 1], F32)
    junk = carve(16448, [128, D], F32)
    fing = carve(20544, [128, D], F32)
    r_ss = Res()
    last_out = []

    def load_x(b):
        for tb in range(16):
            i = tb % 2
            P.op("sp", lambda e, tb=tb, i=i: e.dma_start(out=tok[i], in_=x_d[b, tb * 128:(tb + 1) * 128, :]),
                 writes=[r_tok[i]], dma=d_tok[i])
            for half in range(2):
                bk = rot["a"] % 6
                rot["a"] += 1
                for j in range(4):
                    kc = half * 4 + j
                    P.op("pe", lambda e, i=i, kc=kc, j=j, bk=bk: e.transpose(
                        out=ps[bk][:, j * 128:(j + 1) * 128], in_=tok[i][:, kc * 128:(kc + 1) * 128], identity=identf[:]),
                        reads=[r_tok[i]], writes=[psr[bk]])
                P.op("act", lambda e, half=half, bk=bk, tb=tb: e.activation(
                    out=h[:, half * 4:half * 4 + 4, tb * 128:(tb + 1) * 128],
                    in_=ps[bk][:].rearrange("p (a b) -> p a b", b=128), func=AF.Copy),
                    reads=[psr[bk]], writes=[r_h[tb // 4]])

    def store_out(b, final=True):
        P.op("sp", lambda e: e.dma_start(out=fing, in_=fing_d), writes=[r_ss], dma=d_tok[0])
        for tb in range(16):
            i = tb % 2
            for half in range(2):
                for j in range(4):
                    kc = half * 4 + j
                    P.op("pe", lambda e, kc=kc, j=j, half=half, tb=tb: e.transpose(
                        out=ps[half][:, j * 128:(j + 1) * 128], in_=h[:, kc, tb * 128:(tb + 1) * 128], identity=identf[:]),
                        reads=[r_h[tb // 4]], writes=[psr[half]])
            if final:
                for half in range(2):
                    P.op("act", lambda e, half=half: e.activation(
                        out=junk[:, half * 512:(half + 1) * 512], in_=ps[half][:], func=AF.Square,
                        accum_out=(ss1 if half == 0 else ss2)), reads=[psr[half]], writes=[r_ss])
                P.op("dve", lambda e: e.tensor_tensor(out=ss3, in0=ss1, in1=ss2, op=ALU.add), reads=[r_ss], writes=[r_ss])
                P.op("act", lambda e: e.activation(out=ss1, in_=ss3, func=AF.Sqrt, bias=EPS, scale=1.0 / D),
                     reads=[r_ss], writes=[r_ss])
                P.op("dve", lambda e: e.reciprocal(out=ss2, in_=ss1), reads=[r_ss], writes=[r_ss])
                for half in range(2):
                    P.op("dve", lambda e, half=half, i=i: e.scalar_tensor_tensor(
                        out=otok[i][:, half * 512:(half + 1) * 512], in0=ps[half][:], scalar=ss2,
                        in1=fing[:, half * 512:(half + 1) * 512], op0=ALU.mult, op1=ALU.mult),
                        reads=[psr[half], r_ss], writes=[r_otok[i]])
            else:
                for half in range(2):
                    P.op("act", lambda e, half=half, i=i: e.activation(
                        out=otok[i][:, half * 512:(half + 1) * 512], in_=ps[half][:], func=AF.Copy),
                        reads=[psr[half]], writes=[r_otok[i]])
            me = P.op("sp", lambda e, i=i, tb=tb: e.dma_start(out=out_d[b, tb * 128:(tb + 1) * 128, :], in_=otok[i]),
                      reads=[r_otok[i]], dma=d_otok[i])
            r_otok[i].r.append(me)
            last_out.append(me)

    def s5_prep(l):
        P.barrier()
        o = [0]

        def A(shape, dt=F32, parts=64):
            nel = int(np.prod(shape[1:]))
            nb = ((nel * (2 if dt == BF16 else 4) + 3) // 4) * 4
            ap = carve(o[0], shape, dt, parts)
            o[0] += nb
            return ap
        lre = A([64, 64]); lim = A([64, 64]); ldt = A([64, 1]); dt_ = A([64, 1])
        a_ = A([64, 64]); th = A([64, 64]); ea = A([64, 64]); yy = A([64, 64]); ki = A([64, 64], I32)
        kf = A([64, 64]); ff = A([64, 64]); sn = A([64, 64]); cs = A([64, 64])
        lbr = A([64, 64]); lbi = A([64, 64]); den = A([64, 64]); nr = A([64, 64]); ni = A([64, 64])
        qr = A([64, 64]); qi = A([64, 64]); t1 = A([64, 64]); t2 = A([64, 64])
        br = A([64, 64, 16]); bi_ = A([64, 64, 16])
        bbr = A([64, 64, 16]); bbi = A([64, 64, 16]); t3 = A([64, 64, 16])
        bout = A([64, 16, 2, 64], BF16)
        l2r = A([64, 64]); l2i = A([64, 64])
        R = Res()
        dd = next_misc()
        for dst, src in [(lre, lamre_d[l]), (lim, lamim_d[l]), (ldt, logdt_d[l])]:
            P.op("sp", lambda e, dst=dst, src=src: e.dma_start(out=dst, in_=src), writes=[R], dma=dd)
        P.op("sp", lambda e: e.dma_start(out=br, in_=bre_d[l].rearrange("g (p c) -> g p c", c=16)), writes=[R], dma=dd)
        P.op("sp", lambda e: e.dma_start(out=bi_, in_=bim_d[l].rearrange("g (p c) -> g p c", c=16)), writes=[R], dma=dd)

        def V(fn):
            P.op("dve", fn, reads=[R], writes=[R])

        def ACT(fn):
            P.op("act", fn, reads=[R], writes=[R])
        ACT(lambda e: e.activation(out=dt_, in_=ldt, func=AF.Exp))
        V(lambda e: e.tensor_scalar(out=a_, in0=lre, scalar1=dt_, scalar2=None, op0=ALU.mult))
        V(lambda e: e.tensor_scalar(out=th, in0=lim, scalar1=dt_, scalar2=None, op0=ALU.mult))
        ACT(lambda e: e.activation(out=ea, in_=a_, func=AF.Exp))

        def sin_of(dst, offs):
            V(lambda e: e.tensor_scalar(out=yy, in0=th, scalar1=1.0 / (2 * np.pi), scalar2=offs, op0=ALU.mult, op1=ALU.add))
            V(lambda e: e.tensor_copy(out=ki, in_=yy))
            V(lambda e: e.tensor_copy(out=kf, in_=ki))
            V(lambda e: e.tensor_tensor(out=ff, in0=yy, in1=kf, op=ALU.subtract))
            V(lambda e: e.scalar_tensor_tensor(out=ff, in0=ff, scalar=0.0, in1=ff, op0=ALU.is_lt, op1=ALU.add))
            V(lambda e: e.tensor_scalar(out=ff, in0=ff, scalar1=2 * np.pi, scalar2=-np.pi, op0=ALU.mult, op1=ALU.add))
            V(lambda e: e.tensor_scalar(out=ff, in0=ff, scalar1=-3.14159, scalar2=3.14159, op0=ALU.max, op1=ALU.min))
            ACT(lambda e: e.activation(out=dst, in_=ff, func=AF.Sin))
        sin_of(sn, 0.5)
        sin_of(cs, 0.75)
        V(lambda e: e.tensor_tensor(out=lbr, in0=ea, in1=cs, op=ALU.mult))
        V(lambda e: e.tensor_tensor(out=lbi, in0=ea, in1=sn, op=ALU.mult))
        V(lambda e: e.tensor_scalar(out=nr, in0=lbr, scalar1=-1.0, scalar2=None, op0=ALU.add))
        V(lambda e: e.tensor_tensor(out=t1, in0=lre, in1=lre, op=ALU.mult))
        V(lambda e: e.tensor_tensor(out=t2, in0=lim, in1=lim, op=ALU.mult))
        V(lambda e: e.tensor_tensor(out=den, in0=t1, in1=t2, op=ALU.add))
        V(lambda e: e.reciprocal(out=den, in_=den))
        V(lambda e: e.tensor_tensor(out=t1, in0=nr, in1=lre, op=ALU.mult))
        V(lambda e: e.tensor_tensor(out=t2, in0=lbi, in1=lim, op=ALU.mult))
        V(lambda e: e.tensor_tensor(out=qr, in0=t1, in1=t2, op=ALU.add))
        V(lambda e: e.tensor_tensor(out=qr, in0=qr, in1=den, op=ALU.mult))
        V(lambda e: e.tensor_tensor(out=t1, in0=lbi, in1=lre, op=ALU.mult))
        V(lambda e: e.tensor_tensor(out=t2, in0=nr, in1=lim, op=ALU.mult))
        V(lambda e: e.tensor_tensor(out=qi, in0=t1, in1=t2, op=ALU.subtract))
        V(lambda e: e.tensor_tensor(out=qi, in0=qi, in1=den, op=ALU.mult))
        qrb = qr.unsqueeze(2).to_broadcast([64, 64, 16])
        qib = qi.unsqueeze(2).to_broadcast([64, 64, 16])
        V(lambda e: e.tensor_tensor(out=bbr, in0=br, in1=qrb, op=ALU.mult))
        V(lambda e: e.tensor_tensor(out=t3, in0=bi_, in1=qib, op=ALU.mult))
        V(lambda e: e.tensor_tensor(out=bbr, in0=bbr, in1=t3, op=ALU.subtract))
        V(lambda e: e.tensor_tensor(out=bbi, in0=bi_, in1=qrb, op=ALU.mult))
        V(lambda e: e.tensor_tensor(out=t3, in0=br, in1=qib, op=ALU.mult))
        V(lambda e: e.tensor_tensor(out=bbi, in0=bbi, in1=t3, op=ALU.add))
        V(lambda e: e.tensor_copy(out=bout[:, :, 0, :], in_=bbr.rearrange("g p c -> g c p")))
        V(lambda e: e.tensor_copy(out=bout[:, :, 1, :], in_=bbi.rearrange("g p c -> g c p")))
        P.op("sp", lambda e: e.dma_start(out=bb_s[l].rearrange("g c (r p) -> g c r p", r=2), in_=bout), reads=[R], writes=[R], dma=dd)
        lrb = lbr.unsqueeze(2).to_broadcast([64, 64, 16])
        lib = lbi.unsqueeze(2).to_broadcast([64, 64, 16])
        V(lambda e: e.tensor_tensor(out=br, in0=bbr, in1=lrb, op=ALU.mult))
        V(lambda e: e.tensor_tensor(out=t3, in0=bbi, in1=lib, op=ALU.mult))
        V(lambda e: e.tensor_tensor(out=br, in0=br, in1=t3, op=ALU.subtract))
        V(lambda e: e.tensor_tensor(out=bi_, in0=bbi, in1=lrb, op=ALU.mult))
        V(lambda e: e.tensor_tensor(out=t3, in0=bbr, in1=lib, op=ALU.mult))
        V(lambda e: e.tensor_tensor(out=bi_, in0=bi_, in1=t3, op=ALU.add))
        V(lambda e: e.tensor_copy(out=bout[:, :, 0, :], in_=br.rearrange("g p c -> g c p")))
        V(lambda e: e.tensor_copy(out=bout[:, :, 1, :], in_=bi_.rearrange("g p c -> g c p")))
        P.op("sp", lambda e: e.dma_start(out=bb_s1[l].rearrange("g c (r p) -> g c r p", r=2), in_=bout), reads=[R], writes=[R], dma=dd)
        V(lambda e: e.tensor_tensor(out=t1, in0=lbr, in1=lbr, op=ALU.mult))
        V(lambda e: e.tensor_tensor(out=t2, in0=lbi, in1=lbi, op=ALU.mult))
        V(lambda e: e.tensor_tensor(out=l2r, in0=t1, in1=t2, op=ALU.subtract))
        V(lambda e: e.tensor_tensor(out=t1, in0=lbr, in1=lbi, op=ALU.mult))
        V(lambda e: e.tensor_scalar(out=l2i, in0=t1, scalar1=2.0, scalar2=None, op0=ALU.mult))
        if dbg is not None and dbg[0] == "p":
            for i_, src_ in enumerate((lbr, lbi, qr, qi)):
                P.op("sp", lambda e, i_=i_, src_=src_: e.dma_start(out=dbg_l[:, i_ * 64:(i_ + 1) * 64], in_=src_), reads=[R], writes=[R], dma=dd)
            P.op("sp", lambda e: e.dma_start(out=dbg_bb, in_=bout), reads=[R], writes=[R], dma=dd)
        return l2r, l2i, R

    def s5_layer_setup(l):
        lbr, lbi, R = s5_prep(l)
        cw = carve(32768, [128, 2, 1024], BF16)
        ca = carve(36864, [128, 2, 32], F32)
        cb = carve(37120, [128, 2, 32], F32)
        l2 = [carve(38912, [64, 128], F32, 64), carve(39424, [64, 128], F32, 64)]
        R2 = Res()
        dd = next_misc()
        for src, idx in ((lbr, 0), (lbi, 1)):
            for hf in range(2):
                P.op("dve", lambda e, src=src, idx=idx, hf=hf: e.tensor_copy(out=l2[idx][:, hf * 64:(hf + 1) * 64], in_=src),
                     reads=[R], writes=[R])
            P.op("pe", lambda e, idx=idx: e.transpose(out=ps[6][:, idx * 64:(idx + 1) * 64], in_=l2[idx],
                                                      identity=identf[0:64, 0:64]),
                 reads=[R], writes=[psr[6]])
        for hv in range(2):
            hp_ = slice(64 * hv, 64 * hv + 64)
            gcol = slice(32 * hv, 32 * hv + 32)
            gcol2 = slice(64 + 32 * hv, 64 + 32 * hv + 32)
            P.op("dve", lambda e, hp_=hp_, gcol=gcol: e.tensor_copy(out=ca[hp_, 0, :], in_=ps[6][hp_, gcol]), reads=[psr[6]], writes=[R2])
            P.op("dve", lambda e, hp_=hp_, gcol=gcol: e.tensor_copy(out=ca[hp_, 1, :], in_=ps[6][hp_, gcol]), reads=[psr[6]], writes=[R2])
            P.op("dve", lambda e, hp_=hp_, gcol2=gcol2: e.tensor_copy(out=cb[hp_, 1, :], in_=ps[6][hp_, gcol2]), reads=[psr[6]], writes=[R2])
            P.op("dve", lambda e, hp_=hp_, gcol2=gcol2: e.tensor_scalar(out=cb[hp_, 0, :], in0=ps[6][hp_, gcol2], scalar1=-1.0,
                                                                    scalar2=None, op0=ALU.mult), reads=[psr[6]], writes=[R2])
        P.barrier()
        bbpad = [carve(0, [128, 64, 128], BF16), carve(16384, [128, 64, 128], BF16)]
        cnat = carve(38912, [128, 8, 128], F32)
        for ti_, bsrc in enumerate((bb_s, bb_s1)):
            P.op("pool", lambda e, ti_=ti_: e.memset(bbpad[ti_], 0.0), writes=[R2])
            for g in range(64):
                gl = g % 8
                P.op("sp", lambda e, g=g, gl=gl, ti_=ti_, bsrc=bsrc: e.dma_start(out=bbpad[ti_][16 * gl:16 * gl + 16, g, :], in_=bsrc[l, g]),
                     reads=[R, R2], writes=[R2], dma=dd)
        for idx, cd, sgn in ((0, cre_d, 1.0), (1, cim_d, -1.0)):
            for hf in range(2):
                P.op("sp", lambda e, cd=cd, hf=hf: e.dma_start(out=cnat[:, :, hf * 64:(hf + 1) * 64],
                                                              in_=cd[l].rearrange("(a q) p -> q a p", q=128)),
                     reads=[R2], writes=[R2], dma=dd)
            for half in range(2):
                for j in range(4):
                    a = half * 4 + j
                    P.op("pe", lambda e, a=a, j=j: e.transpose(out=ps[5][:, j * 128:(j + 1) * 128], in_=cnat[:, a, :],
                                                                identity=identf[:]), reads=[R2], writes=[psr[5]])
                P.op("dve", lambda e, idx=idx, half=half, sgn=sgn: e.tensor_scalar(
                    out=cw[:, idx, half * 512:(half + 1) * 512], in0=ps[5][:, :], scalar1=sgn, scalar2=None,
                    op0=ALU.mult), reads=[psr[5]], writes=[R2])
        P.barrier()
        return dict(bbpad=bbpad, cw=cw, ca=ca, cb=cb, R=R2)

    TC = 32

    def s5_mixer(l, b, L):
        R2 = L["R"]
        bbpad, cw, ca, cb = L["bbpad"], L["cw"], L["ca"], L["cb"]
        bigf = big[:, :].bitcast(F32)
        Vbs = [bigf[:, i * 2048:(i + 1) * 2048].rearrange("p (r g t) -> p r g t", r=2, g=32) for i in range(2)]
        Xb = bigf[:, 4096:5120].bitcast(BF16).rearrange("p (r g t) -> p r g t", r=2, g=32)
        ytok = bigf[0:TC, 5120:6144]
        zp = bigf[:, 6144:6400].rearrange("p (k t) -> p k t", t=TC)
        g1 = bigf[:, 6400:6656].rearrange("p (k t) -> p k t", t=TC)
        g2 = bigf[:, 6656:6912].rearrange("p (k t) -> p k t", t=TC)
        Zst = carve(37376, [128, 2, 32, 2], F32)
        m1 = carve(37888, [128, 2, 32, 2], F32)
        m2 = carve(38400, [128, 2, 32, 2], F32)
        rVs = [Res(), Res()]
        rX = Res(); rY = Res(); rZ = Res(); rZs = Res(); rZ2 = Res()
        rm1 = Res(); rm2a = Res(); rm2b = Res()
        r_ul = Res()
        P.op("dve", lambda e: e.memset(Zst, 0.0), writes=[rZs])
        nck = S // TC
        per_bank = 512 // (2 * TC)

        def inproj(ck):
            Vb = Vbs[ck % 2]
            rV = rVs[ck % 2]
            t0 = ck * TC
            t4 = t0 // 512
            for gg0 in range(0, 32, per_bank):
                bk = rot["a"] % 5
                rot["a"] += 1
                for hv in range(2):
                    for j in range(per_bank):
                        g = 32 * hv + gg0 + j
                        for ri in range(2):
                            c_ = (j * 2 + ri) * TC
                            P.op("pe", lambda e, g=g, ri=ri, c_=c_, bk=bk, hv=hv, t0=t0: e.matmul(
                                ps[bk][64 * hv:64 * hv + 64, c_:c_ + TC], lhsT=bbpad[0][:, g, ri * 64:(ri + 1) * 64],
                                rhs=u[:, g // 8, t0:t0 + TC], start=True, stop=False),
                                reads=[R2, r_u[t4]], writes=[psr[bk]])
                            lo = 1 if ck == 0 else 0
                            P.op("pe", lambda e, g=g, ri=ri, c_=c_, bk=bk, hv=hv, t0=t0, lo=lo: e.matmul(
                                ps[bk][64 * hv:64 * hv + 64, c_ + lo:c_ + TC], lhsT=bbpad[1][:, g, ri * 64:(ri + 1) * 64],
                                rhs=u[:, g // 8, t0 - 1 + lo:t0 - 1 + TC], start=False, stop=True),
                                reads=[R2, r_u[t4], r_ul], writes=[psr[bk]])
                P.op("act", lambda e, bk=bk, gg0=gg0, Vb=Vb: e.activation(
                    out=Vb[:, :, gg0:gg0 + per_bank, :].rearrange("p r g t -> p g r t"),
                    in_=ps[bk][:, 0:per_bank * 2 * TC].rearrange("p (g r t) -> p g r t", r=2, t=TC), func=AF.Copy),
                    reads=[psr[bk]], writes=[rV])

        def scan(ck):
            Vb = Vbs[ck % 2]
            rV = rVs[ck % 2]
            for t in range(0, TC, 2):
                if t == 0:
                    prev = Zst[:, :, :, :]
                    pf = [Zst[:, 1, :, :], Zst[:, 0, :, :]]
                    rp = rZs
                else:
                    prev = Vb[:, :, :, t - 2:t]
                    pf = [Vb[:, 1, :, t - 2:t], Vb[:, 0, :, t - 2:t]]
                    rp = rV
                cur = Vb[:, :, :, t:t + 2]
                cab = ca.unsqueeze(3).to_broadcast([128, 2, 32, 2])
                cb0 = cb[:, 0, :].unsqueeze(2).to_broadcast([128, 32, 2])
                cb1 = cb[:, 1, :].unsqueeze(2).to_broadcast([128, 32, 2])
                P.op("dve", lambda e, prev=prev, cab=cab: e.tensor_tensor(out=m1, in0=prev, in1=cab, op=ALU.mult),
                     reads=[rp, R2], writes=[rm1])
                P.op("dve", lambda e, pf=pf, cb0=cb0: e.tensor_tensor(out=m2[:, 0, :, :], in0=pf[0], in1=cb0, op=ALU.mult),
                     reads=[rp, R2], writes=[rm2a])
                P.op("dve", lambda e, pf=pf, cb1=cb1: e.tensor_tensor(out=m2[:, 1, :, :], in0=pf[1], in1=cb1, op=ALU.mult),
                     reads=[rp, R2], writes=[rm2b])
                P.op("dve", lambda e, cur=cur: e.tensor_tensor(out=cur, in0=cur, in1=m1, op=ALU.add),
                     reads=[rm1, rV], writes=[rV])
                P.op("dve", lambda e, cur=cur: e.tensor_tensor(out=cur, in0=cur, in1=m2, op=ALU.add),
                     reads=[rm2a, rm2b, rV], writes=[rV])
            P.op("dve", lambda e, Vb=Vb: e.tensor_copy(out=Zst, in_=Vb[:, :, :, TC - 2:TC]), reads=[rV], writes=[rZs])

        def finalize(ck):
            Vb = Vbs[ck % 2]
            rV = rVs[ck % 2]
            tsl = slice(ck * TC, (ck + 1) * TC)
            t4 = ck * TC // 512
            P.op("pool", lambda e, Vb=Vb: e.tensor_copy(out=Xb, in_=Vb), reads=[rV], writes=[rX])
            for half in range(2):
                for gg in range(32):
                    g = half * 32 + gg
                    hs_ = slice(64 * half, 64 * half + 64)
                    for ri in range(2):
                        P.op("pe", lambda e, g=g, gg=gg, ri=ri, hs_=hs_: e.matmul(
                            ps[5][0:TC, gg * 16:(gg + 1) * 16], lhsT=Xb[hs_, ri, gg, :], rhs=cw[hs_, ri, g * 16:(g + 1) * 16],
                            start=(ri == 0), stop=(ri == 1)), reads=[rX, R2], writes=[psr[5]])
                P.op("act", lambda e, half=half: e.activation(out=ytok[:, half * 512:(half + 1) * 512], in_=ps[5][0:TC, :],
                                                              func=AF.Copy), reads=[psr[5]], writes=[rY])
            for kc in range(8):
                P.op("pe", lambda e, kc=kc: e.transpose(out=ps[6][:, kc * TC:(kc + 1) * TC], in_=ytok[:, kc * 128:(kc + 1) * 128],
                                                        identity=identf[0:TC, 0:TC]), reads=[rY], writes=[psr[6]])
            P.op("pool", lambda e, tsl=tsl: e.tensor_tensor(out=zp, in0=u[:, :, tsl],
                                                           in1=dsk[:, l, :].unsqueeze(2).to_broadcast([128, 8, TC]), op=ALU.mult),
                 reads=[r_u[t4]], writes=[rZ])
            P.op("act", lambda e: e.activation(out=g1, in_=ps[6][:, 0:8 * TC].rearrange("p (k t) -> p k t", t=TC), func=AF.Copy),
                 reads=[psr[6]], writes=[rZ])
            P.op("pool", lambda e: e.tensor_tensor(out=zp, in0=zp, in1=g1, op=ALU.add), reads=[rZ], writes=[rZ])
            P.op("pool", lambda e: e.tensor_tensor(out=g1, in0=zp, in1=zp, op=ALU.mult), reads=[rZ], writes=[rZ])
            P.op("pool", lambda e: e.tensor_scalar(out=g1, in0=g1, scalar1=0.044715, scalar2=1.0, op0=ALU.mult, op1=ALU.add),
                 reads=[rZ], writes=[rZ])
            P.op("pool", lambda e: e.tensor_tensor(out=g1, in0=g1, in1=zp, op=ALU.mult), reads=[rZ], writes=[rZ])
            P.op("pool", lambda e: e.tensor_scalar(out=g1, in0=g1, scalar1=-30.0, scalar2=None, op0=ALU.max), reads=[rZ], writes=[rZ])
            P.op("act", lambda e: e.activation(out=g2, in_=g1, func=AF.Sigmoid, scale=1.5957691216057308),
                 reads=[rZ], writes=[rZ])
            P.op("pool", lambda e, tsl=tsl: e.tensor_tensor(out=u[:, :, tsl], in0=zp, in1=g2, op=ALU.mult),
                 reads=[rZ, r_u[t4]], writes=[rZ2, r_ul])

        for ck in range(nck):
            inproj(ck)
            if ck > 0:
                finalize(ck - 1)
            scan(ck)
        finalize(nck - 1)

    def glu(l, b):
        n = l * 2
        wv = wview(wglu_b[l])
        sg = [carve(0, [128, 512], F32), carve(2048, [128, 512], F32)]
        yv = [carve(4096, [128, 512], F32), carve(6144, [128, 512], F32)]
        r_sg = [Res(), Res()]
        r_yv = [Res(), Res()]
        for t in range(4):
            for c0 in (0, 512):
                pend = {}

                def epi(t_, oc, p_ap, p_res, pend=pend, t=t):
                    if oc < 8:
                        pend[oc] = (p_ap, p_res)
                        if dbg is not None and dbg[0] in ("gi", "gn", "gs"):
                            P.op("dve", lambda e, p_ap=p_ap, oc=oc, t=t: e.tensor_copy(out=h[:, oc, T(t)], in_=p_ap[:]),
                                 reads=[p_res, r_h[t]], writes=[r_h[t]])
                        return
                    if dbg is not None and dbg[0] in ("gi", "gn", "gs"):
                        return
                    ov = oc - 8
                    i = ov % 2
                    vp, vr = pend[ov]
                    if dbg is not None and dbg[0] in ("gv", "gg"):
                        src_, sr_ = (vp, vr) if dbg[0] == "gv" else (p_ap, p_res)
                        P.op("dve", lambda e, src_=src_, ov=ov, t=t: e.tensor_copy(out=h[:, ov, T(t)], in_=src_[:]),
                             reads=[sr_, vr, p_res, r_h[t]], writes=[r_h[t]])
                        return
                    P.op("act", lambda e, i=i, p_ap=p_ap: e.activation(out=sg[i], in_=p_ap[:], func=AF.Sigmoid),
                         reads=[p_res], writes=[r_sg[i]])
                    P.op("dve", lambda e, i=i, vp=vp: e.tensor_tensor(out=yv[i], in0=vp[:], in1=sg[i], op=ALU.mult),
                         reads=[vr, r_sg[i]], writes=[r_yv[i]])
                    P.op("dve", lambda e, i=i, ov=ov, t=t: e.scalar_tensor_tensor(
                        out=h[:, ov, T(t)], in0=yv[i], scalar=mod_gate(n, ov, b), in1=h[:, ov, T(t)],
                        op0=ALU.mult, op1=ALU.add), reads=[r_yv[i], r_mod, r_h[t]], writes=[r_h[t]])
                for sub in (0, 256):
                    gemm_fm(u, r_u, 8, wv, [[c0 + sub, c0 + sub + 128]], epi, tiles=[t])
                    gemm_fm(u, r_u, 8, wv, [[1024 + c0 + sub, 1024 + c0 + sub + 128]], epi, tiles=[t])

    def kv_phase(b):
        norm(8, b, u, r_u)
        wv = wview(wkv_b)
        stg = [carve(i * 1024, [128, 512], BF16) for i in range(4)]
        r_stg = [Res() for _ in range(4)]
        d_stg = [P.dsem() for _ in range(4)]
        cnt = [0]

        def epi(t, oc, p_ap, p_res):
            i = cnt[0] % 4
            cnt[0] += 1
            P.op("act", lambda e, i=i, p_ap=p_ap: e.activation(out=stg[i], in_=p_ap[:], func=AF.Copy),
                 reads=[p_res], writes=[r_stg[i]])
            me = P.op("sp", lambda e, i=i, oc=oc, t=t: e.dma_start(out=kt_s[b, oc, :, T(t)], in_=stg[i]),
                      reads=[r_stg[i]], dma=d_stg[i])
            r_stg[i].r.append(me)
            r_kv.w = me if r_kv.w is None or True else r_kv.w
            kv_w.append(me)
        gemm_fm(u, r_u, 8, wv, [[c0 + 128 * j for j in range(4)] for c0 in range(0, 3072, 512)], epi)
        for c0 in range(3072, 6144, 512):
            si = load_slab(wv[:, 0:8, c0:c0 + 512])
            for tb in range(16):
                bk = rot["a"] % 6
                rot["a"] += 1
                for kc in range(8):
                    P.op("pe", lambda e, si=si, kc=kc, tb=tb, bk=bk: e.matmul(
                        ps[bk][:], lhsT=u[:, kc, tb * 128:(tb + 1) * 128], rhs=slabs[si][:, kc, :],
                        start=(kc == 0), stop=(kc == 7)), reads=[r_slab[si], r_u[tb // 4]], writes=[psr[bk]])
                i = cnt[0] % 4
                cnt[0] += 1
                P.op("act", lambda e, i=i, bk=bk: e.activation(out=stg[i], in_=ps[bk][:], func=AF.Copy),
                     reads=[psr[bk]], writes=[r_stg[i]])
                me = P.op("sp", lambda e, i=i, tb=tb, c0=c0: e.dma_start(
                    out=v_s[b, tb * 128:(tb + 1) * 128, c0 - 3072:c0 - 3072 + 512], in_=stg[i]),
                    reads=[r_stg[i]], dma=d_stg[i])
                r_stg[i].r.append(me)
                kv_w.append(me)

    r_kv = Res()
    kv_w = []
    DIL = (1, 4, 16)

    def sl_(start, n, step):
        return slice(start, start + (n - 1) * step + 1, step)

    def attention(l, b):
        j_ = l - 2
        n = l * 2
        qT = carve(0, [128, 3, S], BF16)
        kT = carve(12288, [128, 3, S], BF16)
        Vt = carve(24576, [128, 3, 16, 128], BF16)
        pT = [carve(36864, [128, 512], BF16), carve(37888, [128, 512], BF16)]
        rec = carve(38912, [128, 1024], F32)
        oT = big[:, :].rearrange("p (a b) -> p a b", b=S)
        r_q = Res(); r_k = Res(); r_v = Res(); r_p = [Res(), Res()]; r_rec = Res(); r_qs = Res(); r_o = Res()
        d_k = next_misc(); d_v = next_misc(); d_q = next_misc()
        wqv = wview(wq_b[j_])
        pcount = [0]
        for hp in range(8):
            for br in range(3):
                P.op("sp", lambda e, br=br, hp=hp: e.dma_start(out=kT[:, br, :], in_=kt_s[b, br * 8 + hp]),
                     reads=[r_kv], writes=[r_k], dma=d_k)
                d = DIL[br]
                vsrc = v_s[b].rearrange("(n p r) f -> p r n f", p=128, r=d)[:, :, :, br * 1024 + hp * 128: br * 1024 + hp * 128 + 128]
                for r in range(d):
                    nb_ = 16 // d
                    P.op("sp", lambda e, br=br, r=r, nb_=nb_, vsrc=vsrc: e.dma_start(
                        out=Vt[:, br, r * nb_:(r + 1) * nb_, :], in_=vsrc[:, r, :, :]),
                        reads=[r_kv], writes=[r_v], dma=d_v)
            for br in range(3):
                c0 = br * 1024 + hp * 128
                si = load_slab(wqv[:, :, c0:c0 + 128])
                for t in range(4):
                    bk = rot["a"] % 4
                    rot["a"] += 1
                    for kc in range(8):
                        P.op("pe", lambda e, kc=kc, t=t, bk=bk, si=si: e.matmul(ps[bk][:], lhsT=slabs[si][:, kc, 0:128], rhs=u[:, kc, T(t)],
                                                                         start=(kc == 0), stop=(kc == 7)),
                             reads=[r_slab[si], r_u[t]], writes=[psr[bk]])
                    P.op("act", lambda e, br=br, t=t, bk=bk: e.activation(out=qT[:, br, T(t)], in_=ps[bk][:], func=AF.Copy,
                                                                          scale=0.125), reads=[psr[bk]], writes=[r_q])
            for qh in range(2):
                NB = (4, 5)
                DB = (6, 7)
                first = {}
                groups = []
                for hh in range(2):
                    hs = slice(64 * hh, 64 * hh + 64)
                    tl = []
                    for nn in range(8):
                        nblk = qh * 8 + nn
                        for kb, msk in ((nblk - 1, mprev), (nblk, mcur)):
                            if kb < 0:
                                continue
                            tl.append((0, slice(kb * 128, kb * 128 + 128), slice(nblk * 128, nblk * 128 + 128),
                                       msk[:, :], kb, nn // 4, slice((nn % 4) * 128, (nn % 4) * 128 + 128), 128))
                    for r in range(4):
                        for nn in range(2):
                            nblk = qh * 2 + nn
                            for kb, msk in ((nblk - 1, mprev), (nblk, mcur)):
                                if kb < 0:
                                    continue
                                tl.append((1, sl_(r + 512 * kb, 128, 4),
                                           sl_(r + 512 * nblk, 128, 4), msk[:, :], r * 4 + kb,
                                           nn, sl_(r, 128, 4), 128))
                    for r in range(16):
                        for mm in range(2):
                            m_ = qh * 2 + mm
                            tl.append((2, sl_(r, 128, 16), sl_(r + 512 * m_, 32, 16),
                                       mcur[:, 32 * m_:32 * m_ + 32], r, mm, sl_(r, 32, 16), 32))
                    i0 = 0
                    while i0 < len(tl):
                        grp = []
                        w_ = 0
                        while i0 < len(tl) and w_ + tl[i0][7] <= 512:
                            grp.append((tl[i0], w_))
                            w_ += tl[i0][7]
                            i0 += 1
                        groups.append((grp, w_, hh, hs))
                base_ = pcount[0]
                pcount[0] += len(groups)

                def emit_S(gi):
                    grp, w_, hh, hs = groups[gi]
                    bk = (base_ + gi) % 4
                    for ii_, ((br, kap, qap, mk, vb, ob, oc_, nc_), off) in enumerate(grp):
                        P.op("pe", lambda e, br=br, kap=kap, qap=qap, off=off, nc_=nc_, bk=bk, hs=hs, ii_=ii_: e.matmul(
                            ps[bk][:, off:off + nc_], lhsT=kT[hs, br, kap], rhs=qT[hs, br, qap], start=(ii_ == 0), stop=False,
                            skip_group_check=True),
                            reads=[r_k, r_q], writes=[psr[bk]])
                    for ii_, ((br, kap, qap, mk, vb, ob, oc_, nc_), off) in enumerate(grp):
                        P.op("pe", lambda e, mk=mk, off=off, nc_=nc_, bk=bk, ii_=ii_, ng_=len(grp): e.matmul(
                            ps[bk][:, off:off + nc_], lhsT=identb[:], rhs=mk, start=False, stop=(ii_ == ng_ - 1), skip_group_check=True),
                            writes=[psr[bk]])

                def emit_PV(gi):
                    grp, w_, hh, hs = groups[gi]
                    bk = (base_ + gi) % 4
                    pi = (base_ + gi) % 2
                    P.op("act", lambda e, bk=bk, pi=pi, w_=w_: e.activation(out=pT[pi][:, 0:w_], in_=ps[bk][:, 0:w_], func=AF.Exp),
                         reads=[psr[bk]], writes=[r_p[pi]])
                    for (br, kap, qap, mk, vb, ob, oc_, nc_), off in grp:
                        for (bank, lhs) in ((NB[ob], Vt[:, br, vb, hs]), (DB[ob], onesb[:, 0:64])):
                            key = (bank, hh)
                            st_ = key not in first
                            first[key] = 1
                            P.op("pe", lambda e, bank=bank, lhs=lhs, oc_=oc_, pi=pi, off=off, nc_=nc_, st_=st_, hs=hs: e.matmul(
                                ps[bank][hs, oc_], lhsT=lhs, rhs=pT[pi][:, off:off + nc_], start=st_, stop=False,
                                skip_group_check=True),
                                reads=[r_p[pi], r_v], writes=[psr[bank]])

                emit_S(0)
                for gi in range(len(groups)):
                    if gi + 1 < len(groups):
                        emit_S(gi + 1)
                    emit_PV(gi)
                for ob in range(2):
                    P.op("dve", lambda e, ob=ob: e.reciprocal(out=rec[:, ob * 512:(ob + 1) * 512], in_=ps[DB[ob]][:]),
                         reads=[psr[DB[ob]]], writes=[r_rec])
                    P.op("dve", lambda e, ob=ob, hp=hp, qh=qh: e.tensor_tensor(
                        out=oT[:, hp, qh * 1024 + ob * 512: qh * 1024 + (ob + 1) * 512], in0=ps[NB[ob]][:],
                        in1=rec[:, ob * 512:(ob + 1) * 512], op=ALU.mult),
                        reads=[psr[NB[ob]], r_rec], writes=[r_o])
        rot["a"] = 0
        gemm_fm(oT, r_o, 8, wview(wo_b[j_]), [[c0 + 128 * j for j in range(4)] for c0 in (0, 512)], resid_epi(n, b))

    step = [0]

    def done():
        step[0] += 1
        return step[0] >= stop_after

    def u_to_h():
        P.barrier()
        for t in range(4):
            for kc in range(8):
                P.op("act", lambda e, t=t, kc=kc: e.activation(out=h[:, kc, T(t)], in_=u[:, kc, T(t)], func=AF.Copy),
                     reads=[r_u[t]], writes=[r_h[t]])

    for b in range(nseq):
        P.barrier()
        load_x(b)
        stopped = False
        for l in range(4):
            if l == 2:
                P.barrier()
                kv_w.clear()
                kv_phase(b)
                r_kv.w = None
                P.barrier()
            P.barrier()
            if dbg == ("w", l):
                wtmp = big[:, :].rearrange("p (a b) -> p a b", b=2048)
                rw_ = Res()
                P.op("sp", lambda e: e.dma_start(out=wtmp, in_=wview(wglu_b[0])), reads=[r_wcast, r_wcast2], writes=[rw_], dma=d_tok[1])
                P.op("sp", lambda e: e.dma_start(out=dbg_w.rearrange("(kc p) n -> p kc n", p=128), in_=wtmp), reads=[rw_], writes=[rw_], dma=d_tok[1])
                stopped = True
                break
            if dbg != ("m", l):
                norm(l * 2, b, u, r_u)
            if dbg == ("u", l):
                u_to_h()
                stopped = True
                break
            if dbg == ("m", l):
                pass
            elif dbg == ("gs", l):
                L = s5_layer_setup(l)
                s5_mixer(l, b, L)
                P.barrier()
                norm(l * 2, b, u, r_u)
                P.barrier()
                glu(l, b)
                stopped = True
                break
            elif dbg == ("gn", l):
                glu(l, b)
                stopped = True
                break
            elif l < 2:
                L = s5_layer_setup(l)
                if dbg == ("p", l):
                    P.barrier()
                    stopped = True
                    break
                s5_mixer(l, b, L)
                P.barrier()
                if dbg is not None and len(dbg) > 2 and dbg[2] == "fix":
                    norm(l * 2, b, big[:, :].rearrange("p (a b) -> p a b", b=S), [Res() for _ in range(4)])
                    P.barrier()
                if dbg is not None and dbg[:2] == ("pu", l):
                    for t_ in range(4):
                        for kc_ in range(8):
                            bk_ = (t_ * 8 + kc_) % 4
                            P.op("pe", lambda e, t_=t_, kc_=kc_, bk_=bk_: e.matmul(ps[bk_][:], lhsT=identb[:], rhs=u[:, kc_, T(t_)],
                                                                                   start=True, stop=True), reads=[r_u[t_]], writes=[psr[bk_]])
                            P.op("dve", lambda e, t_=t_, kc_=kc_, bk_=bk_: e.tensor_copy(out=h[:, kc_, T(t_)], in_=ps[bk_][:]),
                                 reads=[psr[bk_], r_h[t_]], writes=[r_h[t_]])
                    stopped = True
                    break
                if dbg is not None and dbg[:2] in (("z", l), ("y", l)):
                    u_to_h()
                    stopped = True
                    break
                glu(l, b)
            else:
                attention(l, b)
            if done():
                stopped = True
                break
            P.barrier()
            norm(l * 2 + 1, b, u, r_u)
            mlp(l, b)
            if done():
                stopped = True
                break
        step[0] = 0
        P.barrier()
        store_out(b, final=not stopped)
    P.barrier()
    P.op("sp", lambda e: e.dma_start(out=tok[0][0:1, 0:8], in_=x_d[0, 0:1, 0:8]), dma=d_tok[0])
    P.emit([(d_tok[0], P.cnt[d_tok[0]])] + [(k, P.cnt[k]) for k in d_otok])
    st.close()
    return nc


def _bf16(a):
    import ml_dtypes
    return np.asarray(a, dtype=np.float32).astype(ml_dtypes.bfloat16)


def make_in_maps(inp):
    f = lambda a: np.ascontiguousarray(np.asarray(a, dtype=np.float32))
    x = f(inp["x"]); c = f(inp["c"])
    adaw = np.concatenate([f(inp["ada_w"]).reshape(8, D, 3072)[i] for i in range(8)] + [f(inp["kv_ada_w"])], axis=1)
    adab_flat = np.concatenate([f(inp["ada_b"]).reshape(-1), f(inp["kv_ada_b"])])
    adab = np.ascontiguousarray(adab_flat.reshape(NCH, 128).T)
    lng_all = np.concatenate([f(inp["ln_g"]).reshape(8, D), f(inp["kv_g"]).reshape(1, D)], axis=0)
    lng = np.ascontiguousarray(lng_all.reshape(9, 8, 128).transpose(2, 0, 1))
    fing = np.ascontiguousarray(np.broadcast_to(f(inp["final_g"])[None, :], (128, D)))
    dsk = np.ascontiguousarray(f(inp["ssm_d"]).reshape(2, 8, 128).transpose(2, 0, 1))
    kk = np.arange(128)[:, None]; qq = np.arange(128)[None, :]
    mcur = np.where(kk <= qq, 0.0, -30000.0).astype(np.float32)
    mprev = np.where(kk >= qq, 0.0, -30000.0).astype(np.float32)
    common = dict(
        adaw=np.ascontiguousarray(adaw), adab=adab, lng=lng, fing=fing, dsk=dsk,
        lamre=f(inp["ssm_lam_re"]), lamim=f(inp["ssm_lam_im"]), logdt=f(inp["ssm_log_dt"]).reshape(2, 64, 1),
        bre=f(inp["ssm_b_re"]).reshape(2, 64, 1024), bim=f(inp["ssm_b_im"]).reshape(2, 64, 1024),
        cre=f(inp["ssm_c_re"]).reshape(2, 1024, 64), cim=f(inp["ssm_c_im"]).reshape(2, 1024, 64),
        identf=np.eye(128, dtype=np.float32), identb=_bf16(np.eye(128)), mcur=_bf16(mcur), mprev=_bf16(mprev),
        w1=f(inp["mlp_w1"]), w2=f(inp["mlp_w2"]), wglu=f(inp["ssm_w_glu"]), wkv=f(inp["w_kv"]),
        wq=f(inp["attn_w_q"]), wo=f(inp["attn_w_o"]),
    )
    maps = []
    for i in range(8):
        m = dict(common)
        m["x"] = np.ascontiguousarray(x[2 * i:2 * i + 2])
        m["cT"] = np.ascontiguousarray(c[2 * i:2 * i + 2].T.reshape(8, 128, 2).transpose(1, 0, 2))
        maps.append(m)
    return maps


def kernel(**inputs):
    nc = build_program()
    maps = make_in_maps(inputs)
    res = run_bass_kernel_spmd(nc, maps, core_ids=list(range(8)))
    return np.concatenate([r["out"] for r in res.results], axis=0).astype(np.float32)
```

```python
import contextlib
import numpy as np
import concourse.bass as bass
import concourse.mybir as mybir
from concourse.bass_utils import run_bass_kernel_spmd

F32 = mybir.dt.float32
BF16 = mybir.dt.bfloat16
I32 = mybir.dt.int32
AF = mybir.ActivationFunctionType
ALU = mybir.AluOpType

ENGS = ("pe", "act", "dve", "pool", "sp")
S = 2048
D = 1024
NMOD = 26624
NCH = NMOD // 128
EPS = 1e-6


class Res:
    __slots__ = ("w", "r")

    def __init__(self):
        self.w = None
        self.r = []


class Prog:
    def __init__(self, nc):
        self.nc = nc
        self.ops = {e: [] for e in ENGS}
        self.cnt = {}
        self.seen = {e: {} for e in ENGS}
        self.sems = {}
        self.nd = 0
        self.bar = None
        self.bar_done = {e: None for e in ENGS}

    def dsem(self):
        k = "d%d" % self.nd
        self.nd += 1
        self.cnt[k] = 0
        return k

    def barrier(self):
        self.bar = dict(self.cnt)

    def op(self, eng, fn, reads=(), writes=(), dma=None, after=()):
        waits = {}
        for k_, v_ in after:
            waits[k_] = v_

        def need(dep, war=False):
            if dep is None:
                return
            k, v = dep
            if k == eng and (eng == "pe" or (war and eng != "pool")):
                return
            if waits.get(k, 0) < v:
                waits[k] = v

        if self.bar is not None and self.bar_done[eng] is not self.bar:
            self.bar_done[eng] = self.bar
            for k, v in self.bar.items():
                if v > 0 and k != eng:
                    waits[k] = v
        for r in reads:
            need(r.w)
        for w in writes:
            need(w.w)
            for d in w.r:
                need(d, war=True)
        final = []
        seen = self.seen[eng]
        for k, v in waits.items():
            if seen.get(k, 0) >= v:
                continue
            seen[k] = v
            final.append((k, v))
        key, inc = (eng, 1) if dma is None else (dma, 16)
        self.cnt[key] = self.cnt.get(key, 0) + inc
        me = (key, self.cnt[key])
        self.ops[eng].append((final, fn, key, inc))
        for r in reads:
            r.r.append(me)
            if len(r.r) > 64:
                r.r = _compress(r.r)
        for w in writes:
            w.w = me
            w.r = []
        return me

    def emit(self, final_waits):
        nc = self.nc
        with contextlib.ExitStack() as st:
            for k in list(self.cnt.keys()):
                self.sems[k] = st.enter_context(nc.semaphore("s_" + k))
            block = st.enter_context(nc.Block())
            sems = self.sems

            def run(engname, e):
                for waits, fn, key, inc in self.ops[engname]:
                    for k, v in waits:
                        e.wait_ge(sems[k], v)
                    fn(e).then_inc(sems[key], inc)
                if engname == "sp":
                    for k, v in final_waits:
                        e.wait_ge(sems[k], v)

            @block.tensor
            def _(e):
                run("pe", e)

            @block.scalar
            def _(e):
                run("act", e)

            @block.vector
            def _(e):
                run("dve", e)

            @block.gpsimd
            def _(e):
                run("pool", e)

            @block.sync
            def _(e):
                run("sp", e)


def _compress(lst):
    best = {}
    for k, v in lst:
        if best.get(k, 0) < v:
            best[k] = v
    return list(best.items())


def build_program(stop_after=99, dbg=None, nseq=2):
    nc = bass.Bass("TRN2", target_bir_lowering=False)
    P = Prog(nc)

    def din(name, shape, dt=F32):
        return nc.dram_tensor(name, list(shape), dt, kind="ExternalInput").ap()

    def dscr(name, shape, dt=BF16):
        return nc.dram_tensor(name, list(shape), dt, kind="Internal").ap()

    x_d = din("x", [2, S, D])
    cT_d = din("cT", [128, 8, 2])
    adaw_d = din("adaw", [D, NMOD])
    adab_d = din("adab", [128, NCH])
    lng_d = din("lng", [128, 9, 8])
    fing_d = din("fing", [128, D])
    dsk_d = din("dsk", [128, 2, 8])
    lamre_d = din("lamre", [2, 64, 64])
    lamim_d = din("lamim", [2, 64, 64])
    logdt_d = din("logdt", [2, 64, 1])
    bre_d = din("bre", [2, 64, 1024])
    bim_d = din("bim", [2, 64, 1024])
    cre_d = din("cre", [2, 1024, 64])
    cim_d = din("cim", [2, 1024, 64])
    identf_d = din("identf", [128, 128])
    identb_d = din("identb", [128, 128], BF16)
    mcur_d = din("mcur", [128, 128], BF16)
    mprev_d = din("mprev", [128, 128], BF16)
    w1_d = din("w1", [4, D, 4096])
    w2_d = din("w2", [4, 4096, D])
    wglu_d = din("wglu", [2, D, 2048])
    wkv_d = din("wkv", [D, 6144])
    wq_d = din("wq", [2, D, 3072])
    wo_d = din("wo", [2, D, D])
    out_d = nc.dram_tensor("out", [2, S, D], F32, kind="ExternalOutput").ap()
    if dbg is not None and dbg[0] == "w":
        dbg_w = nc.dram_tensor("dbg_w", [D, 2048], BF16, kind="ExternalOutput").ap()
    if dbg is not None and dbg[0] == "p":
        dbg_l = nc.dram_tensor("dbg_l", [64, 256], F32, kind="ExternalOutput").ap()
        dbg_bb = nc.dram_tensor("dbg_bb", [64, 16, 2, 64], BF16, kind="ExternalOutput").ap()

    w1_b = dscr("w1b", [4, D, 4096])
    w2_b = dscr("w2b", [4, 4096, D])
    wglu_b = dscr("wglub", [2, D, 2048])
    wkv_b = dscr("wkvb", [D, 6144])
    wq_b = dscr("wqb", [2, D, 3072])
    wo_b = dscr("wob", [2, D, D])
    kt_s = dscr("kts", [2, 24, 128, S])
    v_s = dscr("vs", [2, S, 3072])
    bb_s = dscr("bbs", [2, 64, 16, 128])
    bb_s1 = dscr("bbs1", [2, 64, 16, 128])

    st = contextlib.ExitStack()

    def sb(name, shape, dt):
        return st.enter_context(nc.sbuf_tensor(name, list(shape), dt))

    h = sb("h", [128, 8, S], F32)
    u = sb("u", [128, 8, S], BF16)
    big = sb("big", [128, 16384], BF16)
    slabs = [sb("slab%d" % i, [128, 8, 512], BF16) for i in range(3)]
    sq = [sb("sq%d" % i, [128, 512], BF16) for i in range(2)]
    rs = sb("rs", [128, 512], F32)
    rstd = rs
    tmpf = [sb("tmpf%d" % i, [128, 512], F32) for i in range(2)]
    identf = sb("identf_s", [128, 128], F32)
    identb = sb("identb_s", [128, 128], BF16)
    onesb = sb("onesb", [128, 128], BF16)
    mcur = sb("mcur_s", [128, 128], BF16)
    mprev = sb("mprev_s", [128, 128], BF16)
    mod = sb("mod", [128, NCH, 2], F32)
    adab = sb("adab_s", [128, NCH], F32)
    lng = sb("lng_s", [128, 9, 8], F32)
    Gm = sb("Gm", [128, 9, 8, 2], F32)
    dsk = sb("dsk_s", [128, 2, 8], F32)
    cT = sb("cT_s", [128, 8, 2], F32)
    scT = sb("scT", [128, 8, 2], F32)
    arena = sb("arena", [128, 10752], F32)

    def carve(off, shape, dt, parts=128):
        nel = int(np.prod(shape[1:]))
        nb = nel * (2 if dt == BF16 else 4)
        assert off % 4 == 0 and off + nb <= 10752 * 4, (off, nb)
        ap = arena[0:parts, off // 4:(off + nb + 3) // 4]
        if dt == BF16:
            ap = ap.bitcast(BF16)
        elif dt == I32:
            ap = ap.bitcast(I32)
        if len(shape) == 3:
            ap = ap.rearrange("p (a b) -> p a b", b=shape[2])
        elif len(shape) == 4:
            ap = ap.rearrange("p (a b c) -> p a b c", b=shape[2], c=shape[3])
        return ap

    ps = [st.enter_context(nc.psum_tensor("ps%d" % i, [128, 512], F32)) for i in range(8)]
    psr = [Res() for _ in range(8)]

    r_h = [Res() for _ in range(4)]
    r_u = [Res() for _ in range(4)]
    r_big = Res()
    r_slab = [Res() for _ in range(3)]
    d_slab = [P.dsem() for _ in range(3)]
    r_sq = [Res(), Res()]
    r_rs = Res()
    r_rstd = Res()
    r_tmpf = [Res(), Res()]
    r_const = Res()
    r_mod = Res()
    d_const = P.dsem()
    d_constp = P.dsem()
    d_misc = [P.dsem() for _ in range(8)]
    misc_i = [0]
    slab_i = [0]
    rot = {"a": 0}

    def next_misc():
        k = d_misc[misc_i[0] % len(d_misc)]
        misc_i[0] += 1
        return k

    for (dst, src) in [(identf, identf_d), (adab, adab_d), (lng, lng_d), (dsk, dsk_d), (cT, cT_d)]:
        P.op("sp", lambda e, dst=dst, src=src: e.dma_start(out=dst[:], in_=src), writes=[r_const], dma=d_const)
    for (dst, src) in [(identb, identb_d), (mcur, mcur_d), (mprev, mprev_d)]:
        P.op("pool", lambda e, dst=dst, src=src: e.dma_start(out=dst[:], in_=src), writes=[r_const], dma=d_constp)
    r_const.w = (d_const, P.cnt[d_const])
    P.op("dve", lambda e: e.memset(onesb[:], 1.0), writes=[r_const])
    r_const.w = None
    P.barrier()
    scTb = carve(40960, [128, 8, 2], BF16)
    P.op("act", lambda e: e.activation(out=scTb, in_=cT[:], func=AF.Silu), writes=[r_const])
    aslab = [carve(i * 8192, [128, 8, 512], BF16) for i in range(4)]
    r_aslab = [Res() for _ in range(4)]
    d_aslab = [P.dsem() for _ in range(4)]
    adaw_v = adaw_d.rearrange("(kc p) n -> p kc n", p=128)
    nsl = NMOD // 512
    for s_ in range(nsl):
        bi = s_ % 4
        P.op("pool", lambda e, s_=s_, bi=bi: e.dma_start(out=aslab[bi], in_=adaw_v[:, :, s_ * 512:(s_ + 1) * 512]),
             writes=[r_aslab[bi]], dma=d_aslab[bi])
        for j in range(4):
            ch = s_ * 4 + j
            for kc in range(8):
                P.op("pe", lambda e, bi=bi, j=j, kc=kc, ch=ch: e.matmul(
                    ps[7][:, 2 * ch:2 * ch + 2], lhsT=aslab[bi][:, kc, j * 128:(j + 1) * 128], rhs=scTb[:, kc, :],
                    start=(kc == 0), stop=(kc == 7)), reads=[r_aslab[bi], r_const], writes=[psr[7]])

    d_castAB = [P.dsem(), P.dsem()]
    cast_jobs = []
    cast_pos = [0]
    grp_end = {}
    key_last = {}
    wres = {}

    def add_cast(key, dst, src, rows, cols):
        for r0 in range(0, rows, 128):
            for c0 in range(0, cols, 1024):
                cast_jobs.append((key, dst, src, r0, c0, min(cols, c0 + 1024)))
        key_last[key] = len(cast_jobs) - 1

    for l in range(2):
        add_cast(("wglu", l), wglu_b[l], wglu_d[l], D, 2048)
        add_cast(("w1", l), w1_b[l], w1_d[l], D, 4096)
        add_cast(("w2", l), w2_b[l], w2_d[l], 4096, D)
    add_cast(("wkv", 0), wkv_b, wkv_d, D, 6144)
    for l in range(2):
        add_cast(("wq", l), wq_b[l], wq_d[l], D, 3072)
        add_cast(("wo", l), wo_b[l], wo_d[l], D, D)
        add_cast(("w1", l + 2), w1_b[l + 2], w1_d[l + 2], D, 4096)
        add_cast(("w2", l + 2), w2_b[l + 2], w2_d[l + 2], 4096, D)

    def pump(n):
        for _ in range(n):
            i = cast_pos[0]
            if i >= len(cast_jobs):
                return
            key, dst, src, r0, c0, c1 = cast_jobs[i]
            g_ = i // 24
            dk = d_castAB[g_ % 2]
            aft = [(dk, P.cnt[dk])] if (i % 24 == 0 and g_ >= 2) else []
            cast_pos[0] += 1
            me = P.op("pool", lambda e, dst=dst, src=src, r0=r0, c0=c0, c1=c1: e.dma_start(
                out=dst[r0:r0 + 128, c0:c1], in_=src[r0:r0 + 128, c0:c1], max_dma_last_dim=4096),
                dma=dk, after=aft)
            grp_end[g_] = me

    def wready(key):
        if key not in wres:
            last = key_last[key]
            tgt = min(len(cast_jobs), (last // 24 + 1) * 24)
            if cast_pos[0] < tgt:
                pump(tgt - cast_pos[0])
            r = Res()
            r.w = grp_end[last // 24]
            wres[key] = r
        return wres[key]

    P.op("dve", lambda e: e.tensor_tensor(out=mod[:], in0=ps[7][:, 0:2 * NCH].rearrange("p (c b) -> p c b", b=2),
                                          in1=adab[:].unsqueeze(2).to_broadcast([128, NCH, 2]), op=ALU.add),
         reads=[psr[7]], writes=[r_mod])
    for n in range(9):
        base = (n * 3072 + 1024) // 128 if n < 8 else (24576 + 1024) // 128
        P.op("dve", lambda e, n=n, base=base: e.scalar_tensor_tensor(
            out=Gm[:, n, :, :], in0=mod[:, base:base + 8, :], scalar=1.0,
            in1=lng[:, n, :].unsqueeze(2).to_broadcast([128, 8, 2]), op0=ALU.add, op1=ALU.mult),
            reads=[r_mod], writes=[r_mod])

    def mod_shift(n, kc, b):
        base = (n * 3072) // 128 if n < 8 else 24576 // 128
        return mod[:, base + kc, b:b + 1]

    def mod_gate(n, kc, b):
        base = (n * 3072 + 2048) // 128
        return mod[:, base + kc, b:b + 1]

    def T(t):
        return slice(t * 512, (t + 1) * 512)

    def norm(n, b, out_buf, r_out):
        for t in range(4):
            for kc in range(8):
                i = kc % 2
                P.op("act", lambda e, kc=kc, i=i, t=t: e.activation(out=sq[i][:], in_=h[:, kc, T(t)], func=AF.Square),
                     reads=[r_h[t]], writes=[r_sq[i]])
                P.op("pe", lambda e, kc=kc, i=i: e.matmul(ps[6][:], lhsT=onesb[:], rhs=sq[i][:],
                                                          start=(kc == 0), stop=(kc == 7)),
                     reads=[r_sq[i]], writes=[psr[6]])
            P.op("act", lambda e: e.activation(out=rs[:], in_=ps[6][:], func=AF.Sqrt, bias=EPS, scale=1.0 / D),
                 reads=[psr[6]], writes=[r_rs, r_rstd])
            P.op("dve", lambda e: e.reciprocal(out=rstd[:], in_=rs[:]), reads=[r_rs], writes=[r_rs, r_rstd])
            for kc in range(8):
                i = kc % 2
                P.op("pool", lambda e, kc=kc, i=i, t=t: e.tensor_tensor(out=tmpf[i][:], in0=h[:, kc, T(t)], in1=rstd[:],
                                                                         op=ALU.mult),
                     reads=[r_h[t], r_rstd], writes=[r_tmpf[i]])
                P.op("act", lambda e, kc=kc, i=i, t=t: e.activation(
                    out=out_buf[:, kc, T(t)], in_=tmpf[i][:], func=AF.Identity,
                    bias=mod_shift(n, kc, b), scale=Gm[:, n, kc, b:b + 1]),
                    reads=[r_tmpf[i], r_mod], writes=[r_out[t]])

    def load_slab(view, wr):
        i = slab_i[0] % 3
        slab_i[0] += 1
        P.op("sp", lambda e, i=i, view=view: e.dma_start(out=slabs[i][:, 0:view.shape[1], 0:view.shape[2]], in_=view),
             reads=[wr], writes=[r_slab[i]], dma=d_slab[i])
        return i

    def gemm_fm(src, r_src, KC, wv, colgroups, epi, tiles=range(4), wr=None):
        for t in tiles:
            for grp in colgroups:
                c0 = grp[0]
                banks = []
                for _ in grp:
                    banks.append(rot["a"] % 6)
                    rot["a"] += 1
                for kg in range(KC // 8):
                    si = load_slab(wv[:, kg * 8:(kg + 1) * 8, c0:grp[-1] + 128], wr)
                    for j, c in enumerate(grp):
                        for kc in range(8):
                            kk = kg * 8 + kc
                            P.op("pe", lambda e, si=si, j=j, c=c, kc=kc, kk=kk, bk=banks[j], t=t, c0=c0: e.matmul(
                                ps[bk][:], lhsT=slabs[si][:, kc, c - c0:c - c0 + 128], rhs=src[:, kk, T(t)],
                                start=(kk == 0), stop=(kk == KC - 1)),
                                reads=[r_slab[si], r_src[t] if isinstance(r_src, list) else r_src], writes=[psr[banks[j]]])
                for j, c in enumerate(grp):
                    epi(t, c // 128, ps[banks[j]], psr[banks[j]])

    def wview(w2d):
        return w2d.rearrange("(kc p) n -> p kc n", p=128)

    def resid_epi(n, b):
        def epi(t, oc, p_ap, p_res):
            P.op("dve", lambda e, t=t, oc=oc, p_ap=p_ap: e.scalar_tensor_tensor(
                out=h[:, oc, T(t)], in0=p_ap[:], scalar=mod_gate(n, oc, b), in1=h[:, oc, T(t)],
                op0=ALU.mult, op1=ALU.add), reads=[p_res, r_mod, r_h[t]], writes=[r_h[t]])
        return epi

    hid = big[:, :].rearrange("p (a b) -> p a b", b=512)
    r_hid = Res()
    relu_t = [carve(0, [128, 512], BF16), carve(1024, [128, 512], BF16)]
    r_relu = [Res(), Res()]

    def mlp(l, b):
        n = l * 2 + 1
        w1v = wview(w1_b[l])
        w2v = wview(w2_b[l])
        for t in range(4):
            def epi1(t_, hc, p_ap, p_res):
                i = hc % 2
                P.op("act", lambda e, i=i, p_ap=p_ap: e.activation(out=relu_t[i], in_=p_ap[:], func=AF.Relu),
                     reads=[p_res], writes=[r_relu[i]])
                P.op("pool", lambda e, i=i, hc=hc: e.tensor_tensor(out=hid[:, hc, :], in0=relu_t[i], in1=relu_t[i],
                                                                    op=ALU.mult),
                     reads=[r_relu[i]], writes=[r_hid])
            gemm_fm(u, r_u, 8, w1v, [[c0 + 128 * j for j in range(4)] for c0 in range(0, 4096, 512)], epi1, tiles=[t],
                    wr=wready(("w1", l)))

            class HidSrc:
                def __getitem__(self, idx):
                    return hid[idx[0], idx[1], :]
            gemm_fm(HidSrc(), r_hid, 32, w2v, [[c0 + 128 * j for j in range(4)] for c0 in (0, 512)],
                    lambda t_, oc, p_ap, p_res, t=t: resid_epi(n, b)(t, oc, p_ap, p_res), tiles=[t], wr=wready(("w2", l)))

    tok = [carve(0, [128, D], F32), carve(4096, [128, D], F32)]
    r_tok = [Res(), Res()]
    d_tok = [P.dsem(), P.dsem()]
    otok = [carve(8192, [128, D], F32), carve(12288, [128, D], F32)]
    r_otok = [Res(), Res()]
    d_otok = [P.dsem(), P.dsem()]
    ss1 = carve(16384, [128, 1], F32)
    ss2 = carve(16400, [128, 1], F32)
    ss3 = carve(16416, [128, 1], F32)
    junk = carve(16448, [128, D], F32)
    fing = carve(20544, [128, D], F32)
    r_ss = Res()
    last_out = []

    def load_x(b):
        for tb in range(16):
            i = tb % 2
            P.op("sp", lambda e, tb=tb, i=i: e.dma_start(out=tok[i], in_=x_d[b, tb * 128:(tb + 1) * 128, :]),
                 writes=[r_tok[i]], dma=d_tok[i])
            for half in range(2):
                bk = rot["a"] % 6
                rot["a"] += 1
                for j in range(4):
                    kc = half * 4 + j
                    P.op("pe", lambda e, i=i, kc=kc, j=j, bk=bk: e.transpose(
                        out=ps[bk][:, j * 128:(j + 1) * 128], in_=tok[i][:, kc * 128:(kc + 1) * 128], identity=identf[:]),
                        reads=[r_tok[i]], writes=[psr[bk]])
                P.op("act", lambda e, half=half, bk=bk, tb=tb: e.activation(
                    out=h[:, half * 4:half * 4 + 4, tb * 128:(tb + 1) * 128],
                    in_=ps[bk][:].rearrange("p (a b) -> p a b", b=128), func=AF.Copy),
                    reads=[psr[bk]], writes=[r_h[tb // 4]])

    def store_out(b, final=True):
        P.op("sp", lambda e: e.dma_start(out=fing, in_=fing_d), writes=[r_ss], dma=d_tok[0])
        for tb in range(16):
            i = tb % 2
            for half in range(2):
                for j in range(4):
                    kc = half * 4 + j
                    P.op("pe", lambda e, kc=kc, j=j, half=half, tb=tb: e.transpose(
                        out=ps[half][:, j * 128:(j + 1) * 128], in_=h[:, kc, tb * 128:(tb + 1) * 128], identity=identf[:]),
                        reads=[r_h[tb // 4]], writes=[psr[half]])
            if final:
                for half in range(2):
                    P.op("act", lambda e, half=half: e.activation(
                        out=junk[:, half * 512:(half + 1) * 512], in_=ps[half][:], func=AF.Square,
                        accum_out=(ss1 if half == 0 else ss2)), reads=[psr[half]], writes=[r_ss])
                P.op("dve", lambda e: e.tensor_tensor(out=ss3, in0=ss1, in1=ss2, op=ALU.add), reads=[r_ss], writes=[r_ss])
                P.op("act", lambda e: e.activation(out=ss1, in_=ss3, func=AF.Sqrt, bias=EPS, scale=1.0 / D),
                     reads=[r_ss], writes=[r_ss])
                P.op("dve", lambda e: e.reciprocal(out=ss2, in_=ss1), reads=[r_ss], writes=[r_ss])
                for half in range(2):
                    P.op("dve", lambda e, half=half, i=i: e.scalar_tensor_tensor(
                        out=otok[i][:, half * 512:(half + 1) * 512], in0=ps[half][:], scalar=ss2,
                        in1=fing[:, half * 512:(half + 1) * 512], op0=ALU.mult, op1=ALU.mult),
                        reads=[psr[half], r_ss], writes=[r_otok[i]])
            else:
                for half in range(2):
                    P.op("act", lambda e, half=half, i=i: e.activation(
                        out=otok[i][:, half * 512:(half + 1) * 512], in_=ps[half][:], func=AF.Copy),
                        reads=[psr[half]], writes=[r_otok[i]])
            me = P.op("sp", lambda e, i=i, tb=tb: e.dma_start(out=out_d[b, tb * 128:(tb + 1) * 128, :], in_=otok[i]),
                      reads=[r_otok[i]], dma=d_otok[i])
            r_otok[i].r.append(me)
            last_out.append(me)

    def s5_prep(l):
        P.barrier()
        o = [0]

        def A(shape, dt=F32, parts=64):
            nel = int(np.prod(shape[1:]))
            nb = ((nel * (2 if dt == BF16 else 4) + 3) // 4) * 4
            ap = carve(o[0], shape, dt, parts)
            o[0] += nb
            return ap
        lre = A([64, 64]); lim = A([64, 64]); ldt = A([64, 1]); dt_ = A([64, 1])
        a_ = A([64, 64]); th = A([64, 64]); ea = A([64, 64]); yy = A([64, 64]); ki = A([64, 64], I32)
        kf = A([64, 64]); ff = A([64, 64]); sn = A([64, 64]); cs = A([64, 64])
        lbr = A([64, 64]); lbi = A([64, 64]); den = A([64, 64]); nr = A([64, 64]); ni = A([64, 64])
        qr = A([64, 64]); qi = A([64, 64]); t1 = A([64, 64]); t2 = A([64, 64])
        br = A([64, 64, 16]); bi_ = A([64, 64, 16])
        bbr = A([64, 64, 16]); bbi = A([64, 64, 16]); t3 = A([64, 64, 16])
        bout = A([64, 16, 2, 64], BF16)
        l2r = A([64, 64]); l2i = A([64, 64])
        R = Res()
        dd = next_misc()
        for dst, src in [(lre, lamre_d[l]), (lim, lamim_d[l]), (ldt, logdt_d[l])]:
            P.op("sp", lambda e, dst=dst, src=src: e.dma_start(out=dst, in_=src), writes=[R], dma=dd)
        P.op("sp", lambda e: e.dma_start(out=br, in_=bre_d[l].rearrange("g (p c) -> g p c", c=16)), writes=[R], dma=dd)
        P.op("sp", lambda e: e.dma_start(out=bi_, in_=bim_d[l].rearrange("g (p c) -> g p c", c=16)), writes=[R], dma=dd)

        def V(fn):
            P.op("dve", fn, reads=[R], writes=[R])

        def ACT(fn):
            P.op("act", fn, reads=[R], writes=[R])
        ACT(lambda e: e.activation(out=dt_, in_=ldt, func=AF.Exp))
        V(lambda e: e.tensor_scalar(out=a_, in0=lre, scalar1=dt_, scalar2=None, op0=ALU.mult))
        V(lambda e: e.tensor_scalar(out=th, in0=lim, scalar1=dt_, scalar2=None, op0=ALU.mult))
        ACT(lambda e: e.activation(out=ea, in_=a_, func=AF.Exp))

        def sin_of(dst, offs):
            V(lambda e: e.tensor_scalar(out=yy, in0=th, scalar1=1.0 / (2 * np.pi), scalar2=offs, op0=ALU.mult, op1=ALU.add))
            V(lambda e: e.tensor_copy(out=ki, in_=yy))
            V(lambda e: e.tensor_copy(out=kf, in_=ki))
            V(lambda e: e.tensor_tensor(out=ff, in0=yy, in1=kf, op=ALU.subtract))
            V(lambda e: e.scalar_tensor_tensor(out=ff, in0=ff, scalar=0.0, in1=ff, op0=ALU.is_lt, op1=ALU.add))
            V(lambda e: e.tensor_scalar(out=ff, in0=ff, scalar1=2 * np.pi, scalar2=-np.pi, op0=ALU.mult, op1=ALU.add))
            V(lambda e: e.tensor_scalar(out=ff, in0=ff, scalar1=-3.14159, scalar2=3.14159, op0=ALU.max, op1=ALU.min))
            ACT(lambda e: e.activation(out=dst, in_=ff, func=AF.Sin))
        sin_of(sn, 0.5)
        sin_of(cs, 0.75)
        V(lambda e: e.tensor_tensor(out=lbr, in0=ea, in1=cs, op=ALU.mult))
        V(lambda e: e.tensor_tensor(out=lbi, in0=ea, in1=sn, op=ALU.mult))
        V(lambda e: e.tensor_scalar(out=nr, in0=lbr, scalar1=-1.0, scalar2=None, op0=ALU.add))
        V(lambda e: e.tensor_tensor(out=t1, in0=lre, in1=lre, op=ALU.mult))
        V(lambda e: e.tensor_tensor(out=t2, in0=lim, in1=lim, op=ALU.mult))
        V(lambda e: e.tensor_tensor(out=den, in0=t1, in1=t2, op=ALU.add))
        V(lambda e: e.reciprocal(out=den, in_=den))
        V(lambda e: e.tensor_tensor(out=t1, in0=nr, in1=lre, op=ALU.mult))
        V(lambda e: e.tensor_tensor(out=t2, in0=lbi, in1=lim, op=ALU.mult))
        V(lambda e: e.tensor_tensor(out=qr, in0=t1, in1=t2, op=ALU.add))
        V(lambda e: e.tensor_tensor(out=qr, in0=qr, in1=den, op=ALU.mult))
        V(lambda e: e.tensor_tensor(out=t1, in0=lbi, in1=lre, op=ALU.mult))
        V(lambda e: e.tensor_tensor(out=t2, in0=nr, in1=lim, op=ALU.mult))
        V(lambda e: e.tensor_tensor(out=qi, in0=t1, in1=t2, op=ALU.subtract))
        V(lambda e: e.tensor_tensor(out=qi, in0=qi, in1=den, op=ALU.mult))
        qrb = qr.unsqueeze(2).to_broadcast([64, 64, 16])
        qib = qi.unsqueeze(2).to_broadcast([64, 64, 16])
        V(lambda e: e.tensor_tensor(out=bbr, in0=br, in1=qrb, op=ALU.mult))
        V(lambda e: e.tensor_tensor(out=t3, in0=bi_, in1=qib, op=ALU.mult))
        V(lambda e: e.tensor_tensor(out=bbr, in0=bbr, in1=t3, op=ALU.subtract))
        V(lambda e: e.tensor_tensor(out=bbi, in0=bi_, in1=qrb, op=ALU.mult))
        V(lambda e: e.tensor_tensor(out=t3, in0=br, in1=qib, op=ALU.mult))
        V(lambda e: e.tensor_tensor(out=bbi, in0=bbi, in1=t3, op=ALU.add))
        V(lambda e: e.tensor_copy(out=bout[:, :, 0, :], in_=bbr.rearrange("g p c -> g c p")))
        V(lambda e: e.tensor_copy(out=bout[:, :, 1, :], in_=bbi.rearrange("g p c -> g c p")))
        P.op("sp", lambda e: e.dma_start(out=bb_s[l].rearrange("g c (r p) -> g c r p", r=2), in_=bout), reads=[R], writes=[R], dma=dd)
        lrb = lbr.unsqueeze(2).to_broadcast([64, 64, 16])
        lib = lbi.unsqueeze(2).to_broadcast([64, 64, 16])
        V(lambda e: e.tensor_tensor(out=br, in0=bbr, in1=lrb, op=ALU.mult))
        V(lambda e: e.tensor_tensor(out=t3, in0=bbi, in1=lib, op=ALU.mult))
        V(lambda e: e.tensor_tensor(out=br, in0=br, in1=t3, op=ALU.subtract))
        V(lambda e: e.tensor_tensor(out=bi_, in0=bbi, in1=lrb, op=ALU.mult))
        V(lambda e: e.tensor_tensor(out=t3, in0=bbr, in1=lib, op=ALU.mult))
        V(lambda e: e.tensor_tensor(out=bi_, in0=bi_, in1=t3, op=ALU.add))
        V(lambda e: e.tensor_copy(out=bout[:, :, 0, :], in_=br.rearrange("g p c -> g c p")))
        V(lambda e: e.tensor_copy(out=bout[:, :, 1, :], in_=bi_.rearrange("g p c -> g c p")))
        P.op("sp", lambda e: e.dma_start(out=bb_s1[l].rearrange("g c (r p) -> g c r p", r=2), in_=bout), reads=[R], writes=[R], dma=dd)
        V(lambda e: e.tensor_tensor(out=t1, in0=lbr, in1=lbr, op=ALU.mult))
        V(lambda e: e.tensor_tensor(out=t2, in0=lbi, in1=lbi, op=ALU.mult))
        V(lambda e: e.tensor_tensor(out=l2r, in0=t1, in1=t2, op=ALU.subtract))
        V(lambda e: e.tensor_tensor(out=t1, in0=lbr, in1=lbi, op=ALU.mult))
        V(lambda e: e.tensor_scalar(out=l2i, in0=t1, scalar1=2.0, scalar2=None, op0=ALU.mult))
        if dbg is not None and dbg[0] == "p":
            for i_, src_ in enumerate((lbr, lbi, qr, qi)):
                P.op("sp", lambda e, i_=i_, src_=src_: e.dma_start(out=dbg_l[:, i_ * 64:(i_ + 1) * 64], in_=src_), reads=[R], writes=[R], dma=dd)
            P.op("sp", lambda e: e.dma_start(out=dbg_bb, in_=bout), reads=[R], writes=[R], dma=dd)
        return l2r, l2i, R

    def s5_layer_setup(l):
        lbr, lbi, R = s5_prep(l)
        cw = carve(32768, [128, 2, 1024], BF16)
        ca = carve(36864, [128, 2, 32], F32)
        cb = carve(37120, [128, 2, 32], F32)
        l2 = [carve(38912, [64, 128], F32, 64), carve(39424, [64, 128], F32, 64)]
        R2 = Res()
        dd = next_misc()
        for src, idx in ((lbr, 0), (lbi, 1)):
            for hf in range(2):
                P.op("dve", lambda e, src=src, idx=idx, hf=hf: e.tensor_copy(out=l2[idx][:, hf * 64:(hf + 1) * 64], in_=src),
                     reads=[R], writes=[R])
            P.op("pe", lambda e, idx=idx: e.transpose(out=ps[6][:, idx * 64:(idx + 1) * 64], in_=l2[idx],
                                                      identity=identf[0:64, 0:64]),
                 reads=[R], writes=[psr[6]])
        for hv in range(2):
            hp_ = slice(64 * hv, 64 * hv + 64)
            gcol = slice(32 * hv, 32 * hv + 32)
            gcol2 = slice(64 + 32 * hv, 64 + 32 * hv + 32)
            P.op("dve", lambda e, hp_=hp_, gcol=gcol: e.tensor_copy(out=ca[hp_, 0, :], in_=ps[6][hp_, gcol]), reads=[psr[6]], writes=[R2])
            P.op("dve", lambda e, hp_=hp_, gcol=gcol: e.tensor_copy(out=ca[hp_, 1, :], in_=ps[6][hp_, gcol]), reads=[psr[6]], writes=[R2])
            P.op("dve", lambda e, hp_=hp_, gcol2=gcol2: e.tensor_copy(out=cb[hp_, 1, :], in_=ps[6][hp_, gcol2]), reads=[psr[6]], writes=[R2])
            P.op("dve", lambda e, hp_=hp_, gcol2=gcol2: e.tensor_scalar(out=cb[hp_, 0, :], in0=ps[6][hp_, gcol2], scalar1=-1.0,
                                                                    scalar2=None, op0=ALU.mult), reads=[psr[6]], writes=[R2])
        P.barrier()
        bbpad = [carve(0, [128, 64, 128], BF16), carve(16384, [128, 64, 128], BF16)]
        cnat = carve(38912, [128, 8, 128], F32)
        for ti_, bsrc in enumerate((bb_s, bb_s1)):
            P.op("pool", lambda e, ti_=ti_: e.memset(bbpad[ti_], 0.0), writes=[R2])
            for g in range(64):
                gl = g % 8
                P.op("sp", lambda e, g=g, gl=gl, ti_=ti_, bsrc=bsrc: e.dma_start(out=bbpad[ti_][16 * gl:16 * gl + 16, g, :], in_=bsrc[l, g]),
                     reads=[R, R2], writes=[R2], dma=dd)
        for idx, cd, sgn in ((0, cre_d, 1.0), (1, cim_d, -1.0)):
            for hf in range(2):
                P.op("sp", lambda e, cd=cd, hf=hf: e.dma_start(out=cnat[:, :, hf * 64:(hf + 1) * 64],
                                                              in_=cd[l].rearrange("(a q) p -> q a p", q=128)),
                     reads=[R2], writes=[R2], dma=dd)
            for half in range(2):
                for j in range(4):
                    a = half * 4 + j
                    P.op("pe", lambda e, a=a, j=j: e.transpose(out=ps[5][:, j * 128:(j + 1) * 128], in_=cnat[:, a, :],
                                                                identity=identf[:]), reads=[R2], writes=[psr[5]])
                P.op("dve", lambda e, idx=idx, half=half, sgn=sgn: e.tensor_scalar(
                    out=cw[:, idx, half * 512:(half + 1) * 512], in0=ps[5][:, :], scalar1=sgn, scalar2=None,
                    op0=ALU.mult), reads=[psr[5]], writes=[R2])
        P.barrier()
        return dict(bbpad=bbpad, cw=cw, ca=ca, cb=cb, R=R2)

    TC = 32

    def s5_mixer(l, b, L):
        R2 = L["R"]
        bbpad, cw, ca, cb = L["bbpad"], L["cw"], L["ca"], L["cb"]
        bigf = big[:, :].bitcast(F32)
        Vbs = [bigf[:, i * 2048:(i + 1) * 2048].rearrange("p (r g t) -> p r g t", r=2, g=32) for i in range(2)]
        Xb = bigf[:, 4096:5120].bitcast(BF16).rearrange("p (r g t) -> p r g t", r=2, g=32)
        ytok = bigf[0:TC, 5120:6144]
        zp = bigf[:, 6144:6400].rearrange("p (k t) -> p k t", t=TC)
        g1 = bigf[:, 6400:6656].rearrange("p (k t) -> p k t", t=TC)
        g2 = bigf[:, 6656:6912].rearrange("p (k t) -> p k t", t=TC)
        Zst = carve(37376, [128, 2, 32, 2], F32)
        m1 = carve(37888, [128, 2, 32, 2], F32)
        m2 = carve(38400, [128, 2, 32, 2], F32)
        rVs = [Res(), Res()]
        rX = Res(); rY = Res(); rZ = Res(); rZs = Res(); rZ2 = Res()
        rm1 = Res(); rm2a = Res(); rm2b = Res()
        r_ul = Res()
        P.op("dve", lambda e: e.memset(Zst, 0.0), writes=[rZs])
        nck = S // TC
        per_bank = 512 // (2 * TC)

        def inproj(ck):
            Vb = Vbs[ck % 2]
            rV = rVs[ck % 2]
            t0 = ck * TC
            t4 = t0 // 512
            for gg0 in range(0, 32, per_bank):
                bk = rot["a"] % 5
                rot["a"] += 1
                for hv in range(2):
                    for j in range(per_bank):
                        g = 32 * hv + gg0 + j
                        for ri in range(2):
                            c_ = (j * 2 + ri) * TC
                            P.op("pe", lambda e, g=g, ri=ri, c_=c_, bk=bk, hv=hv, t0=t0: e.matmul(
                                ps[bk][64 * hv:64 * hv + 64, c_:c_ + TC], lhsT=bbpad[0][:, g, ri * 64:(ri + 1) * 64],
                                rhs=u[:, g // 8, t0:t0 + TC], start=True, stop=False),
                                reads=[R2, r_u[t4]], writes=[psr[bk]])
                            lo = 1 if ck == 0 else 0
                            P.op("pe", lambda e, g=g, ri=ri, c_=c_, bk=bk, hv=hv, t0=t0, lo=lo: e.matmul(
                                ps[bk][64 * hv:64 * hv + 64, c_ + lo:c_ + TC], lhsT=bbpad[1][:, g, ri * 64:(ri + 1) * 64],
                                rhs=u[:, g // 8, t0 - 1 + lo:t0 - 1 + TC], start=False, stop=True),
                                reads=[R2, r_u[t4], r_ul], writes=[psr[bk]])
                P.op("act", lambda e, bk=bk, gg0=gg0, Vb=Vb: e.activation(
                    out=Vb[:, :, gg0:gg0 + per_bank, :].rearrange("p r g t -> p g r t"),
                    in_=ps[bk][:, 0:per_bank * 2 * TC].rearrange("p (g r t) -> p g r t", r=2, t=TC), func=AF.Copy),
                    reads=[psr[bk]], writes=[rV])

        def scan(ck):
            Vb = Vbs[ck % 2]
            rV = rVs[ck % 2]
            for t in range(0, TC, 2):
                if t == 0:
                    prev = Zst[:, :, :, :]
                    pf = [Zst[:, 1, :, :], Zst[:, 0, :, :]]
                    rp = rZs
                else:
                    prev = Vb[:, :, :, t - 2:t]
                    pf = [Vb[:, 1, :, t - 2:t], Vb[:, 0, :, t - 2:t]]
                    rp = rV
                cur = Vb[:, :, :, t:t + 2]
                cab = ca.unsqueeze(3).to_broadcast([128, 2, 32, 2])
                cb0 = cb[:, 0, :].unsqueeze(2).to_broadcast([128, 32, 2])
                cb1 = cb[:, 1, :].unsqueeze(2).to_broadcast([128, 32, 2])
                P.op("dve", lambda e, prev=prev, cab=cab: e.tensor_tensor(out=m1, in0=prev, in1=cab, op=ALU.mult),
                     reads=[rp, R2], writes=[rm1])
                P.op("dve", lambda e, pf=pf, cb0=cb0: e.tensor_tensor(out=m2[:, 0, :, :], in0=pf[0], in1=cb0, op=ALU.mult),
                     reads=[rp, R2], writes=[rm2a])
                P.op("dve", lambda e, pf=pf, cb1=cb1: e.tensor_tensor(out=m2[:, 1, :, :], in0=pf[1], in1=cb1, op=ALU.mult),
                     reads=[rp, R2], writes=[rm2b])
                P.op("dve", lambda e, cur=cur: e.tensor_tensor(out=cur, in0=cur, in1=m1, op=ALU.add),
                     reads=[rm1, rV], writes=[rV])
                P.op("dve", lambda e, cur=cur: e.tensor_tensor(out=cur, in0=cur, in1=m2, op=ALU.add),
                     reads=[rm2a, rm2b, rV], writes=[rV])
            P.op("dve", lambda e, Vb=Vb: e.tensor_copy(out=Zst, in_=Vb[:, :, :, TC - 2:TC]), reads=[rV], writes=[rZs])

        def finalize(ck):
            Vb = Vbs[ck % 2]
            rV = rVs[ck % 2]
            tsl = slice(ck * TC, (ck + 1) * TC)
            t4 = ck * TC // 512
            P.op("pool", lambda e, Vb=Vb: e.tensor_copy(out=Xb, in_=Vb), reads=[rV], writes=[rX])
            for half in range(2):
                for gg in range(32):
                    g = half * 32 + gg
                    hs_ = slice(64 * half, 64 * half + 64)
                    for ri in range(2):
                        P.op("pe", lambda e, g=g, gg=gg, ri=ri, hs_=hs_: e.matmul(
                            ps[5][0:TC, gg * 16:(gg + 1) * 16], lhsT=Xb[hs_, ri, gg, :], rhs=cw[hs_, ri, g * 16:(g + 1) * 16],
                            start=(ri == 0), stop=(ri == 1)), reads=[rX, R2], writes=[psr[5]])
                P.op("act", lambda e, half=half: e.activation(out=ytok[:, half * 512:(half + 1) * 512], in_=ps[5][0:TC, :],
                                                              func=AF.Copy), reads=[psr[5]], writes=[rY])
            for kc in range(8):
                P.op("pe", lambda e, kc=kc: e.transpose(out=ps[6][:, kc * TC:(kc + 1) * TC], in_=ytok[:, kc * 128:(kc + 1) * 128],
                                                        identity=identf[0:TC, 0:TC]), reads=[rY], writes=[psr[6]])
            P.op("pool", lambda e, tsl=tsl: e.tensor_tensor(out=zp, in0=u[:, :, tsl],
                                                           in1=dsk[:, l, :].unsqueeze(2).to_broadcast([128, 8, TC]), op=ALU.mult),
                 reads=[r_u[t4]], writes=[rZ])
            P.op("act", lambda e: e.activation(out=g1, in_=ps[6][:, 0:8 * TC].rearrange("p (k t) -> p k t", t=TC), func=AF.Copy),
                 reads=[psr[6]], writes=[rZ])
            P.op("pool", lambda e: e.tensor_tensor(out=zp, in0=zp, in1=g1, op=ALU.add), reads=[rZ], writes=[rZ])
            P.op("pool", lambda e: e.tensor_tensor(out=g1, in0=zp, in1=zp, op=ALU.mult), reads=[rZ], writes=[rZ])
            P.op("pool", lambda e: e.tensor_scalar(out=g1, in0=g1, scalar1=0.044715, scalar2=1.0, op0=ALU.mult, op1=ALU.add),
                 reads=[rZ], writes=[rZ])
            P.op("pool", lambda e: e.tensor_tensor(out=g1, in0=g1, in1=zp, op=ALU.mult), reads=[rZ], writes=[rZ])
            P.op("pool", lambda e: e.tensor_scalar(out=g1, in0=g1, scalar1=-30.0, scalar2=None, op0=ALU.max), reads=[rZ], writes=[rZ])
            P.op("act", lambda e: e.activation(out=g2, in_=g1, func=AF.Sigmoid, scale=1.5957691216057308),
                 reads=[rZ], writes=[rZ])
            P.op("pool", lambda e, tsl=tsl: e.tensor_tensor(out=u[:, :, tsl], in0=zp, in1=g2, op=ALU.mult),
                 reads=[rZ, r_u[t4]], writes=[rZ2, r_ul])

        for ck in range(nck):
            inproj(ck)
            if ck > 0:
                finalize(ck - 1)
            scan(ck)
            if ck % 3 == 2:
                pump(24)
        finalize(nck - 1)

    def glu(l, b):
        n = l * 2
        wv = wview(wglu_b[l])
        sg = [carve(0, [128, 512], F32), carve(2048, [128, 512], F32)]
        yv = [carve(4096, [128, 512], F32), carve(6144, [128, 512], F32)]
        r_sg = [Res(), Res()]
        r_yv = [Res(), Res()]
        for t in range(4):
            for c0 in (0, 512):
                pend = {}

                def epi(t_, oc, p_ap, p_res, pend=pend, t=t):
                    if oc < 8:
                        pend[oc] = (p_ap, p_res)
                        if dbg is not None and dbg[0] in ("gi", "gn", "gs"):
                            P.op("dve", lambda e, p_ap=p_ap, oc=oc, t=t: e.tensor_copy(out=h[:, oc, T(t)], in_=p_ap[:]),
                                 reads=[p_res, r_h[t]], writes=[r_h[t]])
                        return
                    if dbg is not None and dbg[0] in ("gi", "gn", "gs"):
                        return
                    ov = oc - 8
                    i = ov % 2
                    vp, vr = pend[ov]
                    if dbg is not None and dbg[0] in ("gv", "gg"):
                        src_, sr_ = (vp, vr) if dbg[0] == "gv" else (p_ap, p_res)
                        P.op("dve", lambda e, src_=src_, ov=ov, t=t: e.tensor_copy(out=h[:, ov, T(t)], in_=src_[:]),
                             reads=[sr_, vr, p_res, r_h[t]], writes=[r_h[t]])
                        return
                    P.op("act", lambda e, i=i, p_ap=p_ap: e.activation(out=sg[i], in_=p_ap[:], func=AF.Sigmoid),
                         reads=[p_res], writes=[r_sg[i]])
                    P.op("dve", lambda e, i=i, vp=vp: e.tensor_tensor(out=yv[i], in0=vp[:], in1=sg[i], op=ALU.mult),
                         reads=[vr, r_sg[i]], writes=[r_yv[i]])
                    P.op("dve", lambda e, i=i, ov=ov, t=t: e.scalar_tensor_tensor(
                        out=h[:, ov, T(t)], in0=yv[i], scalar=mod_gate(n, ov, b), in1=h[:, ov, T(t)],
                        op0=ALU.mult, op1=ALU.add), reads=[r_yv[i], r_mod, r_h[t]], writes=[r_h[t]])
                for sub in (0, 256):
                    gemm_fm(u, r_u, 8, wv, [[c0 + sub, c0 + sub + 128]], epi, tiles=[t], wr=wready(("wglu", l)))
                    gemm_fm(u, r_u, 8, wv, [[1024 + c0 + sub, 1024 + c0 + sub + 128]], epi, tiles=[t], wr=wready(("wglu", l)))

    def kv_phase(b):
        norm(8, b, u, r_u)
        wv = wview(wkv_b)
        stg = [carve(i * 1024, [128, 512], BF16) for i in range(4)]
        r_stg = [Res() for _ in range(4)]
        d_stg = [P.dsem() for _ in range(4)]
        cnt = [0]

        def epi(t, oc, p_ap, p_res):
            i = cnt[0] % 4
            cnt[0] += 1
            P.op("act", lambda e, i=i, p_ap=p_ap: e.activation(out=stg[i], in_=p_ap[:], func=AF.Copy),
                 reads=[p_res], writes=[r_stg[i]])
            me = P.op("sp", lambda e, i=i, oc=oc, t=t: e.dma_start(out=kt_s[b, oc, :, T(t)], in_=stg[i]),
                      reads=[r_stg[i]], dma=d_stg[i])
            r_stg[i].r.append(me)
            r_kv.w = me if r_kv.w is None or True else r_kv.w
            kv_w.append(me)
        gemm_fm(u, r_u, 8, wv, [[c0 + 128 * j for j in range(4)] for c0 in range(0, 3072, 512)], epi, wr=wready(("wkv", 0)))
        for c0 in range(3072, 6144, 512):
            si = load_slab(wv[:, 0:8, c0:c0 + 512], wready(("wkv", 0)))
            for tb in range(16):
                bk = rot["a"] % 6
                rot["a"] += 1
                for kc in range(8):
                    P.op("pe", lambda e, si=si, kc=kc, tb=tb, bk=bk: e.matmul(
                        ps[bk][:], lhsT=u[:, kc, tb * 128:(tb + 1) * 128], rhs=slabs[si][:, kc, :],
                        start=(kc == 0), stop=(kc == 7)), reads=[r_slab[si], r_u[tb // 4]], writes=[psr[bk]])
                i = cnt[0] % 4
                cnt[0] += 1
                P.op("act", lambda e, i=i, bk=bk: e.activation(out=stg[i], in_=ps[bk][:], func=AF.Copy),
                     reads=[psr[bk]], writes=[r_stg[i]])
                me = P.op("sp", lambda e, i=i, tb=tb, c0=c0: e.dma_start(
                    out=v_s[b, tb * 128:(tb + 1) * 128, c0 - 3072:c0 - 3072 + 512], in_=stg[i]),
                    reads=[r_stg[i]], dma=d_stg[i])
                r_stg[i].r.append(me)
                kv_w.append(me)

    r_kv = Res()
    kv_w = []
    DIL = (1, 4, 16)

    def sl_(start, n, step):
        return slice(start, start + (n - 1) * step + 1, step)

    def attention(l, b):
        j_ = l - 2
        n = l * 2
        qT = carve(0, [128, 3, S], BF16)
        kT = carve(12288, [128, 3, S], BF16)
        Vt = carve(24576, [128, 3, 16, 128], BF16)
        pT = [carve(36864, [128, 512], BF16), carve(37888, [128, 512], BF16)]
        rec = carve(38912, [128, 1024], F32)
        oT = big[:, :].rearrange("p (a b) -> p a b", b=S)
        r_q = Res(); r_k = Res(); r_v = Res(); r_p = [Res(), Res()]; r_rec = Res(); r_qs = Res(); r_o = Res()
        d_k = next_misc(); d_v = next_misc(); d_q = next_misc()
        wqv = wview(wq_b[j_])
        pcount = [0]
        for hp in range(8):
            for br in range(3):
                P.op("sp", lambda e, br=br, hp=hp: e.dma_start(out=kT[:, br, :], in_=kt_s[b, br * 8 + hp]),
                     reads=[r_kv], writes=[r_k], dma=d_k)
                d = DIL[br]
                vsrc = v_s[b].rearrange("(n p r) f -> p r n f", p=128, r=d)[:, :, :, br * 1024 + hp * 128: br * 1024 + hp * 128 + 128]
                for r in range(d):
                    nb_ = 16 // d
                    P.op("sp", lambda e, br=br, r=r, nb_=nb_, vsrc=vsrc: e.dma_start(
                        out=Vt[:, br, r * nb_:(r + 1) * nb_, :], in_=vsrc[:, r, :, :]),
                        reads=[r_kv], writes=[r_v], dma=d_v)
            for br in range(3):
                c0 = br * 1024 + hp * 128
                si = load_slab(wqv[:, :, c0:c0 + 128], wready(("wq", j_)))
                for t in range(4):
                    bk = rot["a"] % 4
                    rot["a"] += 1
                    for kc in range(8):
                        P.op("pe", lambda e, kc=kc, t=t, bk=bk, si=si: e.matmul(ps[bk][:], lhsT=slabs[si][:, kc, 0:128], rhs=u[:, kc, T(t)],
                                                                         start=(kc == 0), stop=(kc == 7)),
                             reads=[r_slab[si], r_u[t]], writes=[psr[bk]])
                    P.op("act", lambda e, br=br, t=t, bk=bk: e.activation(out=qT[:, br, T(t)], in_=ps[bk][:], func=AF.Copy,
                                                                          scale=0.125), reads=[psr[bk]], writes=[r_q])
            for qh in range(2):
                NB = (4, 5)
                DB = (6, 7)
                first = {}
                groups = []
                for hh in range(2):
                    hs = slice(64 * hh, 64 * hh + 64)
                    tl = []
                    for nn in range(8):
                        nblk = qh * 8 + nn
                        for kb, msk in ((nblk - 1, mprev), (nblk, mcur)):
                            if kb < 0:
                                continue
                            tl.append((0, slice(kb * 128, kb * 128 + 128), slice(nblk * 128, nblk * 128 + 128),
                                       msk[:, :], kb, nn // 4, slice((nn % 4) * 128, (nn % 4) * 128 + 128), 128))
                    for r in range(4):
                        for nn in range(2):
                            nblk = qh * 2 + nn
                            for kb, msk in ((nblk - 1, mprev), (nblk, mcur)):
                                if kb < 0:
                                    continue
                                tl.append((1, sl_(r + 512 * kb, 128, 4),
                                           sl_(r + 512 * nblk, 128, 4), msk[:, :], r * 4 + kb,
                                           nn, sl_(r, 128, 4), 128))
                    for r in range(16):
                        for mm in range(2):
                            m_ = qh * 2 + mm
                            tl.append((2, sl_(r, 128, 16), sl_(r + 512 * m_, 32, 16),
                                       mcur[:, 32 * m_:32 * m_ + 32], r, mm, sl_(r, 32, 16), 32))
                    i0 = 0
                    while i0 < len(tl):
                        grp = []
                        w_ = 0
                        while i0 < len(tl) and w_ + tl[i0][7] <= 512:
                            grp.append((tl[i0], w_))
                            w_ += tl[i0][7]
                            i0 += 1
                        groups.append((grp, w_, hh, hs))
                base_ = pcount[0]
                pcount[0] += len(groups)

                def emit_S(gi):
                    grp, w_, hh, hs = groups[gi]
                    bk = (base_ + gi) % 4
                    for ii_, ((br, kap, qap, mk, vb, ob, oc_, nc_), off) in enumerate(grp):
                        P.op("pe", lambda e, br=br, kap=kap, qap=qap, off=off, nc_=nc_, bk=bk, hs=hs, ii_=ii_: e.matmul(
                            ps[bk][:, off:off + nc_], lhsT=kT[hs, br, kap], rhs=qT[hs, br, qap], start=(ii_ == 0), stop=False,
                            skip_group_check=True),
                            reads=[r_k, r_q], writes=[psr[bk]])
                    for ii_, ((br, kap, qap, mk, vb, ob, oc_, nc_), off) in enumerate(grp):
                        P.op("pe", lambda e, mk=mk, off=off, nc_=nc_, bk=bk, ii_=ii_, ng_=len(grp): e.matmul(
                            ps[bk][:, off:off + nc_], lhsT=identb[:], rhs=mk, start=False, stop=(ii_ == ng_ - 1), skip_group_check=True),
                            writes=[psr[bk]])

                def emit_PV(gi):
                    grp, w_, hh, hs = groups[gi]
                    bk = (base_ + gi) % 4
                    pi = (base_ + gi) % 2
                    P.op("act", lambda e, bk=bk, pi=pi, w_=w_: e.activation(out=pT[pi][:, 0:w_], in_=ps[bk][:, 0:w_], func=AF.Exp),
                         reads=[psr[bk]], writes=[r_p[pi]])
                    for (br, kap, qap, mk, vb, ob, oc_, nc_), off in grp:
                        for (bank, lhs) in ((NB[ob], Vt[:, br, vb, hs]), (DB[ob], onesb[:, 0:64])):
                            key = (bank, hh)
                            st_ = key not in first
                            first[key] = 1
                            P.op("pe", lambda e, bank=bank, lhs=lhs, oc_=oc_, pi=pi, off=off, nc_=nc_, st_=st_, hs=hs: e.matmul(
                                ps[bank][hs, oc_], lhsT=lhs, rhs=pT[pi][:, off:off + nc_], start=st_, stop=False,
                                skip_group_check=True),
                                reads=[r_p[pi], r_v], writes=[psr[bank]])

                emit_S(0)
                for gi in range(len(groups)):
                    if gi + 1 < len(groups):
                        emit_S(gi + 1)
                    emit_PV(gi)
                for ob in range(2):
                    P.op("dve", lambda e, ob=ob: e.reciprocal(out=rec[:, ob * 512:(ob + 1) * 512], in_=ps[DB[ob]][:]),
                         reads=[psr[DB[ob]]], writes=[r_rec])
                    P.op("dve", lambda e, ob=ob, hp=hp, qh=qh: e.tensor_tensor(
                        out=oT[:, hp, qh * 1024 + ob * 512: qh * 1024 + (ob + 1) * 512], in0=ps[NB[ob]][:],
                        in1=rec[:, ob * 512:(ob + 1) * 512], op=ALU.mult),
                        reads=[psr[NB[ob]], r_rec], writes=[r_o])
        rot["a"] = 0
        gemm_fm(oT, r_o, 8, wview(wo_b[j_]), [[c0 + 128 * j for j in range(4)] for c0 in (0, 512)], resid_epi(n, b),
                wr=wready(("wo", j_)))

    step = [0]

    def done():
        step[0] += 1
        return step[0] >= stop_after

    def u_to_h():
        P.barrier()
        for t in range(4):
            for kc in range(8):
                P.op("act", lambda e, t=t, kc=kc: e.activation(out=h[:, kc, T(t)], in_=u[:, kc, T(t)], func=AF.Copy),
                     reads=[r_u[t]], writes=[r_h[t]])

    for b in range(nseq):
        P.barrier()
        load_x(b)
        stopped = False
        for l in range(4):
            if l == 2:
                P.barrier()
                kv_w.clear()
                kv_phase(b)
                r_kv.w = None
                P.barrier()
            P.barrier()
            if dbg == ("w", l):
                wtmp = big[:, :].rearrange("p (a b) -> p a b", b=2048)
                rw_ = Res()
                P.op("sp", lambda e: e.dma_start(out=wtmp, in_=wview(wglu_b[0])), reads=[wready(("wglu", 0))], writes=[rw_], dma=d_tok[1])
                P.op("sp", lambda e: e.dma_start(out=dbg_w.rearrange("(kc p) n -> p kc n", p=128), in_=wtmp), reads=[rw_], writes=[rw_], dma=d_tok[1])
                stopped = True
                break
            if dbg != ("m", l):
                norm(l * 2, b, u, r_u)
            if dbg == ("u", l):
                u_to_h()
                stopped = True
                break
            if dbg == ("m", l):
                pass
            elif dbg == ("gs", l):
                L = s5_layer_setup(l)
                s5_mixer(l, b, L)
                P.barrier()
                norm(l * 2, b, u, r_u)
                P.barrier()
                glu(l, b)
                stopped = True
                break
            elif dbg == ("gn", l):
                glu(l, b)
                stopped = True
                break
            elif l < 2:
                L = s5_layer_setup(l)
                if dbg == ("p", l):
                    P.barrier()
                    stopped = True
                    break
                s5_mixer(l, b, L)
                P.barrier()
                if dbg is not None and len(dbg) > 2 and dbg[2] == "fix":
                    norm(l * 2, b, big[:, :].rearrange("p (a b) -> p a b", b=S), [Res() for _ in range(4)])
                    P.barrier()
                if dbg is not None and dbg[:2] == ("pu", l):
                    for t_ in range(4):
                        for kc_ in range(8):
                            bk_ = (t_ * 8 + kc_) % 4
                            P.op("pe", lambda e, t_=t_, kc_=kc_, bk_=bk_: e.matmul(ps[bk_][:], lhsT=identb[:], rhs=u[:, kc_, T(t_)],
                                                                                   start=True, stop=True), reads=[r_u[t_]], writes=[psr[bk_]])
                            P.op("dve", lambda e, t_=t_, kc_=kc_, bk_=bk_: e.tensor_copy(out=h[:, kc_, T(t_)], in_=ps[bk_][:]),
                                 reads=[psr[bk_], r_h[t_]], writes=[r_h[t_]])
                    stopped = True
                    break
                if dbg is not None and dbg[:2] in (("z", l), ("y", l)):
                    u_to_h()
                    stopped = True
                    break
                glu(l, b)
            else:
                attention(l, b)
            if done():
                stopped = True
                break
            P.barrier()
            norm(l * 2 + 1, b, u, r_u)
            mlp(l, b)
            if done():
                stopped = True
                break
        step[0] = 0
        P.barrier()
        store_out(b, final=not stopped)
    P.barrier()
    P.op("sp", lambda e: e.dma_start(out=tok[0][0:1, 0:8], in_=x_d[0, 0:1, 0:8]), dma=d_tok[0])
    P.emit([(d_tok[0], P.cnt[d_tok[0]])] + [(k, P.cnt[k]) for k in d_otok])
    st.close()
    return nc


def _bf16(a):
    import ml_dtypes
    return np.asarray(a, dtype=np.float32).astype(ml_dtypes.bfloat16)


def make_in_maps(inp):
    f = lambda a: np.ascontiguousarray(np.asarray(a, dtype=np.float32))
    x = f(inp["x"]); c = f(inp["c"])
    adaw = np.concatenate([f(inp["ada_w"]).reshape(8, D, 3072)[i] for i in range(8)] + [f(inp["kv_ada_w"])], axis=1)
    adab_flat = np.concatenate([f(inp["ada_b"]).reshape(-1), f(inp["kv_ada_b"])])
    adab = np.ascontiguousarray(adab_flat.reshape(NCH, 128).T)
    lng_all = np.concatenate([f(inp["ln_g"]).reshape(8, D), f(inp["kv_g"]).reshape(1, D)], axis=0)
    lng = np.ascontiguousarray(lng_all.reshape(9, 8, 128).transpose(2, 0, 1))
    fing = np.ascontiguousarray(np.broadcast_to(f(inp["final_g"])[None, :], (128, D)))
    dsk = np.ascontiguousarray(f(inp["ssm_d"]).reshape(2, 8, 128).transpose(2, 0, 1))
    kk = np.arange(128)[:, None]; qq = np.arange(128)[None, :]
    mcur = np.where(kk <= qq, 0.0, -30000.0).astype(np.float32)
    mprev = np.where(kk >= qq, 0.0, -30000.0).astype(np.float32)
    common = dict(
        adaw=np.ascontiguousarray(adaw), adab=adab, lng=lng, fing=fing, dsk=dsk,
        lamre=f(inp["ssm_lam_re"]), lamim=f(inp["ssm_lam_im"]), logdt=f(inp["ssm_log_dt"]).reshape(2, 64, 1),
        bre=f(inp["ssm_b_re"]).reshape(2, 64, 1024), bim=f(inp["ssm_b_im"]).reshape(2, 64, 1024),
        cre=f(inp["ssm_c_re"]).reshape(2, 1024, 64), cim=f(inp["ssm_c_im"]).reshape(2, 1024, 64),
        identf=np.eye(128, dtype=np.float32), identb=_bf16(np.eye(128)), mcur=_bf16(mcur), mprev=_bf16(mprev),
        w1=f(inp["mlp_w1"]), w2=f(inp["mlp_w2"]), wglu=f(inp["ssm_w_glu"]), wkv=f(inp["w_kv"]),
        wq=f(inp["attn_w_q"]), wo=f(inp["attn_w_o"]),
    )
    maps = []
    for i in range(8):
        m = dict(common)
        m["x"] = np.ascontiguousarray(x[2 * i:2 * i + 2])
        m["cT"] = np.ascontiguousarray(c[2 * i:2 * i + 2].T.reshape(8, 128, 2).transpose(1, 0, 2))
        maps.append(m)
    return maps


def kernel(**inputs):
    nc = build_program()
    maps = make_in_maps(inputs)
    res = run_bass_kernel_spmd(nc, maps, core_ids=list(range(8)))
    return np.concatenate([r["out"] for r in res.results], axis=0).astype(np.float32)
```

```python
import contextlib
import numpy as np
import concourse.bass as bass
import concourse.mybir as mybir
from concourse.bass_utils import run_bass_kernel_spmd

F32 = mybir.dt.float32
BF16 = mybir.dt.bfloat16
I32 = mybir.dt.int32
AF = mybir.ActivationFunctionType
ALU = mybir.AluOpType

ENGS = ("pe", "act", "dve", "pool", "sp")
S = 2048
D = 1024
NMOD = 26624
NCH = NMOD // 128
EPS = 1e-6


class Res:
    __slots__ = ("w", "r")

    def __init__(self):
        self.w = None
        self.r = []


class Prog:
    def __init__(self, nc):
        self.nc = nc
        self.ops = {e: [] for e in ENGS}
        self.cnt = {}
        self.seen = {e: {} for e in ENGS}
        self.sems = {}
        self.nd = 0
        self.bar = None
        self.bar_done = {e: None for e in ENGS}

    def dsem(self):
        k = "d%d" % self.nd
        self.nd += 1
        self.cnt[k] = 0
        return k

    def barrier(self):
        self.bar = dict(self.cnt)

    def op(self, eng, fn, reads=(), writes=(), dma=None, after=()):
        waits = {}
        for k_, v_ in after:
            waits[k_] = v_

        def need(dep, war=False):
            if dep is None:
                return
            k, v = dep
            if k == eng and (eng == "pe" or (war and eng != "pool")):
                return
            if waits.get(k, 0) < v:
                waits[k] = v

        if self.bar is not None and self.bar_done[eng] is not self.bar:
            self.bar_done[eng] = self.bar
            for k, v in self.bar.items():
                if v > 0 and k != eng:
                    waits[k] = v
        for r in reads:
            need(r.w)
        for w in writes:
            need(w.w)
            for d in w.r:
                need(d, war=True)
        final = []
        seen = self.seen[eng]
        for k, v in waits.items():
            if seen.get(k, 0) >= v:
                continue
            seen[k] = v
            final.append((k, v))
        key, inc = (eng, 1) if dma is None else (dma, 16)
        self.cnt[key] = self.cnt.get(key, 0) + inc
        me = (key, self.cnt[key])
        self.ops[eng].append((final, fn, key, inc))
        for r in reads:
            r.r.append(me)
            if len(r.r) > 64:
                r.r = _compress(r.r)
        for w in writes:
            w.w = me
            w.r = []
        return me

    def emit(self, final_waits):
        nc = self.nc
        with contextlib.ExitStack() as st:
            for k in list(self.cnt.keys()):
                self.sems[k] = st.enter_context(nc.semaphore("s_" + k))
            block = st.enter_context(nc.Block())
            sems = self.sems

            def run(engname, e):
                for waits, fn, key, inc in self.ops[engname]:
                    for k, v in waits:
                        e.wait_ge(sems[k], v)
                    fn(e).then_inc(sems[key], inc)
                if engname == "sp":
                    for k, v in final_waits:
                        e.wait_ge(sems[k], v)

            @block.tensor
            def _(e):
                run("pe", e)

            @block.scalar
            def _(e):
                run("act", e)

            @block.vector
            def _(e):
                run("dve", e)

            @block.gpsimd
            def _(e):
                run("pool", e)

            @block.sync
            def _(e):
                run("sp", e)


def _compress(lst):
    best = {}
    for k, v in lst:
        if best.get(k, 0) < v:
            best[k] = v
    return list(best.items())


def build_program(stop_after=99, dbg=None, nseq=2):
    nc = bass.Bass("TRN2", target_bir_lowering=False)
    P = Prog(nc)

    def din(name, shape, dt=F32):
        return nc.dram_tensor(name, list(shape), dt, kind="ExternalInput").ap()

    def dscr(name, shape, dt=BF16):
        return nc.dram_tensor(name, list(shape), dt, kind="Internal").ap()

    x_d = din("x", [2, S, D])
    cT_d = din("cT", [128, 8, 2])
    adaw_d = din("adaw", [D, NMOD])
    adab_d = din("adab", [128, NCH])
    lng_d = din("lng", [128, 9, 8])
    fing_d = din("fing", [128, D])
    dsk_d = din("dsk", [128, 2, 8])
    lamre_d = din("lamre", [2, 64, 64])
    lamim_d = din("lamim", [2, 64, 64])
    logdt_d = din("logdt", [2, 64, 1])
    bre_d = din("bre", [2, 64, 1024])
    bim_d = din("bim", [2, 64, 1024])
    cre_d = din("cre", [2, 1024, 64])
    cim_d = din("cim", [2, 1024, 64])
    identf_d = din("identf", [128, 128])
    identb_d = din("identb", [128, 128], BF16)
    mcur_d = din("mcur", [128, 128], BF16)
    mprev_d = din("mprev", [128, 128], BF16)
    w1_d = din("w1", [4, D, 4096])
    w2_d = din("w2", [4, 4096, D])
    wglu_d = din("wglu", [2, D, 2048])
    wkv_d = din("wkv", [D, 6144])
    wq_d = din("wq", [2, D, 3072])
    wo_d = din("wo", [2, D, D])
    out_d = nc.dram_tensor("out", [2, S, D], F32, kind="ExternalOutput").ap()
    if dbg is not None and dbg[0] == "w":
        dbg_w = nc.dram_tensor("dbg_w", [D, 2048], BF16, kind="ExternalOutput").ap()
    if dbg is not None and dbg[0] == "p":
        dbg_l = nc.dram_tensor("dbg_l", [64, 256], F32, kind="ExternalOutput").ap()
        dbg_bb = nc.dram_tensor("dbg_bb", [64, 16, 2, 64], BF16, kind="ExternalOutput").ap()

    w1_b = dscr("w1b", [4, D, 4096])
    w2_b = dscr("w2b", [4, 4096, D])
    wglu_b = dscr("wglub", [2, D, 2048])
    wkv_b = dscr("wkvb", [D, 6144])
    wq_b = dscr("wqb", [2, D, 3072])
    wo_b = dscr("wob", [2, D, D])
    kt_s = dscr("kts", [2, 24, 128, S])
    v_s = dscr("vs", [2, S, 3072])
    bb_s = dscr("bbs", [2, 64, 16, 128])
    bb_s1 = dscr("bbs1", [2, 64, 16, 128])

    st = contextlib.ExitStack()

    def sb(name, shape, dt):
        return st.enter_context(nc.sbuf_tensor(name, list(shape), dt))

    h = sb("h", [128, 8, S], F32)
    u = sb("u", [128, 8, S], BF16)
    big = sb("big", [128, 16384], BF16)
    slabs = [sb("slab%d" % i, [128, 8, 512], BF16) for i in range(3)]
    sq = [sb("sq%d" % i, [128, 512], BF16) for i in range(2)]
    rs = sb("rs", [128, 512], F32)
    rstd = rs
    tmpf = [sb("tmpf%d" % i, [128, 512], F32) for i in range(2)]
    identf = sb("identf_s", [128, 128], F32)
    identb = sb("identb_s", [128, 128], BF16)
    onesb = sb("onesb", [128, 128], BF16)
    mcur = sb("mcur_s", [128, 128], BF16)
    mprev = sb("mprev_s", [128, 128], BF16)
    mod = sb("mod", [128, NCH, 2], F32)
    adab = sb("adab_s", [128, NCH], F32)
    lng = sb("lng_s", [128, 9, 8], F32)
    Gm = sb("Gm", [128, 9, 8, 2], F32)
    dsk = sb("dsk_s", [128, 2, 8], F32)
    cT = sb("cT_s", [128, 8, 2], F32)
    scT = sb("scT", [128, 8, 2], F32)
    arena = sb("arena", [128, 10752], F32)

    def carve(off, shape, dt, parts=128):
        nel = int(np.prod(shape[1:]))
        nb = nel * (2 if dt == BF16 else 4)
        assert off % 4 == 0 and off + nb <= 10752 * 4, (off, nb)
        ap = arena[0:parts, off // 4:(off + nb + 3) // 4]
        if dt == BF16:
            ap = ap.bitcast(BF16)
        elif dt == I32:
            ap = ap.bitcast(I32)
        if len(shape) == 3:
            ap = ap.rearrange("p (a b) -> p a b", b=shape[2])
        elif len(shape) == 4:
            ap = ap.rearrange("p (a b c) -> p a b c", b=shape[2], c=shape[3])
        return ap

    ps = [st.enter_context(nc.psum_tensor("ps%d" % i, [128, 512], F32)) for i in range(8)]
    psr = [Res() for _ in range(8)]

    r_h = [Res() for _ in range(4)]
    r_u = [Res() for _ in range(4)]
    r_big = Res()
    r_slab = [Res() for _ in range(3)]
    d_slab = [P.dsem() for _ in range(3)]
    r_sq = [Res(), Res()]
    r_rs = Res()
    r_rstd = Res()
    r_tmpf = [Res(), Res()]
    r_const = Res()
    r_mod = Res()
    d_const = P.dsem()
    d_constp = P.dsem()
    d_misc = [P.dsem() for _ in range(8)]
    misc_i = [0]
    slab_i = [0]
    rot = {"a": 0}

    def next_misc():
        k = d_misc[misc_i[0] % len(d_misc)]
        misc_i[0] += 1
        return k

    for (dst, src) in [(identf, identf_d), (adab, adab_d), (lng, lng_d), (dsk, dsk_d), (cT, cT_d)]:
        P.op("sp", lambda e, dst=dst, src=src: e.dma_start(out=dst[:], in_=src), writes=[r_const], dma=d_const)
    for (dst, src) in [(identb, identb_d), (mcur, mcur_d), (mprev, mprev_d)]:
        P.op("pool", lambda e, dst=dst, src=src: e.dma_start(out=dst[:], in_=src), writes=[r_const], dma=d_constp)
    r_const.w = (d_const, P.cnt[d_const])
    P.op("dve", lambda e: e.memset(onesb[:], 1.0), writes=[r_const])
    r_const.w = None
    P.barrier()
    scTb = carve(40960, [128, 8, 2], BF16)
    P.op("act", lambda e: e.activation(out=scTb, in_=cT[:], func=AF.Silu), writes=[r_const])
    aslab = [carve(i * 8192, [128, 8, 512], BF16) for i in range(4)]
    r_aslab = [Res() for _ in range(4)]
    d_aslab = [P.dsem() for _ in range(4)]
    adaw_v = adaw_d.rearrange("(kc p) n -> p kc n", p=128)
    nsl = NMOD // 512
    for s_ in range(nsl):
        bi = s_ % 4
        P.op("pool", lambda e, s_=s_, bi=bi: e.dma_start(out=aslab[bi], in_=adaw_v[:, :, s_ * 512:(s_ + 1) * 512]),
             writes=[r_aslab[bi]], dma=d_aslab[bi])
        for j in range(4):
            ch = s_ * 4 + j
            for kc in range(8):
                P.op("pe", lambda e, bi=bi, j=j, kc=kc, ch=ch: e.matmul(
                    ps[7][:, 2 * ch:2 * ch + 2], lhsT=aslab[bi][:, kc, j * 128:(j + 1) * 128], rhs=scTb[:, kc, :],
                    start=(kc == 0), stop=(kc == 7)), reads=[r_aslab[bi], r_const], writes=[psr[7]])

    d_castAB = [P.dsem(), P.dsem()]
    cast_jobs = []
    cast_pos = [0]
    grp_end = {}
    key_last = {}
    wres = {}

    def add_cast(key, dst, src, rows, cols):
        for r0 in range(0, rows, 128):
            for c0 in range(0, cols, 1024):
                cast_jobs.append((key, dst, src, r0, c0, min(cols, c0 + 1024)))
        key_last[key] = len(cast_jobs) - 1

    for l in range(2):
        add_cast(("wglu", l), wglu_b[l], wglu_d[l], D, 2048)
        add_cast(("w1", l), w1_b[l], w1_d[l], D, 4096)
        add_cast(("w2", l), w2_b[l], w2_d[l], 4096, D)
    add_cast(("wkv", 0), wkv_b, wkv_d, D, 6144)
    for l in range(2):
        add_cast(("wq", l), wq_b[l], wq_d[l], D, 3072)
        add_cast(("wo", l), wo_b[l], wo_d[l], D, D)
        add_cast(("w1", l + 2), w1_b[l + 2], w1_d[l + 2], D, 4096)
        add_cast(("w2", l + 2), w2_b[l + 2], w2_d[l + 2], 4096, D)

    def pump(n):
        for _ in range(n):
            i = cast_pos[0]
            if i >= len(cast_jobs):
                return
            key, dst, src, r0, c0, c1 = cast_jobs[i]
            g_ = i // 24
            dk = d_castAB[g_ % 2]
            aft = [(dk, P.cnt[dk])] if (i % 24 == 0 and g_ >= 2) else []
            cast_pos[0] += 1
            me = P.op("pool", lambda e, dst=dst, src=src, r0=r0, c0=c0, c1=c1: e.dma_start(
                out=dst[r0:r0 + 128, c0:c1], in_=src[r0:r0 + 128, c0:c1], max_dma_last_dim=4096),
                dma=dk, after=aft)
            grp_end[g_] = me

    def wready(key):
        if key not in wres:
            last = key_last[key]
            tgt = min(len(cast_jobs), (last // 24 + 1) * 24)
            if cast_pos[0] < tgt:
                pump(tgt - cast_pos[0])
            r = Res()
            r.w = grp_end[last // 24]
            wres[key] = r
        return wres[key]

    P.op("dve", lambda e: e.tensor_tensor(out=mod[:], in0=ps[7][:, 0:2 * NCH].rearrange("p (c b) -> p c b", b=2),
                                          in1=adab[:].unsqueeze(2).to_broadcast([128, NCH, 2]), op=ALU.add),
         reads=[psr[7]], writes=[r_mod])
    for n in range(9):
        base = (n * 3072 + 1024) // 128 if n < 8 else (24576 + 1024) // 128
        P.op("dve", lambda e, n=n, base=base: e.scalar_tensor_tensor(
            out=Gm[:, n, :, :], in0=mod[:, base:base + 8, :], scalar=1.0,
            in1=lng[:, n, :].unsqueeze(2).to_broadcast([128, 8, 2]), op0=ALU.add, op1=ALU.mult),
            reads=[r_mod], writes=[r_mod])

    def mod_shift(n, kc, b):
        base = (n * 3072) // 128 if n < 8 else 24576 // 128
        return mod[:, base + kc, b:b + 1]

    def mod_gate(n, kc, b):
        base = (n * 3072 + 2048) // 128
        return mod[:, base + kc, b:b + 1]

    def T(t):
        return slice(t * 512, (t + 1) * 512)

    def norm(n, b, out_buf, r_out):
        for t in range(4):
            for kc in range(8):
                i = kc % 2
                P.op("act", lambda e, kc=kc, i=i, t=t: e.activation(out=sq[i][:], in_=h[:, kc, T(t)], func=AF.Square),
                     reads=[r_h[t]], writes=[r_sq[i]])
                P.op("pe", lambda e, kc=kc, i=i: e.matmul(ps[6][:], lhsT=onesb[:], rhs=sq[i][:],
                                                          start=(kc == 0), stop=(kc == 7)),
                     reads=[r_sq[i]], writes=[psr[6]])
            P.op("act", lambda e: e.activation(out=rs[:], in_=ps[6][:], func=AF.Sqrt, bias=EPS, scale=1.0 / D),
                 reads=[psr[6]], writes=[r_rs, r_rstd])
            P.op("dve", lambda e: e.reciprocal(out=rstd[:], in_=rs[:]), reads=[r_rs], writes=[r_rs, r_rstd])
            for kc in range(8):
                i = kc % 2
                P.op("pool", lambda e, kc=kc, i=i, t=t: e.tensor_tensor(out=tmpf[i][:], in0=h[:, kc, T(t)], in1=rstd[:],
                                                                         op=ALU.mult),
                     reads=[r_h[t], r_rstd], writes=[r_tmpf[i]])
                P.op("act", lambda e, kc=kc, i=i, t=t: e.activation(
                    out=out_buf[:, kc, T(t)], in_=tmpf[i][:], func=AF.Identity,
                    bias=mod_shift(n, kc, b), scale=Gm[:, n, kc, b:b + 1]),
                    reads=[r_tmpf[i], r_mod], writes=[r_out[t]])

    def load_slab(view, wr):
        i = slab_i[0] % 3
        slab_i[0] += 1
        P.op("sp", lambda e, i=i, view=view: e.dma_start(out=slabs[i][:, 0:view.shape[1], 0:view.shape[2]], in_=view),
             reads=[wr], writes=[r_slab[i]], dma=d_slab[i])
        return i

    def gemm_fm(src, r_src, KC, wv, colgroups, epi, tiles=range(4), wr=None):
        for t in tiles:
            for grp in colgroups:
                c0 = grp[0]
                banks = []
                for _ in grp:
                    banks.append(rot["a"] % 6)
                    rot["a"] += 1
                for kg in range(KC // 8):
                    si = load_slab(wv[:, kg * 8:(kg + 1) * 8, c0:grp[-1] + 128], wr)
                    for j, c in enumerate(grp):
                        for kc in range(8):
                            kk = kg * 8 + kc
                            P.op("pe", lambda e, si=si, j=j, c=c, kc=kc, kk=kk, bk=banks[j], t=t, c0=c0: e.matmul(
                                ps[bk][:], lhsT=slabs[si][:, kc, c - c0:c - c0 + 128], rhs=src[:, kk, T(t)],
                                start=(kk == 0), stop=(kk == KC - 1)),
                                reads=[r_slab[si], r_src[t] if isinstance(r_src, list) else r_src], writes=[psr[banks[j]]])
                for j, c in enumerate(grp):
                    epi(t, c // 128, ps[banks[j]], psr[banks[j]])

    def wview(w2d):
        return w2d.rearrange("(kc p) n -> p kc n", p=128)

    def resid_epi(n, b):
        def epi(t, oc, p_ap, p_res):
            P.op("dve", lambda e, t=t, oc=oc, p_ap=p_ap: e.scalar_tensor_tensor(
                out=h[:, oc, T(t)], in0=p_ap[:], scalar=mod_gate(n, oc, b), in1=h[:, oc, T(t)],
                op0=ALU.mult, op1=ALU.add), reads=[p_res, r_mod, r_h[t]], writes=[r_h[t]])
        return epi

    hid = big[:, :].rearrange("p (a b) -> p a b", b=512)
    r_hid = Res()
    relu_t = [carve(0, [128, 512], BF16), carve(1024, [128, 512], BF16)]
    r_relu = [Res(), Res()]

    def mlp(l, b):
        n = l * 2 + 1
        w1v = wview(w1_b[l])
        w2v = wview(w2_b[l])
        for t in range(4):
            def epi1(t_, hc, p_ap, p_res):
                i = hc % 2
                P.op("act", lambda e, i=i, p_ap=p_ap: e.activation(out=relu_t[i], in_=p_ap[:], func=AF.Relu),
                     reads=[p_res], writes=[r_relu[i]])
                P.op("pool", lambda e, i=i, hc=hc: e.tensor_tensor(out=hid[:, hc, :], in0=relu_t[i], in1=relu_t[i],
                                                                    op=ALU.mult),
                     reads=[r_relu[i]], writes=[r_hid])
            gemm_fm(u, r_u, 8, w1v, [[c0 + 128 * j for j in range(4)] for c0 in range(0, 4096, 512)], epi1, tiles=[t],
                    wr=wready(("w1", l)))

            class HidSrc:
                def __getitem__(self, idx):
                    return hid[idx[0], idx[1], :]
            gemm_fm(HidSrc(), r_hid, 32, w2v, [[c0 + 128 * j for j in range(4)] for c0 in (0, 512)],
                    lambda t_, oc, p_ap, p_res, t=t: resid_epi(n, b)(t, oc, p_ap, p_res), tiles=[t], wr=wready(("w2", l)))

    tok = [carve(0, [128, D], F32), carve(4096, [128, D], F32)]
    r_tok = [Res(), Res()]
    d_tok = [P.dsem(), P.dsem()]
    otok = [carve(8192, [128, D], F32), carve(12288, [128, D], F32)]
    r_otok = [Res(), Res()]
    d_otok = [P.dsem(), P.dsem()]
    ss1 = carve(16384, [128, 1], F32)
    ss2 = carve(16400, [128, 1], F32)
    ss3 = carve(16416, [128, 1], F32)
    junk = carve(16448, [128, D], F32)
    fing = carve(20544, [128, D], F32)
    r_ss = Res()
    last_out = []

    def load_x(b):
        for tb in range(16):
            i = tb % 2
            P.op("sp", lambda e, tb=tb, i=i: e.dma_start(out=tok[i], in_=x_d[b, tb * 128:(tb + 1) * 128, :]),
                 writes=[r_tok[i]], dma=d_tok[i])
            for half in range(2):
                bk = rot["a"] % 6
                rot["a"] += 1
                for j in range(4):
                    kc = half * 4 + j
                    P.op("pe", lambda e, i=i, kc=kc, j=j, bk=bk: e.transpose(
                        out=ps[bk][:, j * 128:(j + 1) * 128], in_=tok[i][:, kc * 128:(kc + 1) * 128], identity=identf[:]),
                        reads=[r_tok[i]], writes=[psr[bk]])
                P.op("act", lambda e, half=half, bk=bk, tb=tb: e.activation(
                    out=h[:, half * 4:half * 4 + 4, tb * 128:(tb + 1) * 128],
                    in_=ps[bk][:].rearrange("p (a b) -> p a b", b=128), func=AF.Copy),
                    reads=[psr[bk]], writes=[r_h[tb // 4]])

    def store_out(b, final=True):
        P.op("sp", lambda e: e.dma_start(out=fing, in_=fing_d), writes=[r_ss], dma=d_tok[0])
        for tb in range(16):
            i = tb % 2
            for half in range(2):
                for j in range(4):
                    kc = half * 4 + j
                    P.op("pe", lambda e, kc=kc, j=j, half=half, tb=tb: e.transpose(
                        out=ps[half][:, j * 128:(j + 1) * 128], in_=h[:, kc, tb * 128:(tb + 1) * 128], identity=identf[:]),
                        reads=[r_h[tb // 4]], writes=[psr[half]])
            if final:
                for half in range(2):
                    P.op("act", lambda e, half=half: e.activation(
                        out=junk[:, half * 512:(half + 1) * 512], in_=ps[half][:], func=AF.Square,
                        accum_out=(ss1 if half == 0 else ss2)), reads=[psr[half]], writes=[r_ss])
                P.op("dve", lambda e: e.tensor_tensor(out=ss3, in0=ss1, in1=ss2, op=ALU.add), reads=[r_ss], writes=[r_ss])
                P.op("act", lambda e: e.activation(out=ss1, in_=ss3, func=AF.Sqrt, bias=EPS, scale=1.0 / D),
                     reads=[r_ss], writes=[r_ss])
                P.op("dve", lambda e: e.reciprocal(out=ss2, in_=ss1), reads=[r_ss], writes=[r_ss])
                for half in range(2):
                    P.op("dve", lambda e, half=half, i=i: e.scalar_tensor_tensor(
                        out=otok[i][:, half * 512:(half + 1) * 512], in0=ps[half][:], scalar=ss2,
                        in1=fing[:, half * 512:(half + 1) * 512], op0=ALU.mult, op1=ALU.mult),
                        reads=[psr[half], r_ss], writes=[r_otok[i]])
            else:
                for half in range(2):
                    P.op("act", lambda e, half=half, i=i: e.activation(
                        out=otok[i][:, half * 512:(half + 1) * 512], in_=ps[half][:], func=AF.Copy),
                        reads=[psr[half]], writes=[r_otok[i]])
            me = P.op("sp", lambda e, i=i, tb=tb: e.dma_start(out=out_d[b, tb * 128:(tb + 1) * 128, :], in_=otok[i]),
                      reads=[r_otok[i]], dma=d_otok[i])
            r_otok[i].r.append(me)
            last_out.append(me)

    def s5_prep(l):
        P.barrier()
        o = [0]

        def A(shape, dt=F32, parts=64):
            nel = int(np.prod(shape[1:]))
            nb = ((nel * (2 if dt == BF16 else 4) + 3) // 4) * 4
            ap = carve(o[0], shape, dt, parts)
            o[0] += nb
            return ap
        lre = A([64, 64]); lim = A([64, 64]); ldt = A([64, 1]); dt_ = A([64, 1])
        a_ = A([64, 64]); th = A([64, 64]); ea = A([64, 64]); yy = A([64, 64]); ki = A([64, 64], I32)
        kf = A([64, 64]); ff = A([64, 64]); sn = A([64, 64]); cs = A([64, 64])
        lbr = A([64, 64]); lbi = A([64, 64]); den = A([64, 64]); nr = A([64, 64]); ni = A([64, 64])
        qr = A([64, 64]); qi = A([64, 64]); t1 = A([64, 64]); t2 = A([64, 64])
        br = A([64, 64, 16]); bi_ = A([64, 64, 16])
        bbr = A([64, 64, 16]); bbi = A([64, 64, 16]); t3 = A([64, 64, 16])
        bout = A([64, 16, 2, 64], BF16)
        l2r = A([64, 64]); l2i = A([64, 64])
        R = Res()
        dd = next_misc()
        for dst, src in [(lre, lamre_d[l]), (lim, lamim_d[l]), (ldt, logdt_d[l])]:
            P.op("sp", lambda e, dst=dst, src=src: e.dma_start(out=dst, in_=src), writes=[R], dma=dd)
        P.op("sp", lambda e: e.dma_start(out=br, in_=bre_d[l].rearrange("g (p c) -> g p c", c=16)), writes=[R], dma=dd)
        P.op("sp", lambda e: e.dma_start(out=bi_, in_=bim_d[l].rearrange("g (p c) -> g p c", c=16)), writes=[R], dma=dd)

        def V(fn):
            P.op("dve", fn, reads=[R], writes=[R])

        def ACT(fn):
            P.op("act", fn, reads=[R], writes=[R])
        ACT(lambda e: e.activation(out=dt_, in_=ldt, func=AF.Exp))
        V(lambda e: e.tensor_scalar(out=a_, in0=lre, scalar1=dt_, scalar2=None, op0=ALU.mult))
        V(lambda e: e.tensor_scalar(out=th, in0=lim, scalar1=dt_, scalar2=None, op0=ALU.mult))
        ACT(lambda e: e.activation(out=ea, in_=a_, func=AF.Exp))

        def sin_of(dst, offs):
            V(lambda e: e.tensor_scalar(out=yy, in0=th, scalar1=1.0 / (2 * np.pi), scalar2=offs, op0=ALU.mult, op1=ALU.add))
            V(lambda e: e.tensor_copy(out=ki, in_=yy))
            V(lambda e: e.tensor_copy(out=kf, in_=ki))
            V(lambda e: e.tensor_tensor(out=ff, in0=yy, in1=kf, op=ALU.subtract))
            V(lambda e: e.scalar_tensor_tensor(out=ff, in0=ff, scalar=0.0, in1=ff, op0=ALU.is_lt, op1=ALU.add))
            V(lambda e: e.tensor_scalar(out=ff, in0=ff, scalar1=2 * np.pi, scalar2=-np.pi, op0=ALU.mult, op1=ALU.add))
            V(lambda e: e.tensor_scalar(out=ff, in0=ff, scalar1=-3.14159, scalar2=3.14159, op0=ALU.max, op1=ALU.min))
            ACT(lambda e: e.activation(out=dst, in_=ff, func=AF.Sin))
        sin_of(sn, 0.5)
        sin_of(cs, 0.75)
        V(lambda e: e.tensor_tensor(out=lbr, in0=ea, in1=cs, op=ALU.mult))
        V(lambda e: e.tensor_tensor(out=lbi, in0=ea, in1=sn, op=ALU.mult))
        V(lambda e: e.tensor_scalar(out=nr, in0=lbr, scalar1=-1.0, scalar2=None, op0=ALU.add))
        V(lambda e: e.tensor_tensor(out=t1, in0=lre, in1=lre, op=ALU.mult))
        V(lambda e: e.tensor_tensor(out=t2, in0=lim, in1=lim, op=ALU.mult))
        V(lambda e: e.tensor_tensor(out=den, in0=t1, in1=t2, op=ALU.add))
        V(lambda e: e.reciprocal(out=den, in_=den))
        V(lambda e: e.tensor_tensor(out=t1, in0=nr, in1=lre, op=ALU.mult))
        V(lambda e: e.tensor_tensor(out=t2, in0=lbi, in1=lim, op=ALU.mult))
        V(lambda e: e.tensor_tensor(out=qr, in0=t1, in1=t2, op=ALU.add))
        V(lambda e: e.tensor_tensor(out=qr, in0=qr, in1=den, op=ALU.mult))
        V(lambda e: e.tensor_tensor(out=t1, in0=lbi, in1=lre, op=ALU.mult))
        V(lambda e: e.tensor_tensor(out=t2, in0=nr, in1=lim, op=ALU.mult))
        V(lambda e: e.tensor_tensor(out=qi, in0=t1, in1=t2, op=ALU.subtract))
        V(lambda e: e.tensor_tensor(out=qi, in0=qi, in1=den, op=ALU.mult))
        qrb = qr.unsqueeze(2).to_broadcast([64, 64, 16])
        qib = qi.unsqueeze(2).to_broadcast([64, 64, 16])
        V(lambda e: e.tensor_tensor(out=bbr, in0=br, in1=qrb, op=ALU.mult))
        V(lambda e: e.tensor_tensor(out=t3, in0=bi_, in1=qib, op=ALU.mult))
        V(lambda e: e.tensor_tensor(out=bbr, in0=bbr, in1=t3, op=ALU.subtract))
        V(lambda e: e.tensor_tensor(out=bbi, in0=bi_, in1=qrb, op=ALU.mult))
        V(lambda e: e.tensor_tensor(out=t3, in0=br, in1=qib, op=ALU.mult))
        V(lambda e: e.tensor_tensor(out=bbi, in0=bbi, in1=t3, op=ALU.add))
        V(lambda e: e.tensor_copy(out=bout[:, :, 0, :], in_=bbr.rearrange("g p c -> g c p")))
        V(lambda e: e.tensor_copy(out=bout[:, :, 1, :], in_=bbi.rearrange("g p c -> g c p")))
        P.op("sp", lambda e: e.dma_start(out=bb_s[l].rearrange("g c (r p) -> g c r p", r=2), in_=bout), reads=[R], writes=[R], dma=dd)
        lrb = lbr.unsqueeze(2).to_broadcast([64, 64, 16])
        lib = lbi.unsqueeze(2).to_broadcast([64, 64, 16])
        V(lambda e: e.tensor_tensor(out=br, in0=bbr, in1=lrb, op=ALU.mult))
        V(lambda e: e.tensor_tensor(out=t3, in0=bbi, in1=lib, op=ALU.mult))
        V(lambda e: e.tensor_tensor(out=br, in0=br, in1=t3, op=ALU.subtract))
        V(lambda e: e.tensor_tensor(out=bi_, in0=bbi, in1=lrb, op=ALU.mult))
        V(lambda e: e.tensor_tensor(out=t3, in0=bbr, in1=lib, op=ALU.mult))
        V(lambda e: e.tensor_tensor(out=bi_, in0=bi_, in1=t3, op=ALU.add))
        V(lambda e: e.tensor_copy(out=bout[:, :, 0, :], in_=br.rearrange("g p c -> g c p")))
        V(lambda e: e.tensor_copy(out=bout[:, :, 1, :], in_=bi_.rearrange("g p c -> g c p")))
        P.op("sp", lambda e: e.dma_start(out=bb_s1[l].rearrange("g c (r p) -> g c r p", r=2), in_=bout), reads=[R], writes=[R], dma=dd)
        V(lambda e: e.tensor_tensor(out=t1, in0=lbr, in1=lbr, op=ALU.mult))
        V(lambda e: e.tensor_tensor(out=t2, in0=lbi, in1=lbi, op=ALU.mult))
        V(lambda e: e.tensor_tensor(out=l2r, in0=t1, in1=t2, op=ALU.subtract))
        V(lambda e: e.tensor_tensor(out=t1, in0=lbr, in1=lbi, op=ALU.mult))
        V(lambda e: e.tensor_scalar(out=l2i, in0=t1, scalar1=2.0, scalar2=None, op0=ALU.mult))
        if dbg is not None and dbg[0] == "p":
            for i_, src_ in enumerate((lbr, lbi, qr, qi)):
                P.op("sp", lambda e, i_=i_, src_=src_: e.dma_start(out=dbg_l[:, i_ * 64:(i_ + 1) * 64], in_=src_), reads=[R], writes=[R], dma=dd)
            P.op("sp", lambda e: e.dma_start(out=dbg_bb, in_=bout), reads=[R], writes=[R], dma=dd)
        return l2r, l2i, R

    def s5_layer_setup(l):
        lbr, lbi, R = s5_prep(l)
        cw = carve(32768, [128, 2, 1024], BF16)
        ca = carve(36864, [128, 2, 32], F32)
        cb = carve(37120, [128, 2, 32], F32)
        l2 = [carve(38912, [64, 128], F32, 64), carve(39424, [64, 128], F32, 64)]
        R2 = Res()
        dd = next_misc()
        for src, idx in ((lbr, 0), (lbi, 1)):
            for hf in range(2):
                P.op("dve", lambda e, src=src, idx=idx, hf=hf: e.tensor_copy(out=l2[idx][:, hf * 64:(hf + 1) * 64], in_=src),
                     reads=[R], writes=[R])
            P.op("pe", lambda e, idx=idx: e.transpose(out=ps[6][:, idx * 64:(idx + 1) * 64], in_=l2[idx],
                                                      identity=identf[0:64, 0:64]),
                 reads=[R], writes=[psr[6]])
        for hv in range(2):
            hp_ = slice(64 * hv, 64 * hv + 64)
            gcol = slice(32 * hv, 32 * hv + 32)
            gcol2 = slice(64 + 32 * hv, 64 + 32 * hv + 32)
            P.op("dve", lambda e, hp_=hp_, gcol=gcol: e.tensor_copy(out=ca[hp_, 0, :], in_=ps[6][hp_, gcol]), reads=[psr[6]], writes=[R2])
            P.op("dve", lambda e, hp_=hp_, gcol=gcol: e.tensor_copy(out=ca[hp_, 1, :], in_=ps[6][hp_, gcol]), reads=[psr[6]], writes=[R2])
            P.op("dve", lambda e, hp_=hp_, gcol2=gcol2: e.tensor_copy(out=cb[hp_, 1, :], in_=ps[6][hp_, gcol2]), reads=[psr[6]], writes=[R2])
            P.op("dve", lambda e, hp_=hp_, gcol2=gcol2: e.tensor_scalar(out=cb[hp_, 0, :], in0=ps[6][hp_, gcol2], scalar1=-1.0,
                                                                    scalar2=None, op0=ALU.mult), reads=[psr[6]], writes=[R2])
        P.barrier()
        bbpad = [carve(0, [128, 64, 128], BF16), carve(16384, [128, 64, 128], BF16)]
        cnat = carve(38912, [128, 8, 128], F32)
        for ti_, bsrc in enumerate((bb_s, bb_s1)):
            P.op("pool", lambda e, ti_=ti_: e.memset(bbpad[ti_], 0.0), writes=[R2])
            for g in range(64):
                gl = g % 8
                P.op("sp", lambda e, g=g, gl=gl, ti_=ti_, bsrc=bsrc: e.dma_start(out=bbpad[ti_][16 * gl:16 * gl + 16, g, :], in_=bsrc[l, g]),
                     reads=[R, R2], writes=[R2], dma=dd)
        for idx, cd, sgn in ((0, cre_d, 1.0), (1, cim_d, -1.0)):
            for hf in range(2):
                P.op("sp", lambda e, cd=cd, hf=hf: e.dma_start(out=cnat[:, :, hf * 64:(hf + 1) * 64],
                                                              in_=cd[l].rearrange("(a q) p -> q a p", q=128)),
                     reads=[R2], writes=[R2], dma=dd)
            for half in range(2):
                for j in range(4):
                    a = half * 4 + j
                    P.op("pe", lambda e, a=a, j=j: e.transpose(out=ps[5][:, j * 128:(j + 1) * 128], in_=cnat[:, a, :],
                                                                identity=identf[:]), reads=[R2], writes=[psr[5]])
                P.op("dve", lambda e, idx=idx, half=half, sgn=sgn: e.tensor_scalar(
                    out=cw[:, idx, half * 512:(half + 1) * 512], in0=ps[5][:, :], scalar1=sgn, scalar2=None,
                    op0=ALU.mult), reads=[psr[5]], writes=[R2])
        P.barrier()
        return dict(bbpad=bbpad, cw=cw, ca=ca, cb=cb, R=R2)

    TC = 32

    def s5_mixer(l, b, L):
        R2 = L["R"]
        bbpad, cw, ca, cb = L["bbpad"], L["cw"], L["ca"], L["cb"]
        bigf = big[:, :].bitcast(F32)
        Vbs = [bigf[:, i * 2048:(i + 1) * 2048].rearrange("p (r g t) -> p r g t", r=2, g=32) for i in range(2)]
        Xb = bigf[:, 4096:5120].bitcast(BF16).rearrange("p (r g t) -> p r g t", r=2, g=32)
        ytok = bigf[0:TC, 5120:6144]
        zp = bigf[:, 6144:6400].rearrange("p (k t) -> p k t", t=TC)
        g1 = bigf[:, 6400:6656].rearrange("p (k t) -> p k t", t=TC)
        g2 = bigf[:, 6656:6912].rearrange("p (k t) -> p k t", t=TC)
        Zst = carve(37376, [128, 2, 32, 2], F32)
        m1 = carve(37888, [128, 2, 32, 2], F32)
        m2 = carve(38400, [128, 2, 32, 2], F32)
        rVs = [Res(), Res()]
        rX = Res(); rY = Res(); rZ = Res(); rZs = Res(); rZ2 = Res()
        rm1 = Res(); rm2a = Res(); rm2b = Res()
        r_ul = Res()
        P.op("dve", lambda e: e.memset(Zst, 0.0), writes=[rZs])
        nck = S // TC
        per_bank = 512 // (2 * TC)

        def inproj(ck):
            Vb = Vbs[ck % 2]
            rV = rVs[ck % 2]
            t0 = ck * TC
            t4 = t0 // 512
            for gg0 in range(0, 32, per_bank):
                bk = rot["a"] % 5
                rot["a"] += 1
                for hv in range(2):
                    for j in range(per_bank):
                        g = 32 * hv + gg0 + j
                        for ri in range(2):
                            c_ = (j * 2 + ri) * TC
                            P.op("pe", lambda e, g=g, ri=ri, c_=c_, bk=bk, hv=hv, t0=t0: e.matmul(
                                ps[bk][64 * hv:64 * hv + 64, c_:c_ + TC], lhsT=bbpad[0][:, g, ri * 64:(ri + 1) * 64],
                                rhs=u[:, g // 8, t0:t0 + TC], start=True, stop=False),
                                reads=[R2, r_u[t4]], writes=[psr[bk]])
                            lo = 1 if ck == 0 else 0
                            P.op("pe", lambda e, g=g, ri=ri, c_=c_, bk=bk, hv=hv, t0=t0, lo=lo: e.matmul(
                                ps[bk][64 * hv:64 * hv + 64, c_ + lo:c_ + TC], lhsT=bbpad[1][:, g, ri * 64:(ri + 1) * 64],
                                rhs=u[:, g // 8, t0 - 1 + lo:t0 - 1 + TC], start=False, stop=True),
                                reads=[R2, r_u[t4], r_ul], writes=[psr[bk]])
                P.op("act", lambda e, bk=bk, gg0=gg0, Vb=Vb: e.activation(
                    out=Vb[:, :, gg0:gg0 + per_bank, :].rearrange("p r g t -> p g r t"),
                    in_=ps[bk][:, 0:per_bank * 2 * TC].rearrange("p (g r t) -> p g r t", r=2, t=TC), func=AF.Copy),
                    reads=[psr[bk]], writes=[rV])

        def scan(ck):
            Vb = Vbs[ck % 2]
            rV = rVs[ck % 2]
            for t in range(0, TC, 2):
                if t == 0:
                    prev = Zst[:, :, :, :]
                    pf = [Zst[:, 1, :, :], Zst[:, 0, :, :]]
                    rp = rZs
                else:
                    prev = Vb[:, :, :, t - 2:t]
                    pf = [Vb[:, 1, :, t - 2:t], Vb[:, 0, :, t - 2:t]]
                    rp = rV
                cur = Vb[:, :, :, t:t + 2]
                cab = ca.unsqueeze(3).to_broadcast([128, 2, 32, 2])
                cb0 = cb[:, 0, :].unsqueeze(2).to_broadcast([128, 32, 2])
                cb1 = cb[:, 1, :].unsqueeze(2).to_broadcast([128, 32, 2])
                P.op("dve", lambda e, prev=prev, cab=cab: e.tensor_tensor(out=m1, in0=prev, in1=cab, op=ALU.mult),
                     reads=[rp, R2], writes=[rm1])
                P.op("dve", lambda e, pf=pf, cb0=cb0: e.tensor_tensor(out=m2[:, 0, :, :], in0=pf[0], in1=cb0, op=ALU.mult),
                     reads=[rp, R2], writes=[rm2a])
                P.op("dve", lambda e, pf=pf, cb1=cb1: e.tensor_tensor(out=m2[:, 1, :, :], in0=pf[1], in1=cb1, op=ALU.mult),
                     reads=[rp, R2], writes=[rm2b])
                P.op("dve", lambda e, cur=cur: e.tensor_tensor(out=cur, in0=cur, in1=m1, op=ALU.add),
                     reads=[rm1, rV], writes=[rV])
                P.op("dve", lambda e, cur=cur: e.tensor_tensor(out=cur, in0=cur, in1=m2, op=ALU.add),
                     reads=[rm2a, rm2b, rV], writes=[rV])
            P.op("dve", lambda e, Vb=Vb: e.tensor_copy(out=Zst, in_=Vb[:, :, :, TC - 2:TC]), reads=[rV], writes=[rZs])

        def finalize(ck):
            Vb = Vbs[ck % 2]
            rV = rVs[ck % 2]
            tsl = slice(ck * TC, (ck + 1) * TC)
            t4 = ck * TC // 512
            P.op("pool", lambda e, Vb=Vb: e.tensor_copy(out=Xb, in_=Vb), reads=[rV], writes=[rX])
            for half in range(2):
                for gg in range(32):
                    g = half * 32 + gg
                    hs_ = slice(64 * half, 64 * half + 64)
                    for ri in range(2):
                        P.op("pe", lambda e, g=g, gg=gg, ri=ri, hs_=hs_: e.matmul(
                            ps[5][0:TC, gg * 16:(gg + 1) * 16], lhsT=Xb[hs_, ri, gg, :], rhs=cw[hs_, ri, g * 16:(g + 1) * 16],
                            start=(ri == 0), stop=(ri == 1)), reads=[rX, R2], writes=[psr[5]])
                P.op("act", lambda e, half=half: e.activation(out=ytok[:, half * 512:(half + 1) * 512], in_=ps[5][0:TC, :],
                                                              func=AF.Copy), reads=[psr[5]], writes=[rY])
            for kc in range(8):
                P.op("pe", lambda e, kc=kc: e.transpose(out=ps[6][:, kc * TC:(kc + 1) * TC], in_=ytok[:, kc * 128:(kc + 1) * 128],
                                                        identity=identf[0:TC, 0:TC]), reads=[rY], writes=[psr[6]])
            P.op("pool", lambda e, tsl=tsl: e.tensor_tensor(out=zp, in0=u[:, :, tsl],
                                                           in1=dsk[:, l, :].unsqueeze(2).to_broadcast([128, 8, TC]), op=ALU.mult),
                 reads=[r_u[t4]], writes=[rZ])
            P.op("act", lambda e: e.activation(out=g1, in_=ps[6][:, 0:8 * TC].rearrange("p (k t) -> p k t", t=TC), func=AF.Copy),
                 reads=[psr[6]], writes=[rZ])
            P.op("pool", lambda e: e.tensor_tensor(out=zp, in0=zp, in1=g1, op=ALU.add), reads=[rZ], writes=[rZ])
            P.op("pool", lambda e: e.tensor_tensor(out=g1, in0=zp, in1=zp, op=ALU.mult), reads=[rZ], writes=[rZ])
            P.op("pool", lambda e: e.tensor_scalar(out=g1, in0=g1, scalar1=0.044715, scalar2=1.0, op0=ALU.mult, op1=ALU.add),
                 reads=[rZ], writes=[rZ])
            P.op("pool", lambda e: e.tensor_tensor(out=g1, in0=g1, in1=zp, op=ALU.mult), reads=[rZ], writes=[rZ])
            P.op("pool", lambda e: e.tensor_scalar(out=g1, in0=g1, scalar1=-30.0, scalar2=None, op0=ALU.max), reads=[rZ], writes=[rZ])
            P.op("act", lambda e: e.activation(out=g2, in_=g1, func=AF.Sigmoid, scale=1.5957691216057308),
                 reads=[rZ], writes=[rZ])
            P.op("pool", lambda e, tsl=tsl: e.tensor_tensor(out=u[:, :, tsl], in0=zp, in1=g2, op=ALU.mult),
                 reads=[rZ, r_u[t4]], writes=[rZ2, r_ul])

        for ck in range(nck):
            inproj(ck)
            if ck > 0:
                finalize(ck - 1)
            scan(ck)
            if ck % 3 == 2:
                pump(24)
        finalize(nck - 1)

    def glu(l, b):
        n = l * 2
        wv = wview(wglu_b[l])
        sg = [carve(0, [128, 512], F32), carve(2048, [128, 512], F32)]
        yv = [carve(4096, [128, 512], F32), carve(6144, [128, 512], F32)]
        r_sg = [Res(), Res()]
        r_yv = [Res(), Res()]
        for t in range(4):
            for c0 in (0, 512):
                pend = {}

                def epi(t_, oc, p_ap, p_res, pend=pend, t=t):
                    if oc < 8:
                        pend[oc] = (p_ap, p_res)
                        if dbg is not None and dbg[0] in ("gi", "gn", "gs"):
                            P.op("dve", lambda e, p_ap=p_ap, oc=oc, t=t: e.tensor_copy(out=h[:, oc, T(t)], in_=p_ap[:]),
                                 reads=[p_res, r_h[t]], writes=[r_h[t]])
                        return
                    if dbg is not None and dbg[0] in ("gi", "gn", "gs"):
                        return
                    ov = oc - 8
                    i = ov % 2
                    vp, vr = pend[ov]
                    if dbg is not None and dbg[0] in ("gv", "gg"):
                        src_, sr_ = (vp, vr) if dbg[0] == "gv" else (p_ap, p_res)
                        P.op("dve", lambda e, src_=src_, ov=ov, t=t: e.tensor_copy(out=h[:, ov, T(t)], in_=src_[:]),
                             reads=[sr_, vr, p_res, r_h[t]], writes=[r_h[t]])
                        return
                    P.op("act", lambda e, i=i, p_ap=p_ap: e.activation(out=sg[i], in_=p_ap[:], func=AF.Sigmoid),
                         reads=[p_res], writes=[r_sg[i]])
                    P.op("dve", lambda e, i=i, vp=vp: e.tensor_tensor(out=yv[i], in0=vp[:], in1=sg[i], op=ALU.mult),
                         reads=[vr, r_sg[i]], writes=[r_yv[i]])
                    P.op("dve", lambda e, i=i, ov=ov, t=t: e.scalar_tensor_tensor(
                        out=h[:, ov, T(t)], in0=yv[i], scalar=mod_gate(n, ov, b), in1=h[:, ov, T(t)],
                        op0=ALU.mult, op1=ALU.add), reads=[r_yv[i], r_mod, r_h[t]], writes=[r_h[t]])
                for sub in (0, 256):
                    gemm_fm(u, r_u, 8, wv, [[c0 + sub, c0 + sub + 128]], epi, tiles=[t], wr=wready(("wglu", l)))
                    gemm_fm(u, r_u, 8, wv, [[1024 + c0 + sub, 1024 + c0 + sub + 128]], epi, tiles=[t], wr=wready(("wglu", l)))

    def kv_phase(b):
        norm(8, b, u, r_u)
        wv = wview(wkv_b)
        stg = [carve(i * 1024, [128, 512], BF16) for i in range(4)]
        r_stg = [Res() for _ in range(4)]
        d_stg = [P.dsem() for _ in range(4)]
        cnt = [0]

        def epi(t, oc, p_ap, p_res):
            i = cnt[0] % 4
            cnt[0] += 1
            P.op("act", lambda e, i=i, p_ap=p_ap: e.activation(out=stg[i], in_=p_ap[:], func=AF.Copy),
                 reads=[p_res], writes=[r_stg[i]])
            me = P.op("sp", lambda e, i=i, oc=oc, t=t: e.dma_start(out=kt_s[b, oc, :, T(t)], in_=stg[i]),
                      reads=[r_stg[i]], dma=d_stg[i])
            r_stg[i].r.append(me)
            r_kv.w = me if r_kv.w is None or True else r_kv.w
            kv_w.append(me)
        gemm_fm(u, r_u, 8, wv, [[c0 + 128 * j for j in range(4)] for c0 in range(0, 3072, 512)], epi, wr=wready(("wkv", 0)))
        for c0 in range(3072, 6144, 512):
            si = load_slab(wv[:, 0:8, c0:c0 + 512], wready(("wkv", 0)))
            for tb in range(16):
                bk = rot["a"] % 6
                rot["a"] += 1
                for kc in range(8):
                    P.op("pe", lambda e, si=si, kc=kc, tb=tb, bk=bk: e.matmul(
                        ps[bk][:], lhsT=u[:, kc, tb * 128:(tb + 1) * 128], rhs=slabs[si][:, kc, :],
                        start=(kc == 0), stop=(kc == 7)), reads=[r_slab[si], r_u[tb // 4]], writes=[psr[bk]])
                i = cnt[0] % 4
                cnt[0] += 1
                P.op("act", lambda e, i=i, bk=bk: e.activation(out=stg[i], in_=ps[bk][:], func=AF.Copy),
                     reads=[psr[bk]], writes=[r_stg[i]])
                me = P.op("sp", lambda e, i=i, tb=tb, c0=c0: e.dma_start(
                    out=v_s[b, tb * 128:(tb + 1) * 128, c0 - 3072:c0 - 3072 + 512], in_=stg[i]),
                    reads=[r_stg[i]], dma=d_stg[i])
                r_stg[i].r.append(me)
                kv_w.append(me)

    r_kv = Res()
    kv_w = []
    DIL = (1, 4, 16)

    def sl_(start, n, step):
        return slice(start, start + (n - 1) * step + 1, step)

    def attention(l, b):
        j_ = l - 2
        n = l * 2
        qT = carve(0, [128, 3, S], BF16)
        kT = carve(12288, [128, 3, S], BF16)
        Vt = carve(24576, [128, 3, 16, 128], BF16)
        pT = [carve(36864, [128, 512], BF16), carve(37888, [128, 512], BF16)]
        rec = carve(38912, [128, 1024], F32)
        oT = big[:, :].rearrange("p (a b) -> p a b", b=S)
        r_q = Res(); r_p = [Res(), Res()]; r_rec = Res(); r_o = Res()
        r_kk = [Res(), Res()]; r_vv = [Res(), Res()]
        d_kk = [P.dsem(), P.dsem()]; d_vv = [P.dsem(), P.dsem()]
        d_q = next_misc()
        wqv = wview(wq_b[j_])
        pcount = [0]
        P.barrier()
        sf = [slabs[i][:, :, :].rearrange("p a b -> p (a b)") for i in range(3)]
        Kbuf = [[kT[:, 0, :], kT[:, 1, :], kT[:, 2, :]],
                [sf[0][:, 0:2048], sf[0][:, 2048:4096], sf[1][:, 0:2048]]]
        Vbuf = [[Vt[:, 0, :, :], Vt[:, 1, :, :], Vt[:, 2, :, :]],
                [sf[1][:, 2048:4096].rearrange("p (n f) -> p n f", f=128),
                 sf[2][:, 0:2048].rearrange("p (n f) -> p n f", f=128),
                 sf[2][:, 2048:4096].rearrange("p (n f) -> p n f", f=128)]]
        qslab = tmpf[0][:, :].bitcast(BF16).rearrange("p (a b) -> p a b", b=128)
        r_qs = r_tmpf[0]

        def kv_loads(hp):
            bi_ = hp % 2
            for br in range(3):
                P.op("sp", lambda e, br=br, hp=hp, bi_=bi_: e.dma_start(out=Kbuf[bi_][br], in_=kt_s[b, br * 8 + hp]),
                     reads=[r_kv], writes=[r_kk[bi_]], dma=d_kk[bi_])
                d = DIL[br]
                vsrc = v_s[b].rearrange("(n p r) f -> p r n f", p=128, r=d)[:, :, :, br * 1024 + hp * 128: br * 1024 + hp * 128 + 128]
                for r in range(d):
                    nb_ = 16 // d
                    P.op("sp", lambda e, br=br, r=r, nb_=nb_, vsrc=vsrc, bi_=bi_: e.dma_start(
                        out=Vbuf[bi_][br][:, r * nb_:(r + 1) * nb_, :], in_=vsrc[:, r, :, :]),
                        reads=[r_kv], writes=[r_vv[bi_]], dma=d_vv[bi_])

        kv_loads(0)
        for hp in range(8):
            bi_ = hp % 2
            r_k = r_kk[bi_]
            r_v = r_vv[bi_]
            Kb = Kbuf[bi_]
            Vb_ = Vbuf[bi_]
            for br in range(3):
                c0 = br * 1024 + hp * 128
                P.op("sp", lambda e, c0=c0: e.dma_start(out=qslab, in_=wqv[:, :, c0:c0 + 128]),
                     reads=[wready(("wq", j_))], writes=[r_qs], dma=d_q)
                for t in range(4):
                    bk = rot["a"] % 4
                    rot["a"] += 1
                    for kc in range(8):
                        P.op("pe", lambda e, kc=kc, t=t, bk=bk: e.matmul(ps[bk][:], lhsT=qslab[:, kc, :], rhs=u[:, kc, T(t)],
                                                                         start=(kc == 0), stop=(kc == 7)),
                             reads=[r_qs, r_u[t]], writes=[psr[bk]])
                    P.op("act", lambda e, br=br, t=t, bk=bk: e.activation(out=qT[:, br, T(t)], in_=ps[bk][:], func=AF.Copy,
                                                                          scale=0.125), reads=[psr[bk]], writes=[r_q])
            if hp + 1 < 8:
                kv_loads(hp + 1)
            for qh in range(2):
                NB = (4, 5)
                DB = (6, 7)
                first = {}
                groups = []
                for hh in range(2):
                    hs = slice(64 * hh, 64 * hh + 64)
                    tl = []
                    for nn in range(8):
                        nblk = qh * 8 + nn
                        for kb, msk in ((nblk - 1, mprev), (nblk, mcur)):
                            if kb < 0:
                                continue
                            tl.append((0, slice(kb * 128, kb * 128 + 128), slice(nblk * 128, nblk * 128 + 128),
                                       msk[:, :], kb, nn // 4, slice((nn % 4) * 128, (nn % 4) * 128 + 128), 128))
                    for r in range(4):
                        for nn in range(2):
                            nblk = qh * 2 + nn
                            for kb, msk in ((nblk - 1, mprev), (nblk, mcur)):
                                if kb < 0:
                                    continue
                                tl.append((1, sl_(r + 512 * kb, 128, 4),
                                           sl_(r + 512 * nblk, 128, 4), msk[:, :], r * 4 + kb,
                                           nn, sl_(r, 128, 4), 128))
                    for r in range(16):
                        for mm in range(2):
                            m_ = qh * 2 + mm
                            tl.append((2, sl_(r, 128, 16), sl_(r + 512 * m_, 32, 16),
                                       mcur[:, 32 * m_:32 * m_ + 32], r, mm, sl_(r, 32, 16), 32))
                    i0 = 0
                    while i0 < len(tl):
                        grp = []
                        w_ = 0
                        while i0 < len(tl) and w_ + tl[i0][7] <= 512:
                            grp.append((tl[i0], w_))
                            w_ += tl[i0][7]
                            i0 += 1
                        groups.append((grp, w_, hh, hs))
                base_ = pcount[0]
                pcount[0] += len(groups)

                def emit_S(gi):
                    grp, w_, hh, hs = groups[gi]
                    bk = (base_ + gi) % 4
                    for ii_, ((br, kap, qap, mk, vb, ob, oc_, nc_), off) in enumerate(grp):
                        P.op("pe", lambda e, br=br, kap=kap, qap=qap, off=off, nc_=nc_, bk=bk, hs=hs, ii_=ii_, Kb=Kb: e.matmul(
                            ps[bk][:, off:off + nc_], lhsT=Kb[br][hs, kap], rhs=qT[hs, br, qap], start=(ii_ == 0), stop=False,
                            skip_group_check=True),
                            reads=[r_k, r_q], writes=[psr[bk]])
                    for ii_, ((br, kap, qap, mk, vb, ob, oc_, nc_), off) in enumerate(grp):
                        P.op("pe", lambda e, mk=mk, off=off, nc_=nc_, bk=bk, ii_=ii_, ng_=len(grp): e.matmul(
                            ps[bk][:, off:off + nc_], lhsT=identb[:], rhs=mk, start=False, stop=(ii_ == ng_ - 1), skip_group_check=True),
                            writes=[psr[bk]])

                def emit_PV(gi):
                    grp, w_, hh, hs = groups[gi]
                    bk = (base_ + gi) % 4
                    pi = (base_ + gi) % 2
                    P.op("act", lambda e, bk=bk, pi=pi, w_=w_: e.activation(out=pT[pi][:, 0:w_], in_=ps[bk][:, 0:w_], func=AF.Exp),
                         reads=[psr[bk]], writes=[r_p[pi]])
                    for (br, kap, qap, mk, vb, ob, oc_, nc_), off in grp:
                        for (bank, lhs) in ((NB[ob], Vb_[br][:, vb, hs]), (DB[ob], onesb[:, 0:64])):
                            key = (bank, hh)
                            st_ = key not in first
                            first[key] = 1
                            P.op("pe", lambda e, bank=bank, lhs=lhs, oc_=oc_, pi=pi, off=off, nc_=nc_, st_=st_, hs=hs: e.matmul(
                                ps[bank][hs, oc_], lhsT=lhs, rhs=pT[pi][:, off:off + nc_], start=st_, stop=False,
                                skip_group_check=True),
                                reads=[r_p[pi], r_v], writes=[psr[bank]])

                emit_S(0)
                for gi in range(len(groups)):
                    if gi + 1 < len(groups):
                        emit_S(gi + 1)
                    emit_PV(gi)
                for ob in range(2):
                    P.op("dve", lambda e, ob=ob: e.reciprocal(out=rec[:, ob * 512:(ob + 1) * 512], in_=ps[DB[ob]][:]),
                         reads=[psr[DB[ob]]], writes=[r_rec])
                    P.op("dve", lambda e, ob=ob, hp=hp, qh=qh: e.tensor_tensor(
                        out=oT[:, hp, qh * 1024 + ob * 512: qh * 1024 + (ob + 1) * 512], in0=ps[NB[ob]][:],
                        in1=rec[:, ob * 512:(ob + 1) * 512], op=ALU.mult),
                        reads=[psr[NB[ob]], r_rec], writes=[r_o])
        P.barrier()
        rot["a"] = 0
        gemm_fm(oT, r_o, 8, wview(wo_b[j_]), [[c0 + 128 * j for j in range(4)] for c0 in (0, 512)], resid_epi(n, b),
                wr=wready(("wo", j_)))

    step = [0]

    def done():
        step[0] += 1
        return step[0] >= stop_after

    def u_to_h():
        P.barrier()
        for t in range(4):
            for kc in range(8):
                P.op("act", lambda e, t=t, kc=kc: e.activation(out=h[:, kc, T(t)], in_=u[:, kc, T(t)], func=AF.Copy),
                     reads=[r_u[t]], writes=[r_h[t]])

    for b in range(nseq):
        P.barrier()
        load_x(b)
        stopped = False
        for l in range(4):
            if l == 2:
                P.barrier()
                kv_w.clear()
                kv_phase(b)
                r_kv.w = None
                P.barrier()
            P.barrier()
            if dbg == ("w", l):
                wtmp = big[:, :].rearrange("p (a b) -> p a b", b=2048)
                rw_ = Res()
                P.op("sp", lambda e: e.dma_start(out=wtmp, in_=wview(wglu_b[0])), reads=[wready(("wglu", 0))], writes=[rw_], dma=d_tok[1])
                P.op("sp", lambda e: e.dma_start(out=dbg_w.rearrange("(kc p) n -> p kc n", p=128), in_=wtmp), reads=[rw_], writes=[rw_], dma=d_tok[1])
                stopped = True
                break
            if dbg != ("m", l):
                norm(l * 2, b, u, r_u)
            if dbg == ("u", l):
                u_to_h()
                stopped = True
                break
            if dbg == ("m", l):
                pass
            elif dbg == ("gs", l):
                L = s5_layer_setup(l)
                s5_mixer(l, b, L)
                P.barrier()
                norm(l * 2, b, u, r_u)
                P.barrier()
                glu(l, b)
                stopped = True
                break
            elif dbg == ("gn", l):
                glu(l, b)
                stopped = True
                break
            elif l < 2:
                L = s5_layer_setup(l)
                if dbg == ("p", l):
                    P.barrier()
                    stopped = True
                    break
                s5_mixer(l, b, L)
                P.barrier()
                if dbg is not None and len(dbg) > 2 and dbg[2] == "fix":
                    norm(l * 2, b, big[:, :].rearrange("p (a b) -> p a b", b=S), [Res() for _ in range(4)])
                    P.barrier()
                if dbg is not None and dbg[:2] == ("pu", l):
                    for t_ in range(4):
                        for kc_ in range(8):
                            bk_ = (t_ * 8 + kc_) % 4
                            P.op("pe", lambda e, t_=t_, kc_=kc_, bk_=bk_: e.matmul(ps[bk_][:], lhsT=identb[:], rhs=u[:, kc_, T(t_)],
                                                                                   start=True, stop=True), reads=[r_u[t_]], writes=[psr[bk_]])
                            P.op("dve", lambda e, t_=t_, kc_=kc_, bk_=bk_: e.tensor_copy(out=h[:, kc_, T(t_)], in_=ps[bk_][:]),
                                 reads=[psr[bk_], r_h[t_]], writes=[r_h[t_]])
                    stopped = True
                    break
                if dbg is not None and dbg[:2] in (("z", l), ("y", l)):
                    u_to_h()
                    stopped = True
                    break
                glu(l, b)
            else:
                attention(l, b)
            if done():
                stopped = True
                break
            P.barrier()
            norm(l * 2 + 1, b, u, r_u)
            mlp(l, b)
            if done():
                stopped = True
                break
        step[0] = 0
        P.barrier()
        store_out(b, final=not stopped)
    P.barrier()
    P.op("sp", lambda e: e.dma_start(out=tok[0][0:1, 0:8], in_=x_d[0, 0:1, 0:8]), dma=d_tok[0])
    P.emit([(d_tok[0], P.cnt[d_tok[0]])] + [(k, P.cnt[k]) for k in d_otok])
    st.close()
    return nc


def _bf16(a):
    import ml_dtypes
    return np.asarray(a, dtype=np.float32).astype(ml_dtypes.bfloat16)


def make_in_maps(inp):
    f = lambda a: np.ascontiguousarray(np.asarray(a, dtype=np.float32))
    x = f(inp["x"]); c = f(inp["c"])
    adaw = np.concatenate([f(inp["ada_w"]).reshape(8, D, 3072)[i] for i in range(8)] + [f(inp["kv_ada_w"])], axis=1)
    adab_flat = np.concatenate([f(inp["ada_b"]).reshape(-1), f(inp["kv_ada_b"])])
    adab = np.ascontiguousarray(adab_flat.reshape(NCH, 128).T)
    lng_all = np.concatenate([f(inp["ln_g"]).reshape(8, D), f(inp["kv_g"]).reshape(1, D)], axis=0)
    lng = np.ascontiguousarray(lng_all.reshape(9, 8, 128).transpose(2, 0, 1))
    fing = np.ascontiguousarray(np.broadcast_to(f(inp["final_g"])[None, :], (128, D)))
    dsk = np.ascontiguousarray(f(inp["ssm_d"]).reshape(2, 8, 128).transpose(2, 0, 1))
    kk = np.arange(128)[:, None]; qq = np.arange(128)[None, :]
    mcur = np.where(kk <= qq, 0.0, -30000.0).astype(np.float32)
    mprev = np.where(kk >= qq, 0.0, -30000.0).astype(np.float32)
    common = dict(
        adaw=np.ascontiguousarray(adaw), adab=adab, lng=lng, fing=fing, dsk=dsk,
        lamre=f(inp["ssm_lam_re"]), lamim=f(inp["ssm_lam_im"]), logdt=f(inp["ssm_log_dt"]).reshape(2, 64, 1),
        bre=f(inp["ssm_b_re"]).reshape(2, 64, 1024), bim=f(inp["ssm_b_im"]).reshape(2, 64, 1024),
        cre=f(inp["ssm_c_re"]).reshape(2, 1024, 64), cim=f(inp["ssm_c_im"]).reshape(2, 1024, 64),
        identf=np.eye(128, dtype=np.float32), identb=_bf16(np.eye(128)), mcur=_bf16(mcur), mprev=_bf16(mprev),
        w1=f(inp["mlp_w1"]), w2=f(inp["mlp_w2"]), wglu=f(inp["ssm_w_glu"]), wkv=f(inp["w_kv"]),
        wq=f(inp["attn_w_q"]), wo=f(inp["attn_w_o"]),
    )
    maps = []
    for i in range(8):
        m = dict(common)
        m["x"] = np.ascontiguousarray(x[2 * i:2 * i + 2])
        m["cT"] = np.ascontiguousarray(c[2 * i:2 * i + 2].T.reshape(8, 128, 2).transpose(1, 0, 2))
        maps.append(m)
    return maps


def kernel(**inputs):
    nc = build_program()
    maps = make_in_maps(inputs)
    res = run_bass_kernel_spmd(nc, maps, core_ids=list(range(8)))
    return np.concatenate([r["out"] for r in res.results], axis=0).astype(np.float32)
```

```python
import contextlib
import numpy as np
import concourse.bass as bass
import concourse.mybir as mybir
from concourse.bass_utils import run_bass_kernel_spmd

F32 = mybir.dt.float32
BF16 = mybir.dt.bfloat16
I32 = mybir.dt.int32
AF = mybir.ActivationFunctionType
ALU = mybir.AluOpType

ENGS = ("pe", "act", "dve", "pool", "sp")
S = 2048
D = 1024
NMOD = 26624
NCH = NMOD // 128
EPS = 1e-6


class Res:
    __slots__ = ("w", "r")

    def __init__(self):
        self.w = None
        self.r = []


class Prog:
    def __init__(self, nc):
        self.nc = nc
        self.ops = {e: [] for e in ENGS}
        self.cnt = {}
        self.seen = {e: {} for e in ENGS}
        self.sems = {}
        self.nd = 0
        self.bar = None
        self.bar_done = {e: None for e in ENGS}

    def dsem(self):
        k = "d%d" % self.nd
        self.nd += 1
        self.cnt[k] = 0
        return k

    def barrier(self):
        self.bar = dict(self.cnt)

    def op(self, eng, fn, reads=(), writes=(), dma=None, after=()):
        waits = {}
        for k_, v_ in after:
            waits[k_] = v_

        def need(dep, war=False):
            if dep is None:
                return
            k, v = dep
            if k == eng and (eng == "pe" or (war and eng != "pool")):
                return
            if waits.get(k, 0) < v:
                waits[k] = v

        if self.bar is not None and self.bar_done[eng] is not self.bar:
            self.bar_done[eng] = self.bar
            for k, v in self.bar.items():
                if v > 0 and k != eng:
                    waits[k] = v
        for r in reads:
            need(r.w)
        for w in writes:
            need(w.w)
            for d in w.r:
                need(d, war=True)
        final = []
        seen = self.seen[eng]
        for k, v in waits.items():
            if seen.get(k, 0) >= v:
                continue
            seen[k] = v
            final.append((k, v))
        key, inc = (eng, 1) if dma is None else (dma, 16)
        self.cnt[key] = self.cnt.get(key, 0) + inc
        me = (key, self.cnt[key])
        self.ops[eng].append((final, fn, key, inc))
        for r in reads:
            r.r.append(me)
            if len(r.r) > 64:
                r.r = _compress(r.r)
        for w in writes:
            w.w = me
            w.r = []
        return me

    def emit(self, final_waits):
        nc = self.nc
        with contextlib.ExitStack() as st:
            for k in list(self.cnt.keys()):
                self.sems[k] = st.enter_context(nc.semaphore("s_" + k))
            block = st.enter_context(nc.Block())
            sems = self.sems

            def run(engname, e):
                for waits, fn, key, inc in self.ops[engname]:
                    for k, v in waits:
                        e.wait_ge(sems[k], v)
                    fn(e).then_inc(sems[key], inc)
                if engname == "sp":
                    for k, v in final_waits:
                        e.wait_ge(sems[k], v)

            @block.tensor
            def _(e):
                run("pe", e)

            @block.scalar
            def _(e):
                run("act", e)

            @block.vector
            def _(e):
                run("dve", e)

            @block.gpsimd
            def _(e):
                run("pool", e)

            @block.sync
            def _(e):
                run("sp", e)


def _compress(lst):
    best = {}
    for k, v in lst:
        if best.get(k, 0) < v:
            best[k] = v
    return list(best.items())


def build_program(stop_after=99, dbg=None, nseq=2):
    nc = bass.Bass("TRN2", target_bir_lowering=False)
    P = Prog(nc)

    def din(name, shape, dt=F32):
        return nc.dram_tensor(name, list(shape), dt, kind="ExternalInput").ap()

    def dscr(name, shape, dt=BF16):
        return nc.dram_tensor(name, list(shape), dt, kind="Internal").ap()

    x_d = din("x", [2, S, D])
    cT_d = din("cT", [128, 8, 2])
    adaw_d = din("adaw", [D, NMOD])
    adab_d = din("adab", [128, NCH])
    lng_d = din("lng", [128, 9, 8])
    fing_d = din("fing", [128, D])
    dsk_d = din("dsk", [128, 2, 8])
    lamre_d = din("lamre", [2, 64, 64])
    lamim_d = din("lamim", [2, 64, 64])
    logdt_d = din("logdt", [2, 64, 1])
    bre_d = din("bre", [2, 64, 1024])
    bim_d = din("bim", [2, 64, 1024])
    cre_d = din("cre", [2, 1024, 64])
    cim_d = din("cim", [2, 1024, 64])
    identf_d = din("identf", [128, 128])
    identb_d = din("identb", [128, 128], BF16)
    mcur_d = din("mcur", [128, 128], BF16)
    mprev_d = din("mprev", [128, 128], BF16)
    w1_d = din("w1", [4, D, 4096])
    w2_d = din("w2", [4, 4096, D])
    wglu_d = din("wglu", [2, D, 2048])
    wkv_d = din("wkv", [D, 6144])
    wq_d = din("wq", [2, D, 3072])
    wo_d = din("wo", [2, D, D])
    out_d = nc.dram_tensor("out", [2, S, D], F32, kind="ExternalOutput").ap()
    if dbg is not None and dbg[0] == "w":
        dbg_w = nc.dram_tensor("dbg_w", [D, 2048], BF16, kind="ExternalOutput").ap()
    if dbg is not None and dbg[0] == "p":
        dbg_l = nc.dram_tensor("dbg_l", [64, 256], F32, kind="ExternalOutput").ap()
        dbg_bb = nc.dram_tensor("dbg_bb", [64, 16, 2, 64], BF16, kind="ExternalOutput").ap()

    w1_b = dscr("w1b", [4, D, 4096])
    w2_b = dscr("w2b", [4, 4096, D])
    wglu_b = dscr("wglub", [2, D, 2048])
    wkv_b = dscr("wkvb", [D, 6144])
    wq_b = dscr("wqb", [2, D, 3072])
    wo_b = dscr("wob", [2, D, D])
    kt_s = dscr("kts", [2, 24, 128, S])
    v_s = dscr("vs", [2, S, 3072])
    bb_s = dscr("bbs", [2, 64, 16, 128])
    bb_s1 = dscr("bbs1", [2, 64, 16, 128])

    st = contextlib.ExitStack()

    def sb(name, shape, dt):
        return st.enter_context(nc.sbuf_tensor(name, list(shape), dt))

    h = sb("h", [128, 8, S], F32)
    u = sb("u", [128, 8, S], BF16)
    big = sb("big", [128, 16384], BF16)
    slabs = [sb("slab%d" % i, [128, 8, 512], BF16) for i in range(3)]
    sq = [sb("sq%d" % i, [128, 512], BF16) for i in range(2)]
    rs = sb("rs", [128, 512], F32)
    rstd = rs
    tmpf = [sb("tmpf%d" % i, [128, 512], F32) for i in range(2)]
    identf = sb("identf_s", [128, 128], F32)
    identb = sb("identb_s", [128, 128], BF16)
    onesb = sb("onesb", [128, 128], BF16)
    mcur = sb("mcur_s", [128, 128], BF16)
    mprev = sb("mprev_s", [128, 128], BF16)
    mod = sb("mod", [128, NCH, 2], F32)
    adab = sb("adab_s", [128, NCH], F32)
    lng = sb("lng_s", [128, 9, 8], F32)
    Gm = sb("Gm", [128, 9, 8, 2], F32)
    dsk = sb("dsk_s", [128, 2, 8], F32)
    cT = sb("cT_s", [128, 8, 2], F32)
    scT = sb("scT", [128, 8, 2], F32)
    arena = sb("arena", [128, 10752], F32)

    def carve(off, shape, dt, parts=128):
        nel = int(np.prod(shape[1:]))
        nb = nel * (2 if dt == BF16 else 4)
        assert off % 4 == 0 and off + nb <= 10752 * 4, (off, nb)
        ap = arena[0:parts, off // 4:(off + nb + 3) // 4]
        if dt == BF16:
            ap = ap.bitcast(BF16)
        elif dt == I32:
            ap = ap.bitcast(I32)
        if len(shape) == 3:
            ap = ap.rearrange("p (a b) -> p a b", b=shape[2])
        elif len(shape) == 4:
            ap = ap.rearrange("p (a b c) -> p a b c", b=shape[2], c=shape[3])
        return ap

    ps = [st.enter_context(nc.psum_tensor("ps%d" % i, [128, 512], F32)) for i in range(8)]
    psr = [Res() for _ in range(8)]

    r_h = [Res() for _ in range(4)]
    r_u = [Res() for _ in range(4)]
    r_big = Res()
    r_slab = [Res() for _ in range(3)]
    d_slab = [P.dsem() for _ in range(3)]
    r_sq = [Res(), Res()]
    r_rs = Res()
    r_rstd = Res()
    r_tmpf = [Res(), Res()]
    r_const = Res()
    r_mod = Res()
    d_const = P.dsem()
    d_constp = P.dsem()
    d_misc = [P.dsem() for _ in range(8)]
    misc_i = [0]
    slab_i = [0]
    rot = {"a": 0}

    def next_misc():
        k = d_misc[misc_i[0] % len(d_misc)]
        misc_i[0] += 1
        return k

    for (dst, src) in [(identf, identf_d), (adab, adab_d), (lng, lng_d), (dsk, dsk_d), (cT, cT_d)]:
        P.op("sp", lambda e, dst=dst, src=src: e.dma_start(out=dst[:], in_=src), writes=[r_const], dma=d_const)
    for (dst, src) in [(identb, identb_d), (mcur, mcur_d), (mprev, mprev_d)]:
        P.op("pool", lambda e, dst=dst, src=src: e.dma_start(out=dst[:], in_=src), writes=[r_const], dma=d_constp)
    r_const.w = (d_const, P.cnt[d_const])
    P.op("dve", lambda e: e.memset(onesb[:], 1.0), writes=[r_const])
    r_const.w = None
    P.barrier()
    scTb = carve(40960, [128, 8, 2], BF16)
    P.op("act", lambda e: e.activation(out=scTb, in_=cT[:], func=AF.Silu), writes=[r_const])
    aslab = [carve(i * 8192, [128, 8, 512], BF16) for i in range(4)]
    r_aslab = [Res() for _ in range(4)]
    d_aslab = [P.dsem() for _ in range(4)]
    adaw_v = adaw_d.rearrange("(kc p) n -> p kc n", p=128)
    nsl = NMOD // 512
    for s_ in range(nsl):
        bi = s_ % 4
        P.op("pool", lambda e, s_=s_, bi=bi: e.dma_start(out=aslab[bi], in_=adaw_v[:, :, s_ * 512:(s_ + 1) * 512]),
             writes=[r_aslab[bi]], dma=d_aslab[bi])
        for j in range(4):
            ch = s_ * 4 + j
            for kc in range(8):
                P.op("pe", lambda e, bi=bi, j=j, kc=kc, ch=ch: e.matmul(
                    ps[7][:, 2 * ch:2 * ch + 2], lhsT=aslab[bi][:, kc, j * 128:(j + 1) * 128], rhs=scTb[:, kc, :],
                    start=(kc == 0), stop=(kc == 7)), reads=[r_aslab[bi], r_const], writes=[psr[7]])

    d_castAB = [P.dsem(), P.dsem()]
    cast_jobs = []
    cast_pos = [0]
    grp_end = {}
    key_last = {}
    wres = {}

    def add_cast(key, dst, src, rows, cols):
        for r0 in range(0, rows, 128):
            for c0 in range(0, cols, 1024):
                cast_jobs.append((key, dst, src, r0, c0, min(cols, c0 + 1024)))
        key_last[key] = len(cast_jobs) - 1

    for l in range(2):
        add_cast(("wglu", l), wglu_b[l], wglu_d[l], D, 2048)
        add_cast(("w1", l), w1_b[l], w1_d[l], D, 4096)
        add_cast(("w2", l), w2_b[l], w2_d[l], 4096, D)
    add_cast(("wkv", 0), wkv_b, wkv_d, D, 6144)
    for l in range(2):
        add_cast(("wq", l), wq_b[l], wq_d[l], D, 3072)
        add_cast(("wo", l), wo_b[l], wo_d[l], D, D)
        add_cast(("w1", l + 2), w1_b[l + 2], w1_d[l + 2], D, 4096)
        add_cast(("w2", l + 2), w2_b[l + 2], w2_d[l + 2], 4096, D)

    def pump(n):
        for _ in range(n):
            i = cast_pos[0]
            if i >= len(cast_jobs):
                return
            key, dst, src, r0, c0, c1 = cast_jobs[i]
            g_ = i // 24
            dk = d_castAB[g_ % 2]
            aft = [(dk, P.cnt[dk])] if (i % 24 == 0 and g_ >= 2) else []
            cast_pos[0] += 1
            me = P.op("pool", lambda e, dst=dst, src=src, r0=r0, c0=c0, c1=c1: e.dma_start(
                out=dst[r0:r0 + 128, c0:c1], in_=src[r0:r0 + 128, c0:c1], max_dma_last_dim=4096),
                dma=dk, after=aft)
            grp_end[g_] = me

    def wready(key):
        if key not in wres:
            last = key_last[key]
            tgt = min(len(cast_jobs), (last // 24 + 1) * 24)
            if cast_pos[0] < tgt:
                pump(tgt - cast_pos[0])
            r = Res()
            r.w = grp_end[last // 24]
            wres[key] = r
        return wres[key]

    P.op("dve", lambda e: e.tensor_tensor(out=mod[:], in0=ps[7][:, 0:2 * NCH].rearrange("p (c b) -> p c b", b=2),
                                          in1=adab[:].unsqueeze(2).to_broadcast([128, NCH, 2]), op=ALU.add),
         reads=[psr[7]], writes=[r_mod])
    for n in range(9):
        base = (n * 3072 + 1024) // 128 if n < 8 else (24576 + 1024) // 128
        P.op("dve", lambda e, n=n, base=base: e.scalar_tensor_tensor(
            out=Gm[:, n, :, :], in0=mod[:, base:base + 8, :], scalar=1.0,
            in1=lng[:, n, :].unsqueeze(2).to_broadcast([128, 8, 2]), op0=ALU.add, op1=ALU.mult),
            reads=[r_mod], writes=[r_mod])

    def mod_shift(n, kc, b):
        base = (n * 3072) // 128 if n < 8 else 24576 // 128
        return mod[:, base + kc, b:b + 1]

    def mod_gate(n, kc, b):
        base = (n * 3072 + 2048) // 128
        return mod[:, base + kc, b:b + 1]

    def T(t):
        return slice(t * 512, (t + 1) * 512)

    def norm(n, b, out_buf, r_out):
        for t in range(4):
            for kc in range(8):
                i = kc % 2
                P.op("act", lambda e, kc=kc, i=i, t=t: e.activation(out=sq[i][:], in_=h[:, kc, T(t)], func=AF.Square),
                     reads=[r_h[t]], writes=[r_sq[i]])
                P.op("pe", lambda e, kc=kc, i=i: e.matmul(ps[6][:], lhsT=onesb[:], rhs=sq[i][:],
                                                          start=(kc == 0), stop=(kc == 7)),
                     reads=[r_sq[i]], writes=[psr[6]])
            P.op("act", lambda e: e.activation(out=rs[:], in_=ps[6][:], func=AF.Sqrt, bias=EPS, scale=1.0 / D),
                 reads=[psr[6]], writes=[r_rs, r_rstd])
            P.op("dve", lambda e: e.reciprocal(out=rstd[:], in_=rs[:]), reads=[r_rs], writes=[r_rs, r_rstd])
            for kc in range(8):
                i = kc % 2
                P.op("pool", lambda e, kc=kc, i=i, t=t: e.tensor_tensor(out=tmpf[i][:], in0=h[:, kc, T(t)], in1=rstd[:],
                                                                         op=ALU.mult),
                     reads=[r_h[t], r_rstd], writes=[r_tmpf[i]])
                P.op("act", lambda e, kc=kc, i=i, t=t: e.activation(
                    out=out_buf[:, kc, T(t)], in_=tmpf[i][:], func=AF.Identity,
                    bias=mod_shift(n, kc, b), scale=Gm[:, n, kc, b:b + 1]),
                    reads=[r_tmpf[i], r_mod], writes=[r_out[t]])

    def load_slab(view, wr):
        i = slab_i[0] % 3
        slab_i[0] += 1
        P.op("sp", lambda e, i=i, view=view: e.dma_start(out=slabs[i][:, 0:view.shape[1], 0:view.shape[2]], in_=view),
             reads=[wr], writes=[r_slab[i]], dma=d_slab[i])
        return i

    def gemm_fm(src, r_src, KC, wv, colgroups, epi, tiles=range(4), wr=None):
        for t in tiles:
            for grp in colgroups:
                c0 = grp[0]
                banks = []
                for _ in grp:
                    banks.append(rot["a"] % 6)
                    rot["a"] += 1
                for kg in range(KC // 8):
                    si = load_slab(wv[:, kg * 8:(kg + 1) * 8, c0:grp[-1] + 128], wr)
                    for j, c in enumerate(grp):
                        for kc in range(8):
                            kk = kg * 8 + kc
                            P.op("pe", lambda e, si=si, j=j, c=c, kc=kc, kk=kk, bk=banks[j], t=t, c0=c0: e.matmul(
                                ps[bk][:], lhsT=slabs[si][:, kc, c - c0:c - c0 + 128], rhs=src[:, kk, T(t)],
                                start=(kk == 0), stop=(kk == KC - 1)),
                                reads=[r_slab[si], r_src[t] if isinstance(r_src, list) else r_src], writes=[psr[banks[j]]])
                for j, c in enumerate(grp):
                    epi(t, c // 128, ps[banks[j]], psr[banks[j]])

    def wview(w2d):
        return w2d.rearrange("(kc p) n -> p kc n", p=128)

    def resid_epi(n, b):
        def epi(t, oc, p_ap, p_res):
            P.op("dve", lambda e, t=t, oc=oc, p_ap=p_ap: e.scalar_tensor_tensor(
                out=h[:, oc, T(t)], in0=p_ap[:], scalar=mod_gate(n, oc, b), in1=h[:, oc, T(t)],
                op0=ALU.mult, op1=ALU.add), reads=[p_res, r_mod, r_h[t]], writes=[r_h[t]])
        return epi

    hid = big[:, :].rearrange("p (a b) -> p a b", b=512)
    r_hid = Res()
    relu_t = [carve(0, [128, 512], BF16), carve(1024, [128, 512], BF16)]
    r_relu = [Res(), Res()]

    def mlp(l, b):
        n = l * 2 + 1
        w1v = wview(w1_b[l])
        w2v = wview(w2_b[l])
        for t in range(4):
            def epi1(t_, hc, p_ap, p_res):
                i = hc % 2
                P.op("act", lambda e, i=i, p_ap=p_ap: e.activation(out=relu_t[i], in_=p_ap[:], func=AF.Relu),
                     reads=[p_res], writes=[r_relu[i]])
                P.op("pool", lambda e, i=i, hc=hc: e.tensor_tensor(out=hid[:, hc, :], in0=relu_t[i], in1=relu_t[i],
                                                                    op=ALU.mult),
                     reads=[r_relu[i]], writes=[r_hid])
            gemm_fm(u, r_u, 8, w1v, [[c0 + 128 * j for j in range(4)] for c0 in range(0, 4096, 512)], epi1, tiles=[t],
                    wr=wready(("w1", l)))

            class HidSrc:
                def __getitem__(self, idx):
                    return hid[idx[0], idx[1], :]
            gemm_fm(HidSrc(), r_hid, 32, w2v, [[c0 + 128 * j for j in range(4)] for c0 in (0, 512)],
                    lambda t_, oc, p_ap, p_res, t=t: resid_epi(n, b)(t, oc, p_ap, p_res), tiles=[t], wr=wready(("w2", l)))

    tok = [carve(0, [128, D], F32), carve(4096, [128, D], F32)]
    r_tok = [Res(), Res()]
    d_tok = [P.dsem(), P.dsem()]
    otok = [carve(8192, [128, D], F32), carve(12288, [128, D], F32)]
    r_otok = [Res(), Res()]
    d_otok = [P.dsem(), P.dsem()]
    ss1 = carve(16384, [128, 1], F32)
    ss2 = carve(16400, [128, 1], F32)
    ss3 = carve(16416, [128, 1], F32)
    junk = carve(16448, [128, D], F32)
    fing = carve(20544, [128, D], F32)
    r_ss = Res()
    last_out = []

    def load_x(b):
        for tb in range(16):
            i = tb % 2
            P.op("sp", lambda e, tb=tb, i=i: e.dma_start(out=tok[i], in_=x_d[b, tb * 128:(tb + 1) * 128, :]),
                 writes=[r_tok[i]], dma=d_tok[i])
            for half in range(2):
                bk = rot["a"] % 6
                rot["a"] += 1
                for j in range(4):
                    kc = half * 4 + j
                    P.op("pe", lambda e, i=i, kc=kc, j=j, bk=bk: e.transpose(
                        out=ps[bk][:, j * 128:(j + 1) * 128], in_=tok[i][:, kc * 128:(kc + 1) * 128], identity=identf[:]),
                        reads=[r_tok[i]], writes=[psr[bk]])
                P.op("act", lambda e, half=half, bk=bk, tb=tb: e.activation(
                    out=h[:, half * 4:half * 4 + 4, tb * 128:(tb + 1) * 128],
                    in_=ps[bk][:].rearrange("p (a b) -> p a b", b=128), func=AF.Copy),
                    reads=[psr[bk]], writes=[r_h[tb // 4]])

    def store_out(b, final=True):
        P.op("sp", lambda e: e.dma_start(out=fing, in_=fing_d), writes=[r_ss], dma=d_tok[0])
        for tb in range(16):
            i = tb % 2
            for half in range(2):
                for j in range(4):
                    kc = half * 4 + j
                    P.op("pe", lambda e, kc=kc, j=j, half=half, tb=tb: e.transpose(
                        out=ps[half][:, j * 128:(j + 1) * 128], in_=h[:, kc, tb * 128:(tb + 1) * 128], identity=identf[:]),
                        reads=[r_h[tb // 4]], writes=[psr[half]])
            if final:
                for half in range(2):
                    P.op("act", lambda e, half=half: e.activation(
                        out=junk[:, half * 512:(half + 1) * 512], in_=ps[half][:], func=AF.Square,
                        accum_out=(ss1 if half == 0 else ss2)), reads=[psr[half]], writes=[r_ss])
                P.op("dve", lambda e: e.tensor_tensor(out=ss3, in0=ss1, in1=ss2, op=ALU.add), reads=[r_ss], writes=[r_ss])
                P.op("act", lambda e: e.activation(out=ss1, in_=ss3, func=AF.Sqrt, bias=EPS, scale=1.0 / D),
                     reads=[r_ss], writes=[r_ss])
                P.op("dve", lambda e: e.reciprocal(out=ss2, in_=ss1), reads=[r_ss], writes=[r_ss])
                for half in range(2):
                    P.op("dve", lambda e, half=half, i=i: e.scalar_tensor_tensor(
                        out=otok[i][:, half * 512:(half + 1) * 512], in0=ps[half][:], scalar=ss2,
                        in1=fing[:, half * 512:(half + 1) * 512], op0=ALU.mult, op1=ALU.mult),
                        reads=[psr[half], r_ss], writes=[r_otok[i]])
            else:
                for half in range(2):
                    P.op("act", lambda e, half=half, i=i: e.activation(
                        out=otok[i][:, half * 512:(half + 1) * 512], in_=ps[half][:], func=AF.Copy),
                        reads=[psr[half]], writes=[r_otok[i]])
            me = P.op("sp", lambda e, i=i, tb=tb: e.dma_start(out=out_d[b, tb * 128:(tb + 1) * 128, :], in_=otok[i]),
                      reads=[r_otok[i]], dma=d_otok[i])
            r_otok[i].r.append(me)
            last_out.append(me)

    def s5_prep(l):
        P.barrier()
        o = [0]

        def A(shape, dt=F32, parts=64):
            nel = int(np.prod(shape[1:]))
            nb = ((nel * (2 if dt == BF16 else 4) + 3) // 4) * 4
            ap = carve(o[0], shape, dt, parts)
            o[0] += nb
            return ap
        lre = A([64, 64]); lim = A([64, 64]); ldt = A([64, 1]); dt_ = A([64, 1])
        a_ = A([64, 64]); th = A([64, 64]); ea = A([64, 64]); yy = A([64, 64]); ki = A([64, 64], I32)
        kf = A([64, 64]); ff = A([64, 64]); sn = A([64, 64]); cs = A([64, 64])
        lbr = A([64, 64]); lbi = A([64, 64]); den = A([64, 64]); nr = A([64, 64]); ni = A([64, 64])
        qr = A([64, 64]); qi = A([64, 64]); t1 = A([64, 64]); t2 = A([64, 64])
        br = A([64, 64, 16]); bi_ = A([64, 64, 16])
        bbr = A([64, 64, 16]); bbi = A([64, 64, 16]); t3 = A([64, 64, 16])
        bout = A([64, 16, 2, 64], BF16)
        l2r = A([64, 64]); l2i = A([64, 64])
        R = Res()
        dd = next_misc()
        for dst, src in [(lre, lamre_d[l]), (lim, lamim_d[l]), (ldt, logdt_d[l])]:
            P.op("sp", lambda e, dst=dst, src=src: e.dma_start(out=dst, in_=src), writes=[R], dma=dd)
        P.op("sp", lambda e: e.dma_start(out=br, in_=bre_d[l].rearrange("g (p c) -> g p c", c=16)), writes=[R], dma=dd)
        P.op("sp", lambda e: e.dma_start(out=bi_, in_=bim_d[l].rearrange("g (p c) -> g p c", c=16)), writes=[R], dma=dd)

        def V(fn):
            P.op("dve", fn, reads=[R], writes=[R])

        def ACT(fn):
            P.op("act", fn, reads=[R], writes=[R])
        ACT(lambda e: e.activation(out=dt_, in_=ldt, func=AF.Exp))
        V(lambda e: e.tensor_scalar(out=a_, in0=lre, scalar1=dt_, scalar2=None, op0=ALU.mult))
        V(lambda e: e.tensor_scalar(out=th, in0=lim, scalar1=dt_, scalar2=None, op0=ALU.mult))
        ACT(lambda e: e.activation(out=ea, in_=a_, func=AF.Exp))

        def sin_of(dst, offs):
            V(lambda e: e.tensor_scalar(out=yy, in0=th, scalar1=1.0 / (2 * np.pi), scalar2=offs, op0=ALU.mult, op1=ALU.add))
            V(lambda e: e.tensor_copy(out=ki, in_=yy))
            V(lambda e: e.tensor_copy(out=kf, in_=ki))
            V(lambda e: e.tensor_tensor(out=ff, in0=yy, in1=kf, op=ALU.subtract))
            V(lambda e: e.scalar_tensor_tensor(out=ff, in0=ff, scalar=0.0, in1=ff, op0=ALU.is_lt, op1=ALU.add))
            V(lambda e: e.tensor_scalar(out=ff, in0=ff, scalar1=2 * np.pi, scalar2=-np.pi, op0=ALU.mult, op1=ALU.add))
            V(lambda e: e.tensor_scalar(out=ff, in0=ff, scalar1=-3.14159, scalar2=3.14159, op0=ALU.max, op1=ALU.min))
            ACT(lambda e: e.activation(out=dst, in_=ff, func=AF.Sin))
        sin_of(sn, 0.5)
        sin_of(cs, 0.75)
        V(lambda e: e.tensor_tensor(out=lbr, in0=ea, in1=cs, op=ALU.mult))
        V(lambda e: e.tensor_tensor(out=lbi, in0=ea, in1=sn, op=ALU.mult))
        V(lambda e: e.tensor_scalar(out=nr, in0=lbr, scalar1=-1.0, scalar2=None, op0=ALU.add))
        V(lambda e: e.tensor_tensor(out=t1, in0=lre, in1=lre, op=ALU.mult))
        V(lambda e: e.tensor_tensor(out=t2, in0=lim, in1=lim, op=ALU.mult))
        V(lambda e: e.tensor_tensor(out=den, in0=t1, in1=t2, op=ALU.add))
        V(lambda e: e.reciprocal(out=den, in_=den))
        V(lambda e: e.tensor_tensor(out=t1, in0=nr, in1=lre, op=ALU.mult))
        V(lambda e: e.tensor_tensor(out=t2, in0=lbi, in1=lim, op=ALU.mult))
        V(lambda e: e.tensor_tensor(out=qr, in0=t1, in1=t2, op=ALU.add))
        V(lambda e: e.tensor_tensor(out=qr, in0=qr, in1=den, op=ALU.mult))
        V(lambda e: e.tensor_tensor(out=t1, in0=lbi, in1=lre, op=ALU.mult))
        V(lambda e: e.tensor_tensor(out=t2, in0=nr, in1=lim, op=ALU.mult))
        V(lambda e: e.tensor_tensor(out=qi, in0=t1, in1=t2, op=ALU.subtract))
        V(lambda e: e.tensor_tensor(out=qi, in0=qi, in1=den, op=ALU.mult))
        qrb = qr.unsqueeze(2).to_broadcast([64, 64, 16])
        qib = qi.unsqueeze(2).to_broadcast([64, 64, 16])
        V(lambda e: e.tensor_tensor(out=bbr, in0=br, in1=qrb, op=ALU.mult))
        V(lambda e: e.tensor_tensor(out=t3, in0=bi_, in1=qib, op=ALU.mult))
        V(lambda e: e.tensor_tensor(out=bbr, in0=bbr, in1=t3, op=ALU.subtract))
        V(lambda e: e.tensor_tensor(out=bbi, in0=bi_, in1=qrb, op=ALU.mult))
        V(lambda e: e.tensor_tensor(out=t3, in0=br, in1=qib, op=ALU.mult))
        V(lambda e: e.tensor_tensor(out=bbi, in0=bbi, in1=t3, op=ALU.add))
        V(lambda e: e.tensor_copy(out=bout[:, :, 0, :], in_=bbr.rearrange("g p c -> g c p")))
        V(lambda e: e.tensor_copy(out=bout[:, :, 1, :], in_=bbi.rearrange("g p c -> g c p")))
        P.op("sp", lambda e: e.dma_start(out=bb_s[l].rearrange("g c (r p) -> g c r p", r=2), in_=bout), reads=[R], writes=[R], dma=dd)
        lrb = lbr.unsqueeze(2).to_broadcast([64, 64, 16])
        lib = lbi.unsqueeze(2).to_broadcast([64, 64, 16])
        V(lambda e: e.tensor_tensor(out=br, in0=bbr, in1=lrb, op=ALU.mult))
        V(lambda e: e.tensor_tensor(out=t3, in0=bbi, in1=lib, op=ALU.mult))
        V(lambda e: e.tensor_tensor(out=br, in0=br, in1=t3, op=ALU.subtract))
        V(lambda e: e.tensor_tensor(out=bi_, in0=bbi, in1=lrb, op=ALU.mult))
        V(lambda e: e.tensor_tensor(out=t3, in0=bbr, in1=lib, op=ALU.mult))
        V(lambda e: e.tensor_tensor(out=bi_, in0=bi_, in1=t3, op=ALU.add))
        V(lambda e: e.tensor_copy(out=bout[:, :, 0, :], in_=br.rearrange("g p c -> g c p")))
        V(lambda e: e.tensor_copy(out=bout[:, :, 1, :], in_=bi_.rearrange("g p c -> g c p")))
        P.op("sp", lambda e: e.dma_start(out=bb_s1[l].rearrange("g c (r p) -> g c r p", r=2), in_=bout), reads=[R], writes=[R], dma=dd)
        V(lambda e: e.tensor_tensor(out=t1, in0=lbr, in1=lbr, op=ALU.mult))
        V(lambda e: e.tensor_tensor(out=t2, in0=lbi, in1=lbi, op=ALU.mult))
        V(lambda e: e.tensor_tensor(out=l2r, in0=t1, in1=t2, op=ALU.subtract))
        V(lambda e: e.tensor_tensor(out=t1, in0=lbr, in1=lbi, op=ALU.mult))
        V(lambda e: e.tensor_scalar(out=l2i, in0=t1, scalar1=2.0, scalar2=None, op0=ALU.mult))
        if dbg is not None and dbg[0] == "p":
            for i_, src_ in enumerate((lbr, lbi, qr, qi)):
                P.op("sp", lambda e, i_=i_, src_=src_: e.dma_start(out=dbg_l[:, i_ * 64:(i_ + 1) * 64], in_=src_), reads=[R], writes=[R], dma=dd)
            P.op("sp", lambda e: e.dma_start(out=dbg_bb, in_=bout), reads=[R], writes=[R], dma=dd)
        return l2r, l2i, R

    def s5_layer_setup(l):
        lbr, lbi, R = s5_prep(l)
        cw = carve(32768, [128, 2, 1024], BF16)
        ca = carve(36864, [128, 2, 32], F32)
        cb = carve(37120, [128, 2, 32], F32)
        l2 = [carve(38912, [64, 128], F32, 64), carve(39424, [64, 128], F32, 64)]
        R2 = Res()
        dd = next_misc()
        for src, idx in ((lbr, 0), (lbi, 1)):
            for hf in range(2):
                P.op("dve", lambda e, src=src, idx=idx, hf=hf: e.tensor_copy(out=l2[idx][:, hf * 64:(hf + 1) * 64], in_=src),
                     reads=[R], writes=[R])
            P.op("pe", lambda e, idx=idx: e.transpose(out=ps[6][:, idx * 64:(idx + 1) * 64], in_=l2[idx],
                                                      identity=identf[0:64, 0:64]),
                 reads=[R], writes=[psr[6]])
        for hv in range(2):
            hp_ = slice(64 * hv, 64 * hv + 64)
            gcol = slice(32 * hv, 32 * hv + 32)
            gcol2 = slice(64 + 32 * hv, 64 + 32 * hv + 32)
            P.op("dve", lambda e, hp_=hp_, gcol=gcol: e.tensor_copy(out=ca[hp_, 0, :], in_=ps[6][hp_, gcol]), reads=[psr[6]], writes=[R2])
            P.op("dve", lambda e, hp_=hp_, gcol=gcol: e.tensor_copy(out=ca[hp_, 1, :], in_=ps[6][hp_, gcol]), reads=[psr[6]], writes=[R2])
            P.op("dve", lambda e, hp_=hp_, gcol2=gcol2: e.tensor_copy(out=cb[hp_, 1, :], in_=ps[6][hp_, gcol2]), reads=[psr[6]], writes=[R2])
            P.op("dve", lambda e, hp_=hp_, gcol2=gcol2: e.tensor_scalar(out=cb[hp_, 0, :], in0=ps[6][hp_, gcol2], scalar1=-1.0,
                                                                    scalar2=None, op0=ALU.mult), reads=[psr[6]], writes=[R2])
        P.barrier()
        bbpad = [carve(0, [128, 64, 128], BF16), carve(16384, [128, 64, 128], BF16)]
        cnat = carve(38912, [128, 8, 128], F32)
        for ti_, bsrc in enumerate((bb_s, bb_s1)):
            P.op("pool", lambda e, ti_=ti_: e.memset(bbpad[ti_], 0.0), writes=[R2])
            for g in range(64):
                gl = g % 8
                P.op("sp", lambda e, g=g, gl=gl, ti_=ti_, bsrc=bsrc: e.dma_start(out=bbpad[ti_][16 * gl:16 * gl + 16, g, :], in_=bsrc[l, g]),
                     reads=[R, R2], writes=[R2], dma=dd)
        for idx, cd, sgn in ((0, cre_d, 1.0), (1, cim_d, -1.0)):
            for hf in range(2):
                P.op("sp", lambda e, cd=cd, hf=hf: e.dma_start(out=cnat[:, :, hf * 64:(hf + 1) * 64],
                                                              in_=cd[l].rearrange("(a q) p -> q a p", q=128)),
                     reads=[R2], writes=[R2], dma=dd)
            for half in range(2):
                for j in range(4):
                    a = half * 4 + j
                    P.op("pe", lambda e, a=a, j=j: e.transpose(out=ps[5][:, j * 128:(j + 1) * 128], in_=cnat[:, a, :],
                                                                identity=identf[:]), reads=[R2], writes=[psr[5]])
                P.op("dve", lambda e, idx=idx, half=half, sgn=sgn: e.tensor_scalar(
                    out=cw[:, idx, half * 512:(half + 1) * 512], in0=ps[5][:, :], scalar1=sgn, scalar2=None,
                    op0=ALU.mult), reads=[psr[5]], writes=[R2])
        P.barrier()
        return dict(bbpad=bbpad, cw=cw, ca=ca, cb=cb, R=R2)

    TC = 32

    def s5_mixer(l, b, L):
        R2 = L["R"]
        bbpad, cw, ca, cb = L["bbpad"], L["cw"], L["ca"], L["cb"]
        bigf = big[:, :].bitcast(F32)
        Vbs = [bigf[:, i * 2048:(i + 1) * 2048].rearrange("p (r g t) -> p r g t", r=2, g=32) for i in range(2)]
        Xb = bigf[:, 4096:5120].bitcast(BF16).rearrange("p (r g t) -> p r g t", r=2, g=32)
        ytok = bigf[0:TC, 5120:6144]
        zp = bigf[:, 6144:6400].rearrange("p (k t) -> p k t", t=TC)
        g1 = bigf[:, 6400:6656].rearrange("p (k t) -> p k t", t=TC)
        g2 = bigf[:, 6656:6912].rearrange("p (k t) -> p k t", t=TC)
        Zst = carve(37376, [128, 2, 32, 2], F32)
        m1 = carve(37888, [128, 2, 32, 2], F32)
        m2 = carve(38400, [128, 2, 32, 2], F32)
        rVs = [Res(), Res()]
        rX = Res(); rY = Res(); rZ = Res(); rZs = Res(); rZ2 = Res()
        rm1 = Res(); rm2a = Res(); rm2b = Res()
        r_ul = Res()
        P.op("dve", lambda e: e.memset(Zst, 0.0), writes=[rZs])
        nck = S // TC
        per_bank = 512 // (2 * TC)

        def inproj(ck):
            Vb = Vbs[ck % 2]
            rV = rVs[ck % 2]
            t0 = ck * TC
            t4 = t0 // 512
            for gg0 in range(0, 32, per_bank):
                bk = rot["a"] % 5
                rot["a"] += 1
                for hv in range(2):
                    for j in range(per_bank):
                        g = 32 * hv + gg0 + j
                        for ri in range(2):
                            c_ = (j * 2 + ri) * TC
                            P.op("pe", lambda e, g=g, ri=ri, c_=c_, bk=bk, hv=hv, t0=t0: e.matmul(
                                ps[bk][64 * hv:64 * hv + 64, c_:c_ + TC], lhsT=bbpad[0][:, g, ri * 64:(ri + 1) * 64],
                                rhs=u[:, g // 8, t0:t0 + TC], start=True, stop=False),
                                reads=[R2, r_u[t4]], writes=[psr[bk]])
                            lo = 1 if ck == 0 else 0
                            P.op("pe", lambda e, g=g, ri=ri, c_=c_, bk=bk, hv=hv, t0=t0, lo=lo: e.matmul(
                                ps[bk][64 * hv:64 * hv + 64, c_ + lo:c_ + TC], lhsT=bbpad[1][:, g, ri * 64:(ri + 1) * 64],
                                rhs=u[:, g // 8, t0 - 1 + lo:t0 - 1 + TC], start=False, stop=True),
                                reads=[R2, r_u[t4], r_ul], writes=[psr[bk]])
                P.op("act", lambda e, bk=bk, gg0=gg0, Vb=Vb: e.activation(
                    out=Vb[:, :, gg0:gg0 + per_bank, :].rearrange("p r g t -> p g r t"),
                    in_=ps[bk][:, 0:per_bank * 2 * TC].rearrange("p (g r t) -> p g r t", r=2, t=TC), func=AF.Copy),
                    reads=[psr[bk]], writes=[rV])

        def scan(ck):
            Vb = Vbs[ck % 2]
            rV = rVs[ck % 2]
            for t in range(0, TC, 2):
                if t == 0:
                    prev = Zst[:, :, :, :]
                    pf = [Zst[:, 1, :, :], Zst[:, 0, :, :]]
                    rp = rZs
                else:
                    prev = Vb[:, :, :, t - 2:t]
                    pf = [Vb[:, 1, :, t - 2:t], Vb[:, 0, :, t - 2:t]]
                    rp = rV
                cur = Vb[:, :, :, t:t + 2]
                cab = ca.unsqueeze(3).to_broadcast([128, 2, 32, 2])
                cb0 = cb[:, 0, :].unsqueeze(2).to_broadcast([128, 32, 2])
                cb1 = cb[:, 1, :].unsqueeze(2).to_broadcast([128, 32, 2])
                P.op("dve", lambda e, prev=prev, cab=cab: e.tensor_tensor(out=m1, in0=prev, in1=cab, op=ALU.mult),
                     reads=[rp, R2], writes=[rm1])
                P.op("dve", lambda e, pf=pf, cb0=cb0: e.tensor_tensor(out=m2[:, 0, :, :], in0=pf[0], in1=cb0, op=ALU.mult),
                     reads=[rp, R2], writes=[rm2a])
                P.op("dve", lambda e, pf=pf, cb1=cb1: e.tensor_tensor(out=m2[:, 1, :, :], in0=pf[1], in1=cb1, op=ALU.mult),
                     reads=[rp, R2], writes=[rm2b])
                P.op("dve", lambda e, cur=cur: e.tensor_tensor(out=cur, in0=cur, in1=m1, op=ALU.add),
                     reads=[rm1, rV], writes=[rV])
                P.op("dve", lambda e, cur=cur: e.tensor_tensor(out=cur, in0=cur, in1=m2, op=ALU.add),
                     reads=[rm2a, rm2b, rV], writes=[rV])
            P.op("dve", lambda e, Vb=Vb: e.tensor_copy(out=Zst, in_=Vb[:, :, :, TC - 2:TC]), reads=[rV], writes=[rZs])

        def finalize(ck):
            Vb = Vbs[ck % 2]
            rV = rVs[ck % 2]
            tsl = slice(ck * TC, (ck + 1) * TC)
            t4 = ck * TC // 512
            P.op("pool", lambda e, Vb=Vb: e.tensor_copy(out=Xb, in_=Vb), reads=[rV], writes=[rX])
            for half in range(2):
                for gg in range(32):
                    g = half * 32 + gg
                    hs_ = slice(64 * half, 64 * half + 64)
                    for ri in range(2):
                        P.op("pe", lambda e, g=g, gg=gg, ri=ri, hs_=hs_: e.matmul(
                            ps[5][0:TC, gg * 16:(gg + 1) * 16], lhsT=Xb[hs_, ri, gg, :], rhs=cw[hs_, ri, g * 16:(g + 1) * 16],
                            start=(ri == 0), stop=(ri == 1)), reads=[rX, R2], writes=[psr[5]])
                P.op("act", lambda e, half=half: e.activation(out=ytok[:, half * 512:(half + 1) * 512], in_=ps[5][0:TC, :],
                                                              func=AF.Copy), reads=[psr[5]], writes=[rY])
            for kc in range(8):
                P.op("pe", lambda e, kc=kc: e.transpose(out=ps[6][:, kc * TC:(kc + 1) * TC], in_=ytok[:, kc * 128:(kc + 1) * 128],
                                                        identity=identf[0:TC, 0:TC]), reads=[rY], writes=[psr[6]])
            P.op("pool", lambda e, tsl=tsl: e.tensor_tensor(out=zp, in0=u[:, :, tsl],
                                                           in1=dsk[:, l, :].unsqueeze(2).to_broadcast([128, 8, TC]), op=ALU.mult),
                 reads=[r_u[t4]], writes=[rZ])
            P.op("act", lambda e: e.activation(out=g1, in_=ps[6][:, 0:8 * TC].rearrange("p (k t) -> p k t", t=TC), func=AF.Copy),
                 reads=[psr[6]], writes=[rZ])
            P.op("pool", lambda e: e.tensor_tensor(out=zp, in0=zp, in1=g1, op=ALU.add), reads=[rZ], writes=[rZ])
            P.op("pool", lambda e: e.tensor_tensor(out=g1, in0=zp, in1=zp, op=ALU.mult), reads=[rZ], writes=[rZ])
            P.op("pool", lambda e: e.tensor_scalar(out=g1, in0=g1, scalar1=0.044715, scalar2=1.0, op0=ALU.mult, op1=ALU.add),
                 reads=[rZ], writes=[rZ])
            P.op("pool", lambda e: e.tensor_tensor(out=g1, in0=g1, in1=zp, op=ALU.mult), reads=[rZ], writes=[rZ])
            P.op("pool", lambda e: e.tensor_scalar(out=g1, in0=g1, scalar1=-30.0, scalar2=None, op0=ALU.max), reads=[rZ], writes=[rZ])
            P.op("act", lambda e: e.activation(out=g2, in_=g1, func=AF.Sigmoid, scale=1.5957691216057308),
                 reads=[rZ], writes=[rZ])
            P.op("pool", lambda e, tsl=tsl: e.tensor_tensor(out=u[:, :, tsl], in0=zp, in1=g2, op=ALU.mult),
                 reads=[rZ, r_u[t4]], writes=[rZ2, r_ul])

        for ck in range(nck):
            inproj(ck)
            if ck > 0:
                finalize(ck - 1)
            scan(ck)
            if ck % 3 == 2:
                pump(24)
        finalize(nck - 1)

    def glu(l, b):
        n = l * 2
        wv = wview(wglu_b[l])
        sg = [carve(0, [128, 512], F32), carve(2048, [128, 512], F32)]
        yv = [carve(4096, [128, 512], F32), carve(6144, [128, 512], F32)]
        r_sg = [Res(), Res()]
        r_yv = [Res(), Res()]
        for t in range(4):
            for c0 in (0, 512):
                pend = {}

                def epi(t_, oc, p_ap, p_res, pend=pend, t=t):
                    if oc < 8:
                        pend[oc] = (p_ap, p_res)
                        if dbg is not None and dbg[0] in ("gi", "gn", "gs"):
                            P.op("dve", lambda e, p_ap=p_ap, oc=oc, t=t: e.tensor_copy(out=h[:, oc, T(t)], in_=p_ap[:]),
                                 reads=[p_res, r_h[t]], writes=[r_h[t]])
                        return
                    if dbg is not None and dbg[0] in ("gi", "gn", "gs"):
                        return
                    ov = oc - 8
                    i = ov % 2
                    vp, vr = pend[ov]
                    if dbg is not None and dbg[0] in ("gv", "gg"):
                        src_, sr_ = (vp, vr) if dbg[0] == "gv" else (p_ap, p_res)
                        P.op("dve", lambda e, src_=src_, ov=ov, t=t: e.tensor_copy(out=h[:, ov, T(t)], in_=src_[:]),
                             reads=[sr_, vr, p_res, r_h[t]], writes=[r_h[t]])
                        return
                    P.op("act", lambda e, i=i, p_ap=p_ap: e.activation(out=sg[i], in_=p_ap[:], func=AF.Sigmoid),
                         reads=[p_res], writes=[r_sg[i]])
                    P.op("dve", lambda e, i=i, vp=vp: e.tensor_tensor(out=yv[i], in0=vp[:], in1=sg[i], op=ALU.mult),
                         reads=[vr, r_sg[i]], writes=[r_yv[i]])
                    P.op("dve", lambda e, i=i, ov=ov, t=t: e.scalar_tensor_tensor(
                        out=h[:, ov, T(t)], in0=yv[i], scalar=mod_gate(n, ov, b), in1=h[:, ov, T(t)],
                        op0=ALU.mult, op1=ALU.add), reads=[r_yv[i], r_mod, r_h[t]], writes=[r_h[t]])
                for sub in (0, 256):
                    gemm_fm(u, r_u, 8, wv, [[c0 + sub, c0 + sub + 128]], epi, tiles=[t], wr=wready(("wglu", l)))
                    gemm_fm(u, r_u, 8, wv, [[1024 + c0 + sub, 1024 + c0 + sub + 128]], epi, tiles=[t], wr=wready(("wglu", l)))

    def kv_phase(b):
        norm(8, b, u, r_u)
        wv = wview(wkv_b)
        stg = [carve(i * 1024, [128, 512], BF16) for i in range(4)]
        r_stg = [Res() for _ in range(4)]
        d_stg = [P.dsem() for _ in range(4)]
        cnt = [0]

        def epi(t, oc, p_ap, p_res):
            i = cnt[0] % 4
            cnt[0] += 1
            P.op("act", lambda e, i=i, p_ap=p_ap: e.activation(out=stg[i], in_=p_ap[:], func=AF.Copy),
                 reads=[p_res], writes=[r_stg[i]])
            me = P.op("sp", lambda e, i=i, oc=oc, t=t: e.dma_start(out=kt_s[b, oc, :, T(t)], in_=stg[i]),
                      reads=[r_stg[i]], dma=d_stg[i])
            r_stg[i].r.append(me)
            r_kv.w = me if r_kv.w is None or True else r_kv.w
            kv_w.append(me)
        gemm_fm(u, r_u, 8, wv, [[c0 + 128 * j for j in range(4)] for c0 in range(0, 3072, 512)], epi, wr=wready(("wkv", 0)))
        for c0 in range(3072, 6144, 512):
            si = load_slab(wv[:, 0:8, c0:c0 + 512], wready(("wkv", 0)))
            for tb in range(16):
                bk = rot["a"] % 6
                rot["a"] += 1
                for kc in range(8):
                    P.op("pe", lambda e, si=si, kc=kc, tb=tb, bk=bk: e.matmul(
                        ps[bk][:], lhsT=u[:, kc, tb * 128:(tb + 1) * 128], rhs=slabs[si][:, kc, :],
                        start=(kc == 0), stop=(kc == 7)), reads=[r_slab[si], r_u[tb // 4]], writes=[psr[bk]])
                i = cnt[0] % 4
                cnt[0] += 1
                P.op("act", lambda e, i=i, bk=bk: e.activation(out=stg[i], in_=ps[bk][:], func=AF.Copy),
                     reads=[psr[bk]], writes=[r_stg[i]])
                me = P.op("sp", lambda e, i=i, tb=tb, c0=c0: e.dma_start(
                    out=v_s[b, tb * 128:(tb + 1) * 128, c0 - 3072:c0 - 3072 + 512], in_=stg[i]),
                    reads=[r_stg[i]], dma=d_stg[i])
                r_stg[i].r.append(me)
                kv_w.append(me)

    r_kv = Res()
    kv_w = []
    DIL = (1, 4, 16)

    def sl_(start, n, step):
        return slice(start, start + (n - 1) * step + 1, step)

    def attention(l, b):
        j_ = l - 2
        n = l * 2
        qT = carve(0, [128, 3, S], BF16)
        kT = carve(12288, [128, 3, S], BF16)
        Vt = carve(24576, [128, 3, 16, 128], BF16)
        pT = [carve(36864, [128, 512], BF16), carve(37888, [128, 512], BF16)]
        rec = carve(38912, [128, 1024], F32)
        oT = big[:, :].rearrange("p (a b) -> p a b", b=S)
        r_q = Res(); r_p = [Res(), Res()]; r_rec = Res(); r_o = Res()
        r_kk = [Res(), Res()]; r_vv = [Res(), Res()]
        d_kk = [P.dsem(), P.dsem()]; d_vv = [P.dsem(), P.dsem()]
        d_q = next_misc()
        wqv = wview(wq_b[j_])
        pcount = [0]
        P.barrier()
        sf = [slabs[i][:, :, :].rearrange("p a b -> p (a b)") for i in range(3)]
        Kbuf = [[kT[:, 0, :], kT[:, 1, :], kT[:, 2, :]],
                [sf[0][:, 0:2048], sf[0][:, 2048:4096], sf[1][:, 0:2048]]]
        Vbuf = [[Vt[:, 0, :, :], Vt[:, 1, :, :], Vt[:, 2, :, :]],
                [sf[1][:, 2048:4096].rearrange("p (n f) -> p n f", f=128),
                 sf[2][:, 0:2048].rearrange("p (n f) -> p n f", f=128),
                 sf[2][:, 2048:4096].rearrange("p (n f) -> p n f", f=128)]]
        qslab = tmpf[0][:, :].bitcast(BF16).rearrange("p (a b) -> p a b", b=128)
        r_qs = r_tmpf[0]

        def kv_loads(hp):
            bi_ = hp % 2
            for br in range(3):
                P.op("sp", lambda e, br=br, hp=hp, bi_=bi_: e.dma_start(out=Kbuf[bi_][br], in_=kt_s[b, br * 8 + hp]),
                     reads=[r_kv], writes=[r_kk[bi_]], dma=d_kk[bi_])
                d = DIL[br]
                vsrc = v_s[b].rearrange("(n p r) f -> p r n f", p=128, r=d)[:, :, :, br * 1024 + hp * 128: br * 1024 + hp * 128 + 128]
                for r in range(d):
                    nb_ = 16 // d
                    P.op("sp", lambda e, br=br, r=r, nb_=nb_, vsrc=vsrc, bi_=bi_: e.dma_start(
                        out=Vbuf[bi_][br][:, r * nb_:(r + 1) * nb_, :], in_=vsrc[:, r, :, :]),
                        reads=[r_kv], writes=[r_vv[bi_]], dma=d_vv[bi_])

        kv_loads(0)
        for hp in range(8):
            bi_ = hp % 2
            r_k = r_kk[bi_]
            r_v = r_vv[bi_]
            Kb = Kbuf[bi_]
            Vb_ = Vbuf[bi_]
            for br in range(3):
                c0 = br * 1024 + hp * 128
                P.op("sp", lambda e, c0=c0: e.dma_start(out=qslab, in_=wqv[:, :, c0:c0 + 128]),
                     reads=[wready(("wq", j_))], writes=[r_qs], dma=d_q)
                for t in range(4):
                    bk = rot["a"] % 4
                    rot["a"] += 1
                    for kc in range(8):
                        P.op("pe", lambda e, kc=kc, t=t, bk=bk: e.matmul(ps[bk][:], lhsT=qslab[:, kc, :], rhs=u[:, kc, T(t)],
                                                                         start=(kc == 0), stop=(kc == 7)),
                             reads=[r_qs, r_u[t]], writes=[psr[bk]])
                    P.op("act", lambda e, br=br, t=t, bk=bk: e.activation(out=qT[:, br, T(t)], in_=ps[bk][:], func=AF.Copy,
                                                                          scale=0.125), reads=[psr[bk]], writes=[r_q])
            if hp + 1 < 8:
                kv_loads(hp + 1)
            for qh in range(2):
                NB = (4, 5)
                DB = (6, 7)
                first = {}
                groups = []
                for hh in range(2):
                    hs = slice(64 * hh, 64 * hh + 64)
                    tl = []
                    for nn in range(8):
                        nblk = qh * 8 + nn
                        for kb, msk in ((nblk - 1, mprev), (nblk, mcur)):
                            if kb < 0:
                                continue
                            tl.append((0, slice(kb * 128, kb * 128 + 128), slice(nblk * 128, nblk * 128 + 128),
                                       msk[:, :], kb, nn // 4, slice((nn % 4) * 128, (nn % 4) * 128 + 128), 128))
                    for r in range(4):
                        for nn in range(2):
                            nblk = qh * 2 + nn
                            for kb, msk in ((nblk - 1, mprev), (nblk, mcur)):
                                if kb < 0:
                                    continue
                                tl.append((1, sl_(r + 512 * kb, 128, 4),
                                           sl_(r + 512 * nblk, 128, 4), msk[:, :], r * 4 + kb,
                                           nn, sl_(r, 128, 4), 128))
                    for r in range(16):
                        for mm in range(2):
                            m_ = qh * 2 + mm
                            tl.append((2, sl_(r, 128, 16), sl_(r + 512 * m_, 32, 16),
                                       mcur[:, 32 * m_:32 * m_ + 32], r, mm, sl_(r, 32, 16), 32))
                    i0 = 0
                    while i0 < len(tl):
                        grp = []
                        w_ = 0
                        while i0 < len(tl) and w_ + tl[i0][7] <= 512:
                            grp.append((tl[i0], w_))
                            w_ += tl[i0][7]
                            i0 += 1
                        groups.append((grp, w_, hh, hs))
                base_ = pcount[0]
                pcount[0] += len(groups)

                def emit_S(gi):
                    grp, w_, hh, hs = groups[gi]
                    bk = (base_ + gi) % 4
                    for ii_, ((br, kap, qap, mk, vb, ob, oc_, nc_), off) in enumerate(grp):
                        P.op("pe", lambda e, br=br, kap=kap, qap=qap, off=off, nc_=nc_, bk=bk, hs=hs, ii_=ii_, Kb=Kb: e.matmul(
                            ps[bk][:, off:off + nc_], lhsT=Kb[br][hs, kap], rhs=qT[hs, br, qap], start=(ii_ == 0), stop=False,
                            skip_group_check=True),
                            reads=[r_k, r_q], writes=[psr[bk]])
                    for ii_, ((br, kap, qap, mk, vb, ob, oc_, nc_), off) in enumerate(grp):
                        P.op("pe", lambda e, mk=mk, off=off, nc_=nc_, bk=bk, ii_=ii_, ng_=len(grp): e.matmul(
                            ps[bk][:, off:off + nc_], lhsT=identb[:], rhs=mk, start=False, stop=(ii_ == ng_ - 1), skip_group_check=True),
                            writes=[psr[bk]])

                def emit_PV(gi):
                    grp, w_, hh, hs = groups[gi]
                    bk = (base_ + gi) % 4
                    pi = (base_ + gi) % 2
                    P.op("act", lambda e, bk=bk, pi=pi, w_=w_: e.activation(out=pT[pi][:, 0:w_], in_=ps[bk][:, 0:w_], func=AF.Exp),
                         reads=[psr[bk]], writes=[r_p[pi]])
                    for (br, kap, qap, mk, vb, ob, oc_, nc_), off in grp:
                        for (bank, lhs) in ((NB[ob], Vb_[br][:, vb, hs]), (DB[ob], onesb[:, 0:64])):
                            key = (bank, hh)
                            st_ = key not in first
                            first[key] = 1
                            P.op("pe", lambda e, bank=bank, lhs=lhs, oc_=oc_, pi=pi, off=off, nc_=nc_, st_=st_, hs=hs: e.matmul(
                                ps[bank][hs, oc_], lhsT=lhs, rhs=pT[pi][:, off:off + nc_], start=st_, stop=False,
                                skip_group_check=True),
                                reads=[r_p[pi], r_v], writes=[psr[bank]])

                emit_S(0)
                if len(groups) > 1:
                    emit_S(1)
                for gi in range(len(groups)):
                    if gi + 2 < len(groups):
                        emit_S(gi + 2)
                    emit_PV(gi)
                for ob in range(2):
                    P.op("dve", lambda e, ob=ob: e.reciprocal(out=rec[:, ob * 512:(ob + 1) * 512], in_=ps[DB[ob]][:]),
                         reads=[psr[DB[ob]]], writes=[r_rec])
                    P.op("dve", lambda e, ob=ob, hp=hp, qh=qh: e.tensor_tensor(
                        out=oT[:, hp, qh * 1024 + ob * 512: qh * 1024 + (ob + 1) * 512], in0=ps[NB[ob]][:],
                        in1=rec[:, ob * 512:(ob + 1) * 512], op=ALU.mult),
                        reads=[psr[NB[ob]], r_rec], writes=[r_o])
        P.barrier()
        rot["a"] = 0
        gemm_fm(oT, r_o, 8, wview(wo_b[j_]), [[c0 + 128 * j for j in range(4)] for c0 in (0, 512)], resid_epi(n, b),
                wr=wready(("wo", j_)))

    step = [0]

    def done():
        step[0] += 1
        return step[0] >= stop_after

    def u_to_h():
        P.barrier()
        for t in range(4):
            for kc in range(8):
                P.op("act", lambda e, t=t, kc=kc: e.activation(out=h[:, kc, T(t)], in_=u[:, kc, T(t)], func=AF.Copy),
                     reads=[r_u[t]], writes=[r_h[t]])

    for b in range(nseq):
        P.barrier()
        load_x(b)
        stopped = False
        for l in range(4):
            if l == 2:
                P.barrier()
                kv_w.clear()
                kv_phase(b)
                r_kv.w = None
                P.barrier()
            P.barrier()
            if dbg == ("w", l):
                wtmp = big[:, :].rearrange("p (a b) -> p a b", b=2048)
                rw_ = Res()
                P.op("sp", lambda e: e.dma_start(out=wtmp, in_=wview(wglu_b[0])), reads=[wready(("wglu", 0))], writes=[rw_], dma=d_tok[1])
                P.op("sp", lambda e: e.dma_start(out=dbg_w.rearrange("(kc p) n -> p kc n", p=128), in_=wtmp), reads=[rw_], writes=[rw_], dma=d_tok[1])
                stopped = True
                break
            if dbg != ("m", l):
                norm(l * 2, b, u, r_u)
            if dbg == ("u", l):
                u_to_h()
                stopped = True
                break
            if dbg == ("m", l):
                pass
            elif dbg == ("gs", l):
                L = s5_layer_setup(l)
                s5_mixer(l, b, L)
                P.barrier()
                norm(l * 2, b, u, r_u)
                P.barrier()
                glu(l, b)
                stopped = True
                break
            elif dbg == ("gn", l):
                glu(l, b)
                stopped = True
                break
            elif l < 2:
                L = s5_layer_setup(l)
                if dbg == ("p", l):
                    P.barrier()
                    stopped = True
                    break
                s5_mixer(l, b, L)
                P.barrier()
                if dbg is not None and len(dbg) > 2 and dbg[2] == "fix":
                    norm(l * 2, b, big[:, :].rearrange("p (a b) -> p a b", b=S), [Res() for _ in range(4)])
                    P.barrier()
                if dbg is not None and dbg[:2] == ("pu", l):
                    for t_ in range(4):
                        for kc_ in range(8):
                            bk_ = (t_ * 8 + kc_) % 4
                            P.op("pe", lambda e, t_=t_, kc_=kc_, bk_=bk_: e.matmul(ps[bk_][:], lhsT=identb[:], rhs=u[:, kc_, T(t_)],
                                                                                   start=True, stop=True), reads=[r_u[t_]], writes=[psr[bk_]])
                            P.op("dve", lambda e, t_=t_, kc_=kc_, bk_=bk_: e.tensor_copy(out=h[:, kc_, T(t_)], in_=ps[bk_][:]),
                                 reads=[psr[bk_], r_h[t_]], writes=[r_h[t_]])
                    stopped = True
                    break
                if dbg is not None and dbg[:2] in (("z", l), ("y", l)):
                    u_to_h()
                    stopped = True
                    break
                glu(l, b)
            else:
                attention(l, b)
            if done():
                stopped = True
                break
            P.barrier()
            norm(l * 2 + 1, b, u, r_u)
            mlp(l, b)
            if done():
                stopped = True
                break
        step[0] = 0
        P.barrier()
        store_out(b, final=not stopped)
    P.barrier()
    P.op("sp", lambda e: e.dma_start(out=tok[0][0:1, 0:8], in_=x_d[0, 0:1, 0:8]), dma=d_tok[0])
    P.emit([(d_tok[0], P.cnt[d_tok[0]])] + [(k, P.cnt[k]) for k in d_otok])
    st.close()
    return nc


def _bf16(a):
    import ml_dtypes
    return np.asarray(a, dtype=np.float32).astype(ml_dtypes.bfloat16)


def make_in_maps(inp):
    f = lambda a: np.ascontiguousarray(np.asarray(a, dtype=np.float32))
    x = f(inp["x"]); c = f(inp["c"])
    adaw = np.concatenate([f(inp["ada_w"]).reshape(8, D, 3072)[i] for i in range(8)] + [f(inp["kv_ada_w"])], axis=1)
    adab_flat = np.concatenate([f(inp["ada_b"]).reshape(-1), f(inp["kv_ada_b"])])
    adab = np.ascontiguousarray(adab_flat.reshape(NCH, 128).T)
    lng_all = np.concatenate([f(inp["ln_g"]).reshape(8, D), f(inp["kv_g"]).reshape(1, D)], axis=0)
    lng = np.ascontiguousarray(lng_all.reshape(9, 8, 128).transpose(2, 0, 1))
    fing = np.ascontiguousarray(np.broadcast_to(f(inp["final_g"])[None, :], (128, D)))
    dsk = np.ascontiguousarray(f(inp["ssm_d"]).reshape(2, 8, 128).transpose(2, 0, 1))
    kk = np.arange(128)[:, None]; qq = np.arange(128)[None, :]
    mcur = np.where(kk <= qq, 0.0, -30000.0).astype(np.float32)
    mprev = np.where(kk >= qq, 0.0, -30000.0).astype(np.float32)
    common = dict(
        adaw=np.ascontiguousarray(adaw), adab=adab, lng=lng, fing=fing, dsk=dsk,
        lamre=f(inp["ssm_lam_re"]), lamim=f(inp["ssm_lam_im"]), logdt=f(inp["ssm_log_dt"]).reshape(2, 64, 1),
        bre=f(inp["ssm_b_re"]).reshape(2, 64, 1024), bim=f(inp["ssm_b_im"]).reshape(2, 64, 1024),
        cre=f(inp["ssm_c_re"]).reshape(2, 1024, 64), cim=f(inp["ssm_c_im"]).reshape(2, 1024, 64),
        identf=np.eye(128, dtype=np.float32), identb=_bf16(np.eye(128)), mcur=_bf16(mcur), mprev=_bf16(mprev),
        w1=f(inp["mlp_w1"]), w2=f(inp["mlp_w2"]), wglu=f(inp["ssm_w_glu"]), wkv=f(inp["w_kv"]),
        wq=f(inp["attn_w_q"]), wo=f(inp["attn_w_o"]),
    )
    maps = []
    for i in range(8):
        m = dict(common)
        m["x"] = np.ascontiguousarray(x[2 * i:2 * i + 2])
        m["cT"] = np.ascontiguousarray(c[2 * i:2 * i + 2].T.reshape(8, 128, 2).transpose(1, 0, 2))
        maps.append(m)
    return maps


def kernel(**inputs):
    nc = build_program()
    maps = make_in_maps(inputs)
    res = run_bass_kernel_spmd(nc, maps, core_ids=list(range(8)))
    return np.concatenate([r["out"] for r in res.results], axis=0).astype(np.float32)
```
